# Optimizing a Trainium2 kernel written in Bass

```python
import math
import jax, jax.numpy as jnp
from jax import lax
import numpy as np

D_MODEL = 1024
BATCH = 4
SEQ = 4096
DEPTH = 4

CONV_WIDTH = D_MODEL
CONV_K = 3
POOL_WIDTH = D_MODEL
POOL_GROUPS = 4
POOL_WINDOWS = (2, 4, 8, 16)
N_HEADS = 16
N_KV_HEADS = 4
HEAD_DIM = D_MODEL // N_HEADS
WINDOW = 128
BLOCK = 128
N_BUCKETS = 32
MAX_DISTANCE = 128
N_BRANCHES = 3
D_FF = -(-8 * D_MODEL // (3 * 256)) * 256

EPS = 1e-6
NEG_INF = -1e30

Q_WIDTH = N_HEADS * HEAD_DIM
KV_WIDTH = N_KV_HEADS * HEAD_DIM
IN_SIZES = (CONV_WIDTH, CONV_WIDTH, CONV_WIDTH, POOL_WIDTH, Q_WIDTH, KV_WIDTH, KV_WIDTH,
            D_MODEL, D_MODEL, D_MODEL)
IN_TOTAL = sum(IN_SIZES)

kernel_name = "hybrid_conv_pool_swa_encoder"


def rms_norm(x, g):
    xf = x.astype(jnp.float32)
    y = xf * lax.rsqrt(jnp.mean(xf * xf, axis=-1, keepdims=True) + EPS)
    return (y * g.astype(jnp.float32)).astype(x.dtype)


def split_points():
    pts, acc = [], 0
    for s in IN_SIZES[:-1]:
        acc += s
        pts.append(acc)
    return pts


def t5_bucket(rel):
    half = N_BUCKETS // 2
    max_exact = half // 2
    ret = jnp.where(rel > 0, half, 0)
    n = jnp.abs(rel)
    nf = jnp.maximum(n, 1).astype(jnp.float32)
    large = max_exact + (jnp.log(nf / max_exact) / math.log(MAX_DISTANCE / max_exact)
                         * (half - max_exact)).astype(jnp.int32)
    large = jnp.minimum(large, half - 1)
    return ret + jnp.where(n < max_exact, n, large)


def short_conv_mixer(b_gate, c_gate, xin, conv_w, w_out):
    u = c_gate * xin
    y = lax.conv_general_dilated(u, conv_w, window_strides=(1,),
                                 padding=[(CONV_K // 2, CONV_K // 2)],
                                 dimension_numbers=("NWC", "WIO", "NWC"),
                                 feature_group_count=u.shape[-1])
    return (b_gate * y) @ w_out


def multiscale_pool_mixer(u, w_pool, pool_scale):
    B, S, W = u.shape
    cg = W // POOL_GROUPS
    uf = u.astype(jnp.float32).reshape(B, S, POOL_GROUPS, cg)
    cs = jnp.pad(jnp.cumsum(uf, axis=1), ((0, 0), (1, 0), (0, 0), (0, 0)))
    t = jnp.arange(S)
    outs = []
    for gi, w in enumerate(POOL_WINDOWS):
        lo = jnp.maximum(t - w // 2, 0)
        hi = jnp.minimum(t + (w - 1 - w // 2), S - 1)
        csg = cs[:, :, gi]
        s = jnp.take(csg, hi + 1, axis=1) - jnp.take(csg, lo, axis=1)
        cnt = (hi - lo + 1).astype(jnp.float32)[None, :, None]
        outs.append(s / cnt - uf[:, :, gi])
    p = jnp.stack(outs, axis=2).astype(u.dtype)
    y = jnp.einsum("bsgc,gcd->bsgd", p, w_pool).reshape(B, S, W)
    return y * pool_scale


def windowed_gqa(q, k, v, rel_bias, sink):
    B, S, _ = q.shape
    nb = S // BLOCK
    G = N_HEADS // N_KV_HEADS
    qb = q.reshape(B, nb, BLOCK, N_KV_HEADS, G, HEAD_DIM)
    pad = ((0, 0), (BLOCK, BLOCK), (0, 0))
    kp = jnp.pad(k, pad).reshape(B, nb + 2, BLOCK, N_KV_HEADS, HEAD_DIM)
    vp = jnp.pad(v, pad).reshape(B, nb + 2, BLOCK, N_KV_HEADS, HEAD_DIM)
    kb = jnp.concatenate([kp[:, :-2], kp[:, 1:-1], kp[:, 2:]], axis=2)
    vb = jnp.concatenate([vp[:, :-2], vp[:, 1:-1], vp[:, 2:]], axis=2)
    scores = jnp.einsum("bnqhgd,bnkhd->bhgnqk", qb, kb,
                        preferred_element_type=jnp.float32) * (HEAD_DIM ** -0.5)
    qi = jnp.arange(BLOCK)[:, None]
    kj = jnp.arange(3 * BLOCK)[None, :]
    rel = kj - BLOCK - qi
    bias = rel_bias[t5_bucket(rel)].astype(jnp.float32)
    bias = jnp.transpose(bias, (2, 0, 1)).reshape(N_KV_HEADS, G, 1, BLOCK, 3 * BLOCK)
    kabs = jnp.arange(nb)[:, None, None] * BLOCK + kj[None] - BLOCK
    valid = (jnp.abs(rel)[None] <= WINDOW) & (kabs >= 0) & (kabs < S)
    scores = jnp.where(valid, scores + bias, NEG_INF)
    sink_l = sink.astype(jnp.float32).reshape(N_KV_HEADS, G, 1, 1, 1)
    m = jnp.maximum(jnp.max(scores, axis=-1, keepdims=True), sink_l)
    p = jnp.exp(scores - m)
    denom = jnp.sum(p, axis=-1, keepdims=True) + jnp.exp(sink_l - m)
    p = (p / denom).astype(v.dtype)
    out = jnp.einsum("bhgnqk,bnkhd->bnqhgd", p, vb)
    return out.reshape(B, S, N_HEADS * HEAD_DIM)


def hybrid_layer(x, w_in, conv_w, w_a_out, w_pool, pool_scale, w_attn_out, sink, w_o,
                 g_mix, g_ffn, w_gu, w_down, rel_bias):
    h = rms_norm(x, g_mix)
    proj = h @ w_in
    b_a, c_a, x_a, u_p, q, k, v, ga, gp, gt = jnp.split(proj, split_points(), axis=-1)
    y_a = short_conv_mixer(b_a, c_a, x_a, conv_w, w_a_out)
    y_p = multiscale_pool_mixer(u_p, w_pool, pool_scale)
    y_t = windowed_gqa(q, k, v, rel_bias, sink) @ w_attn_out
    merged = jax.nn.sigmoid(ga) * y_a + jax.nn.sigmoid(gp) * y_p + jax.nn.sigmoid(gt) * y_t
    x = x + merged @ w_o
    h2 = rms_norm(x, g_ffn)
    gate, up = jnp.split(h2 @ w_gu, [D_FF], axis=-1)
    return x + (jax.nn.silu(gate) * up) @ w_down


def setup_inputs(seed: int = 0) -> dict:
    key = jax.random.key(seed)
    ks = jax.random.split(key, 16)
    nrm = lambda k, shape, scale: jax.random.normal(k, shape, jnp.float32) * scale
    cg = POOL_WIDTH // POOL_GROUPS
    return {
        "x": nrm(ks[0], (BATCH, SEQ, D_MODEL), 1.0),
        "w_in": nrm(ks[1], (DEPTH, D_MODEL, IN_TOTAL), D_MODEL ** -0.5),
        "conv_w": nrm(ks[2], (DEPTH, CONV_K, 1, CONV_WIDTH), CONV_K ** -0.5),
        "w_a_out": nrm(ks[3], (DEPTH, CONV_WIDTH, D_MODEL), CONV_WIDTH ** -0.5),
        "w_pool": nrm(ks[4], (DEPTH, POOL_GROUPS, cg, cg), cg ** -0.5),
        "pool_scale": 1.0 + nrm(ks[5], (DEPTH, POOL_WIDTH), 0.02),
        "w_attn_out": nrm(ks[6], (DEPTH, Q_WIDTH, D_MODEL), Q_WIDTH ** -0.5),
        "attn_sink": nrm(ks[7], (DEPTH, N_HEADS), 0.5),
        "w_o": nrm(ks[8], (DEPTH, D_MODEL, D_MODEL), D_MODEL ** -0.5),
        "g_mix": 1.0 + nrm(ks[9], (DEPTH, D_MODEL), 0.02),
        "g_ffn": 1.0 + nrm(ks[10], (DEPTH, D_MODEL), 0.02),
        "w_gu": nrm(ks[11], (DEPTH, D_MODEL, 2 * D_FF), D_MODEL ** -0.5),
        "w_down": nrm(ks[12], (DEPTH, D_FF, D_MODEL), D_FF ** -0.5),
        "rel_bias": nrm(ks[13], (N_BUCKETS, N_HEADS), 0.5),
        "g_final": 1.0 + nrm(ks[14], (D_MODEL,), 0.02),
    }


def reference(x, w_in, conv_w, w_a_out, w_pool, pool_scale, w_attn_out, attn_sink, w_o,
              g_mix, g_ffn, w_gu, w_down, rel_bias, g_final):
    for layer in range(DEPTH):
        x = hybrid_layer(x, w_in[layer], conv_w[layer], w_a_out[layer], w_pool[layer],
                         pool_scale[layer], w_attn_out[layer], attn_sink[layer], w_o[layer],
                         g_mix[layer], g_ffn[layer], w_gu[layer], w_down[layer], rel_bias)
    return rms_norm(x, g_final)
```

```python
import numpy as np
import contextlib
import concourse.bass as bass
import concourse.mybir as mybir
from concourse.bass_utils import run_bass_kernel_spmd

F32 = mybir.dt.float32
BF16 = mybir.dt.bfloat16
AF = mybir.ActivationFunctionType
ALU = mybir.AluOpType

D = 1024
NCH = 8
SEQ = 4096
BATCH = 4
DEPTH = 4
NH = 16
HD = 64
DFF = 2816
NFC = 22
NFG = 2
FPG = 11
EPS = 1e-6
OWN_BLK = 16
MAXST = 6
MAXEXT = 8
TT = MAXST * 128
HEAD_ORDER = [0, 4, 1, 5, 2, 6, 3, 7, 8, 12, 9, 13, 10, 14, 11, 15]
POOL_W = (2, 4, 8, 16)
EPP = 163840
ARENA = 12288
NWSEM = 12
NEGB = -30000.0

CST_PER_L = 56


class Src:
    def __init__(self, sem):
        self.sem = sem
        self.cnt = 0


class Eng:
    def __init__(self, eng, sem, name):
        self.eng = eng
        self.src = Src(sem)
        self.name = name
        self.seen = {}

    def wait(self, tok):
        if tok is None:
            return
        src, c = tok
        if self.seen.get(id(src), 0) >= c:
            return
        self.eng.wait_ge(src.sem, c)
        self.seen[id(src)] = c

    def issue(self, inst):
        self.src.cnt += 1
        inst.then_inc(self.src.sem, 1)
        return (self.src, self.src.cnt)


class Tracker:
    def __init__(self):
        self.w = {}
        self.r = {}

    def deps(self, reads, writes):
        toks = []
        for k in reads:
            t = self.w.get(k)
            if t is not None:
                toks.append(t)
        for k in writes:
            t = self.w.get(k)
            if t is not None:
                toks.append(t)
            rr = self.r.get(k)
            if rr:
                toks.extend(rr.values())
        return toks

    def commit(self, tok, reads, writes):
        for k in reads:
            rr = self.r.setdefault(k, {})
            rr[id(tok[0])] = tok
        for k in writes:
            self.w[k] = tok
            self.r[k] = {}


class PsumAlloc:
    def __init__(self, banks):
        self.free_list = list(range(len(banks)))
        self.banks = banks

    def alloc(self):
        assert self.free_list, "out of PSUM banks"
        return self.free_list.pop(0)

    def free(self, b):
        self.free_list.append(b)


class WStream:
    def __init__(self, nc, pool, arena, sems):
        self.nc = nc
        self.pool = pool
        self.arena = arena
        self.sems = [Src(s) for s in sems]
        self.plan = []
        self.ni = 0
        self.ng = 0
        self.head = 0
        self.regions = []
        self.rel = {}
        self.info = {}

    def add(self, name, dram_ap, n):
        self.plan.append((name, dram_ap, n))

    def pump(self):
        while self.ni < len(self.plan):
            name, dap, n = self.plan[self.ni]
            idx = self.ni
            off = self.head
            if off + n > ARENA:
                off = 0
            conflicts = [rg for rg in self.regions if rg[0] < off + n and off < rg[0] + rg[1]]
            if any(rg[2] not in self.rel for rg in conflicts):
                return
            old = idx - NWSEM
            if old >= 0 and old not in self.rel:
                return
            for rg in conflicts:
                self.pool.wait(self.rel[rg[2]])
                self.regions.remove(rg)
            if old >= 0:
                self.pool.wait(self.rel[old])
            s = self.sems[idx % NWSEM]
            s.cnt += 16
            self.nc.gpsimd.dma_start(out=self.arena[:, off:off + n], in_=dap).then_inc(s.sem, 16)
            self.info[idx] = (off, n, (s, s.cnt))
            self.regions.append((off, n, idx))
            self.head = off + n
            self.ni += 1

    def get(self, name):
        self.pump()
        pname, _, n = self.plan[self.ng]
        assert pname == name, (pname, name)
        assert self.ng in self.info, f"weight arena deadlock at {name}"
        off, n, tok = self.info[self.ng]
        idx = self.ng
        self.ng += 1
        return idx, off, tok

    def release(self, idx, tok):
        self.rel[idx] = tok
        self.pump()


def split_blocks(n, maxpart):
    parts = -(-n // maxpart)
    base = n // parts
    rem = n % parts
    out = []
    s = 0
    for i in range(parts):
        m = base + (1 if i < rem else 0)
        out.append((s, m))
        s += m
    return out


def build_program(n_layers, regions, tin_blk, do_final):
    nc = bass.Bass("TRN2", target_bir_lowering=False)
    TIN = tin_blk * 128
    OUTB = regions[-1]
    ncst = n_layers * CST_PER_L + 8
    xT_d = nc.dram_tensor("xT", [D, TIN], F32, kind="ExternalInput").ap()
    wst_d = nc.dram_tensor("wst", [n_layers, 128, EPP], F32, kind="ExternalInput").ap()
    cst_d = nc.dram_tensor("cst", [128, ncst], F32, kind="ExternalInput").ap()
    bias_d = nc.dram_tensor("biasT", [128, NH, 384], F32, kind="ExternalInput").ap()
    band_d = nc.dram_tensor("bands", [128, 16 * 128], F32, kind="ExternalInput").ap()
    out_d = nc.dram_tensor("outT", [D, OUTB * 128], F32, kind="ExternalOutput").ap()

    es = contextlib.ExitStack()
    with es:
        def sb(name, shape, dt):
            return es.enter_context(nc.sbuf_tensor(name, shape, dt))

        def sem(name):
            return es.enter_context(nc.semaphore(name))

        xT = sb("xT_sb", [128, NCH, TIN], F32)
        hT = sb("hT", [128, NCH, MAXEXT * 128], BF16)
        stash = sb("stash", [128, NCH, 128], BF16)
        BM = sb("BM", [128, 16, TT], BF16)
        arena = sb("arena", [128, ARENA], BF16)
        sq = sb("sq", [128, NCH, 256], BF16)
        cst = sb("cst_sb", [128, ncst], F32)
        esink = sb("esink", [128, n_layers * 8], F32)
        EB = sb("EB", [128, NH, 384], BF16)
        bands = sb("bands_sb", [128, 16, 128], BF16)
        onesM = sb("onesM", [128, 128], BF16)
        fbuf = [sb(f"fbuf{i}", [128, 512], F32) for i in range(4)]
        SCRN = 13312
        scr = sb("scr", [128, SCRN], BF16)
        ubuf = scr[:, 0:2 * (TT + 2)].bitcast(F32)
        ybuf = [scr[:, 2048 + i * 1024:2048 + (i + 1) * 1024].bitcast(F32) for i in range(2)]
        Usb = scr[:, 0:4096].rearrange("p (a b) -> p a b", a=4)
        _o = 0
        KT = scr[:, _o:_o + 2 * MAXEXT * 128].rearrange("p (a b) -> p a b", a=2); _o += 2 * MAXEXT * 128
        Vsb = scr[:, _o:_o + MAXEXT * 512].rearrange("p (a b c) -> p a b c", a=MAXEXT, b=4); _o += MAXEXT * 512
        QTA = scr[:, _o:_o + TT]; _o += TT
        QTB = scr[:, _o:_o + TT]; _o += TT
        Ebuf = [[None, None], [None, None]]
        for r in range(2):
            for i in range(2):
                Ebuf[r][i] = scr[:, _o:_o + 384]; _o += 384
        PT = [[None] * 4, [None] * 4]
        for r in range(2):
            for i in range(4):
                PT[r][i] = scr[:, _o:_o + 384]; _o += 384
        rden = scr[:, _o:_o + 1024].bitcast(F32); _o += 1024
        assert _o <= SCRN, _o
        psb = [es.enter_context(nc.psum_tensor(f"ps{i}", [128, 512], F32)) for i in range(8)]

        PE = Eng(nc.tensor, sem("s_pe"), "pe")
        ACT = Eng(nc.scalar, sem("s_act"), "act")
        DVE = Eng(nc.vector, sem("s_dve"), "dve")
        POOL = Eng(nc.gpsimd, sem("s_pool"), "pool")
        SP = Eng(nc.sync, sem("s_sp"), "sp")
        T = Tracker()
        PS = PsumAlloc(psb)
        W = WStream(nc, POOL, arena, [sem(f"s_w{i}") for i in range(NWSEM)])

        def op(E, fn, reads=(), writes=(), extra=()):
            for t in T.deps(reads, writes):
                E.wait(t)
            for t in extra:
                E.wait(t)
            tok = E.issue(fn())
            T.commit(tok, reads, writes)
            return tok

        def mm(mms, reads=(), writes=(), extra=()):
            for t in T.deps(reads, writes):
                PE.wait(t)
            for t in extra:
                PE.wait(t)
            n = len(mms)
            inst = None
            for i, (o, l, r) in enumerate(mms):
                inst = nc.tensor.matmul(o, l, r, start=(i == 0), stop=(i == n - 1))
            tok = PE.issue(inst)
            T.commit(tok, reads, writes)
            return tok

        def dma(E, out, in_, s, reads=(), writes=()):
            for t in T.deps(reads, writes):
                E.wait(t)
            s.cnt += 16
            E.eng.dma_start(out=out, in_=in_).then_inc(s.sem, 16)
            tok = (s, s.cnt)
            T.commit(tok, reads, writes)
            return tok

        def barrier():
            toks = [(E_.src, E_.src.cnt) for E_ in (PE, ACT, DVE)]
            for E_ in (ACT, DVE):
                for t in toks:
                    if t[0] is not E_.src and t[1] > 0:
                        E_.wait(t)

        fb_i = [0]

        def nextf():
            fb_i[0] = (fb_i[0] + 1) % len(fbuf)
            return fb_i[0]

        def kx(c, b0, nb):
            return [("x", c, b) for b in range(b0, b0 + nb)]

        def kxall(b0, nb):
            return [("x", c, b) for c in range(NCH) for b in range(b0, b0 + nb)]

        def kh(c, b0, nb):
            return [("h", c, b) for b in range(b0, b0 + nb)]

        def khall(b0, nb):
            return [("h", c, b) for c in range(NCH) for b in range(b0, b0 + nb)]

        def kbm(slot, b0, nb):
            return [("bm", slot, b) for b in range(b0, b0 + nb)]

        def kbmall(slots, b0, nb):
            return [("bm", s_, b) for s_ in slots for b in range(b0, b0 + nb)]

        s_c = Src(sem("s_cst"))
        dma(SP, cst[:, :], cst_d[:, :], s_c, writes=[("cst",)])
        xtiles = split_blocks(tin_blk, 4)
        s_x = [Src(sem(f"s_x{i}")) for i in range(len(xtiles))]
        xv = xT_d.rearrange("(c p) t -> p c t", p=128)
        for i, (b0, nb) in enumerate(xtiles):
            dma(SP, xT[:, :, b0 * 128:(b0 + nb) * 128], xv[:, :, b0 * 128:(b0 + nb) * 128],
                s_x[i], writes=kxall(b0, nb))
        s_b = Src(sem("s_band"))
        s_b.cnt += 16
        nc.gpsimd.dma_start(out=bands[:, :, :], in_=band_d.rearrange("p (a b) -> p a b", b=128)).then_inc(s_b.sem, 16)
        T.commit((s_b, s_b.cnt), [], [("bands",)])
        op(DVE, lambda: nc.vector.memset(onesM[:, :], 1.0 / D), writes=[("ones",)])
        s_bi = [Src(sem(f"s_bias{i}")) for i in range(2)]
        for h in range(NH):
            fi = h % 2
            dma(SP, fbuf[fi][:, 0:384], bias_d[:, h, :], s_bi[fi], writes=[("f", fi)])
            op(ACT, lambda: nc.scalar.activation(out=EB[:, h, :], in_=fbuf[fi][:, 0:384], func=AF.Exp),
               reads=[("f", fi)], writes=[("eb", h)])
        for l in range(n_layers):
            o = l * CST_PER_L + 48
            op(ACT, lambda: nc.scalar.activation(out=esink[:, l * 8:(l + 1) * 8], in_=cst[:, o:o + 8], func=AF.Exp),
               reads=[("cst",)], writes=[("esink", l)])

        for l in range(n_layers):
            nst = len(split_blocks(regions[l], MAXST))
            for s_i in range(nst):
                off = 0

                def addp(name, n):
                    nonlocal off
                    W.add((l, s_i, name), wst_d[l, :, off:off + n], n)
                    off += n
                for c in range(8):
                    addp(f"cv{c}", 3072)
                for n_ in range(8):
                    addp(f"ao{n_}", 2048)
                addp("wu0", 4096)
                addp("wu1", 4096)
                addp("wp", 2048)
                for n_ in range(8):
                    addp(f"gp{n_}", 1024)
                addp("wk", 2048)
                addp("wv", 2048)
                for q in range(8):
                    addp(f"wq{q}", 1024)
                for n_ in range(8):
                    addp(f"at{n_}", 2048)
                for n_ in range(8):
                    addp(f"wo{n_}", 1024)
                for fg in range(NFG):
                    for f in range(FPG):
                        addp(f"gu{fg}_{f}", 2048)
                    for n_ in range(8):
                        addp(f"wd{fg}_{n_}", 1408)
                assert off == EPP

        def rmsnorm(gb0, nb, goff, dst_fn, dst_keys_fn):
            for t0 in range(gb0 * 128, (gb0 + nb) * 128, 256):
                b = t0 // 128
                op(ACT, lambda: nc.scalar.activation(out=sq[:, :, :], in_=xT[:, :, t0:t0 + 256], func=AF.Square),
                   reads=kxall(b, 2), writes=[("sq",)])
                pb = PS.alloc()
                mm([(psb[pb][:, 0:256], onesM[:, :], sq[:, c, :]) for c in range(NCH)],
                   reads=[("sq",), ("ones",)], writes=[("ps", pb)])
                f1 = nextf()
                op(ACT, lambda: nc.scalar.activation(out=fbuf[f1][:, 0:256], in_=psb[pb][:, 0:256], func=AF.Sqrt,
                                                     bias=EPS, scale=1.0),
                   reads=[("ps", pb)], writes=[("f", f1)])
                PS.free(pb)
                op(DVE, lambda: nc.vector.reciprocal(out=fbuf[f1][:, 0:256], in_=fbuf[f1][:, 0:256]),
                   reads=[("f", f1)], writes=[("f", f1)])
                for c in range(NCH):
                    op(DVE, lambda: nc.vector.scalar_tensor_tensor(
                        out=dst_fn(c, t0, 256), in0=xT[:, c, t0:t0 + 256], scalar=cst[:, goff + c:goff + c + 1],
                        in1=fbuf[f1][:, 0:256], op0=ALU.mult, op1=ALU.mult),
                        reads=kx(c, b, 2) + [("f", f1), ("cst",)], writes=[dst_keys_fn(c, b), dst_keys_fn(c, b + 1)])

        for l in range(n_layers):
            R = regions[l]
            co = l * CST_PER_L
            G_MIX, G_FFN, PSC, CVW, SNK = co, co + 8, co + 16, co + 24, co + 48
            sts = split_blocks(R, MAXST)
            for s_i, (b0, nb) in enumerate(sts):
                b1 = b0 + nb
                e0 = b0 - 1 if s_i > 0 else 0
                e1 = b1 + 1
                ne = e1 - e0
                lo = b0 - e0
                n_own = nb * 128
                own_tiles = split_blocks(nb, 4)
                ext_tiles = split_blocks(ne, 4)

                def wget(name):
                    idx, off, tok = W.get((l, s_i, name))
                    return idx, off, tok

                assert nb % 2 == 0 or True
                if s_i > 0:
                    op(DVE, lambda: nc.vector.tensor_copy(out=hT[:, :, 0:128], in_=stash[:, :, :]),
                       reads=[("stash",)], writes=khall(0, 1))
                nstart = b0 if s_i > 0 else 0
                nnb = e1 - nstart
                blks = list(range(nstart, e1, 2))
                for gb in blks:
                    if gb + 2 > e1:
                        gb = e1 - 2
                    rmsnorm(gb, 2, G_MIX,
                            lambda c, t0, n: hT[:, c, t0 - e0 * 128:t0 - e0 * 128 + n],
                            lambda c, blk: ("h", c, blk - e0))
                if s_i + 1 < len(sts):
                    sl = (b1 - 1 - e0) * 128
                    op(DVE, lambda: nc.vector.tensor_copy(out=stash[:, :, :], in_=hT[:, :, sl:sl + 128]),
                       reads=khall(b1 - 1 - e0, 1), writes=[("stash",)])

                barrier()
                base_u = lo * 128 - 1
                cs = max(base_u, 0)
                ce = lo * 128 + n_own + 1
                ctl = []
                nct = -(-(ce - cs) // 512)
                step = -(-(ce - cs) // nct)
                t_ = cs
                while t_ < ce:
                    ctl.append((t_, min(step, ce - t_)))
                    t_ += step
                for c in range(8):
                    widx, woff, wtok = wget(f"cv{c}")
                    wv = arena[:, woff:woff + 3072].rearrange("p (a k n) -> p a k n", a=3, k=8)
                    if s_i == 0:
                        op(DVE, lambda: nc.vector.memset(ubuf[:, 0:1], 0.0), writes=[("u",)])
                    for (ts, tn) in ctl:
                        hb0 = ts // 128
                        hnb = (ts + tn - 1) // 128 - hb0 + 1
                        pc = PS.alloc()
                        px = PS.alloc()
                        mm([(psb[pc][:, 0:tn], wv[:, 0, k, :], hT[:, k, ts:ts + tn]) for k in range(8)],
                           reads=khall(hb0, hnb), writes=[("ps", pc)], extra=[wtok])
                        mm([(psb[px][:, 0:tn], wv[:, 1, k, :], hT[:, k, ts:ts + tn]) for k in range(8)],
                           reads=khall(hb0, hnb), writes=[("ps", px)])
                        f1 = nextf()
                        op(ACT, lambda: nc.scalar.activation(out=fbuf[f1][:, 0:tn], in_=psb[pc][:, 0:tn], func=AF.Copy),
                           reads=[("ps", pc)], writes=[("f", f1)])
                        PS.free(pc)
                        ui = ts - base_u
                        op(DVE, lambda: nc.vector.tensor_tensor(out=ubuf[:, ui:ui + tn], in0=psb[px][:, 0:tn],
                                                                in1=fbuf[f1][:, 0:tn], op=ALU.mult),
                           reads=[("ps", px), ("f", f1)], writes=[("u",)])
                        PS.free(px)
                    for ti_, (ob, onb) in enumerate(own_tiles):
                        on = onb * 128
                        os_ = ob * 128
                        pbk = PS.alloc()
                        hs = lo * 128 + os_
                        tokb = mm([(psb[pbk][:, 0:on], wv[:, 2, k, :], hT[:, k, hs:hs + on]) for k in range(8)],
                                  reads=khall(lo + ob, onb), writes=[("ps", pbk)])
                        yi = ti_ % 2
                        yb = ybuf[yi]
                        cw = CVW + c * 3
                        op(DVE, lambda: nc.vector.tensor_scalar(out=yb[:, 0:on], in0=ubuf[:, os_ + 1:os_ + 1 + on],
                                                                scalar1=cst[:, cw + 1:cw + 2], scalar2=None, op0=ALU.mult),
                           reads=[("u",), ("cst",)], writes=[("y", yi)])
                        op(DVE, lambda: nc.vector.scalar_tensor_tensor(out=yb[:, 0:on], in0=ubuf[:, os_:os_ + on],
                                                                       scalar=cst[:, cw:cw + 1], in1=yb[:, 0:on],
                                                                       op0=ALU.mult, op1=ALU.add),
                           reads=[("u",), ("y", yi)], writes=[("y", yi)])
                        op(DVE, lambda: nc.vector.scalar_tensor_tensor(out=yb[:, 0:on], in0=ubuf[:, os_ + 2:os_ + 2 + on],
                                                                       scalar=cst[:, cw + 2:cw + 3], in1=yb[:, 0:on],
                                                                       op0=ALU.mult, op1=ALU.add),
                           reads=[("u",), ("y", yi)], writes=[("y", yi)])
                        op(DVE, lambda: nc.vector.tensor_tensor(out=BM[:, c, os_:os_ + on], in0=psb[pbk][:, 0:on],
                                                                in1=yb[:, 0:on], op=ALU.mult),
                           reads=[("ps", pbk), ("y", yi)], writes=kbm(c, ob, onb))
                        PS.free(pbk)
                    W.release(widx, tokb)

                def branch_out(prefix, first, ymm_fn, post_scale=None):
                    for n_ in range(8):
                        widx, woff, wtok = wget(f"{prefix}{n_}")
                        last = None
                        for (ob, onb) in own_tiles:
                            on = onb * 128
                            os_ = ob * 128
                            hs = lo * 128 + os_
                            py = PS.alloc()
                            pg = PS.alloc()
                            mms, rk, gw = ymm_fn(n_, woff, py, os_, on, ob, onb)
                            mm(mms, reads=rk, writes=[("ps", py)], extra=[wtok])
                            last = mm([(psb[pg][:, 0:on], gw[:, k, :], hT[:, k, hs:hs + on]) for k in range(8)],
                                      reads=khall(lo + ob, onb), writes=[("ps", pg)])
                            f1 = nextf()
                            op(ACT, lambda: nc.scalar.activation(out=fbuf[f1][:, 0:on], in_=psb[pg][:, 0:on], func=AF.Sigmoid),
                               reads=[("ps", pg)], writes=[("f", f1)])
                            PS.free(pg)
                            if first:
                                op(DVE, lambda: nc.vector.tensor_tensor(out=BM[:, 8 + n_, os_:os_ + on], in0=psb[py][:, 0:on],
                                                                        in1=fbuf[f1][:, 0:on], op=ALU.mult),
                                   reads=[("ps", py), ("f", f1)], writes=kbm(8 + n_, ob, onb))
                            else:
                                if post_scale is None:
                                    op(DVE, lambda: nc.vector.tensor_tensor(out=fbuf[f1][:, 0:on], in0=psb[py][:, 0:on],
                                                                            in1=fbuf[f1][:, 0:on], op=ALU.mult),
                                       reads=[("ps", py), ("f", f1)], writes=[("f", f1)])
                                else:
                                    sc = post_scale + n_
                                    op(DVE, lambda: nc.vector.scalar_tensor_tensor(
                                        out=fbuf[f1][:, 0:on], in0=psb[py][:, 0:on], scalar=cst[:, sc:sc + 1],
                                        in1=fbuf[f1][:, 0:on], op0=ALU.mult, op1=ALU.mult),
                                        reads=[("ps", py), ("f", f1), ("cst",)], writes=[("f", f1)])
                                op(DVE, lambda: nc.vector.tensor_tensor(out=BM[:, 8 + n_, os_:os_ + on], in0=BM[:, 8 + n_, os_:os_ + on],
                                                                        in1=fbuf[f1][:, 0:on], op=ALU.add),
                                   reads=kbm(8 + n_, ob, onb) + [("f", f1)], writes=kbm(8 + n_, ob, onb))
                            PS.free(py)
                        W.release(widx, last)

                def ymm_full(n_, woff, py, os_, on, ob, onb):
                    wv2 = arena[:, woff:woff + 2048].rearrange("p (a k n) -> p a k n", a=2, k=8)
                    mms = [(psb[py][:, 0:on], wv2[:, 0, c, :], BM[:, c, os_:os_ + on]) for c in range(8)]
                    return mms, kbmall(range(8), ob, onb), wv2[:, 1]

                branch_out("ao", True, ymm_full)

                barrier()
                i0, o0, t0_ = wget("wu0")
                i1, o1, t1_ = wget("wu1")
                ip, op_, tp_ = wget("wp")
                wu = [arena[:, o0:o0 + 4096].rearrange("p (k n) -> p k n", k=8),
                      arena[:, o1:o1 + 4096].rearrange("p (k n) -> p k n", k=8)]
                wp = arena[:, op_:op_ + 2048].rearrange("p (g k n) -> p g k n", g=4, k=2)
                lastu = None
                for i in range(ne):
                    slot = i % 4
                    for hf in range(2):
                        pu = PS.alloc()
                        lastu = mm([(psb[pu][:, :], hT[:, k, i * 128:(i + 1) * 128], wu[hf][:, k, :]) for k in range(8)],
                                   reads=khall(i, 1), writes=[("ps", pu)], extra=[t0_, t1_])
                        if hf == 0:
                            op(ACT, lambda: nc.scalar.activation(out=Usb[:, slot, 0:512], in_=psb[pu][:, :], func=AF.Copy),
                               reads=[("ps", pu)], writes=[("usb", slot, 0)])
                        else:
                            op(DVE, lambda: nc.vector.tensor_copy(out=Usb[:, slot, 512:1024], in_=psb[pu][:, :]),
                               reads=[("ps", pu)], writes=[("usb", slot, 1)])
                        PS.free(pu)
                    j = i - 1
                    if j >= lo and j < lo + nb:
                        gj = e0 + j
                        srcs = [d for d in (-1, 0, 1) if 0 <= j + d < ne]
                        for half in range(2):
                            pp = PS.alloc()
                            for cc in range(4):
                                c = half * 4 + cc
                                g = c // 2
                                mms = []
                                for d in srcs:
                                    if gj == 0:
                                        bnd = bands[:, 12 + g, :] if d == 0 else bands[:, g * 3 + 2, :]
                                    else:
                                        bnd = bands[:, g * 3 + (d + 1), :]
                                    mms.append((psb[pp][:, cc * 128:(cc + 1) * 128],
                                                Usb[:, (j + d) % 4, c * 128:(c + 1) * 128], bnd))
                                mm(mms, reads=[("usb", (j + d) % 4, c // 4) for d in srcs] + [("bands",)],
                                   writes=[("ps", pp)])
                            ob_ = j - lo
                            if half == 0:
                                op(ACT, lambda: nc.scalar.activation(
                                    out=BM[:, 0:4, ob_ * 128:(ob_ + 1) * 128],
                                    in_=psb[pp][:, :].rearrange("p (a b) -> p a b", a=4), func=AF.Copy),
                                    reads=[("ps", pp)], writes=kbmall(range(0, 4), ob_, 1))
                            else:
                                op(DVE, lambda: nc.vector.tensor_copy(
                                    out=BM[:, 4:8, ob_ * 128:(ob_ + 1) * 128],
                                    in_=psb[pp][:, :].rearrange("p (a b) -> p a b", a=4)),
                                    reads=[("ps", pp)], writes=kbmall(range(4, 8), ob_, 1))
                            PS.free(pp)
                W.release(i0, lastu)
                W.release(i1, lastu)

                def ymm_pool(n_, woff, py, os_, on, ob, onb):
                    g = n_ // 2
                    gw = arena[:, woff:woff + 1024].rearrange("p (k n) -> p k n", k=8)
                    mms = [(psb[py][:, 0:on], wp[:, g, kk, (n_ % 2) * 128:(n_ % 2) * 128 + 128],
                            BM[:, 2 * g + kk, os_:os_ + on]) for kk in range(2)]
                    return mms, kbmall([2 * g, 2 * g + 1], ob, onb), gw

                PE.wait(tp_)
                branch_out("gp", False, ymm_pool, post_scale=PSC)
                W.release(ip, (PE.src, PE.src.cnt))

                barrier()
                op(DVE, lambda: nc.vector.memset(Vsb[:, :, :, :], 1.0), writes=[("v", i) for i in range(MAXEXT)])
                op(DVE, lambda: nc.vector.memset(QTA[:, :], 0.0), writes=[("qta",)])
                op(DVE, lambda: nc.vector.memset(QTB[:, :], 0.0), writes=[("qtb",)])
                ik, ok_, tk_ = wget("wk")
                iv, ov_, tv_ = wget("wv")
                wk = arena[:, ok_:ok_ + 2048].rearrange("p (k n) -> p k n", k=8)
                wvv = arena[:, ov_:ov_ + 2048].rearrange("p (k n) -> p k n", k=8)
                lastk = None
                for kc in range(2):
                    for (eb, enb) in ext_tiles:
                        en = enb * 128
                        pk = PS.alloc()
                        lastk = mm([(psb[pk][:, 0:en], wk[:, k, kc * 128:(kc + 1) * 128], hT[:, k, eb * 128:eb * 128 + en])
                                    for k in range(8)], reads=khall(eb, enb), writes=[("ps", pk)], extra=[tk_])
                        op(ACT, lambda: nc.scalar.activation(out=KT[:, kc, eb * 128:eb * 128 + en], in_=psb[pk][:, 0:en], func=AF.Copy),
                           reads=[("ps", pk)], writes=[("kt", kc, b) for b in range(eb, eb + enb)])
                        PS.free(pk)
                W.release(ik, lastk)
                lastv = None
                for i in range(ne):
                    pv = PS.alloc()
                    lastv = mm([(psb[pv][:, 0:256], hT[:, k, i * 128:(i + 1) * 128], wvv[:, k, :]) for k in range(8)],
                               reads=khall(i, 1), writes=[("ps", pv)], extra=[tv_])
                    pvv = psb[pv][:, 0:256].rearrange("p (a b c) -> p a b c", a=2, b=2)
                    op(DVE, lambda: nc.vector.tensor_copy(out=Vsb[:, i, 0::2, 0:64], in_=pvv[:, :, 0, :]),
                       reads=[("ps", pv)], writes=[("v", i)])
                    op(DVE, lambda: nc.vector.tensor_copy(out=Vsb[:, i, 1::2, 64:128], in_=pvv[:, :, 1, :]),
                       reads=[("ps", pv)], writes=[("v", i)])
                    PS.free(pv)
                W.release(iv, lastv)

                for qc in range(8):
                    iq, oq, tq = wget(f"wq{qc}")
                    wq = arena[:, oq:oq + 1024].rearrange("p (k n) -> p k n", k=8)
                    kvc = qc // 4
                    lastq = None
                    for (ob, onb) in own_tiles:
                        on = onb * 128
                        os_ = ob * 128
                        hs = lo * 128 + os_
                        pq = PS.alloc()
                        lastq = mm([(psb[pq][:, 0:on], wq[:, k, :], hT[:, k, hs:hs + on]) for k in range(8)],
                                   reads=khall(lo + ob, onb), writes=[("ps", pq)], extra=[tq])
                        op(ACT, lambda: nc.scalar.activation(out=QTA[0:64, os_:os_ + on], in_=psb[pq][0:64, 0:on],
                                                             func=AF.Copy, scale=0.125),
                           reads=[("ps", pq)], writes=[("qta",)])
                        op(ACT, lambda: nc.scalar.activation(out=QTB[64:128, os_:os_ + on], in_=psb[pq][64:128, 0:on],
                                                             func=AF.Copy, scale=0.125),
                           reads=[("ps", pq)], writes=[("qtb",)])
                        PS.free(pq)
                    W.release(iq, lastq)
                    QT = [QTA, QTB]
                    qrange = {}
                    po = [None, None]
                    for j in range(ne + 1):
                        if j < ne:
                            qlo = max(j - lo - 1, 0)
                            qhi = min(j - lo + 1, nb - 1)
                            if qlo <= qhi:
                                nq = qhi - qlo + 1
                                qrange[j] = (qlo, qhi)
                                dlo = qlo + lo - j
                                for r in range(2):
                                    pss = PS.alloc()
                                    mm([(psb[pss][:, 0:nq * 128], KT[:, kvc, j * 128:(j + 1) * 128],
                                         QT[r][:, qlo * 128:(qhi + 1) * 128])],
                                       reads=[("kt", kvc, j), ("qta",) if r == 0 else ("qtb",)], writes=[("ps", pss)])
                                    eb_ = Ebuf[r][j % 2]
                                    op(ACT, lambda: nc.scalar.activation(out=eb_[:, 0:nq * 128], in_=psb[pss][:, 0:nq * 128], func=AF.Exp),
                                       reads=[("ps", pss)], writes=[("e", r, j % 2)])
                                    PS.free(pss)
                                    hidx = 2 * qc + r
                                    op(DVE, lambda: nc.vector.tensor_tensor(
                                        out=PT[r][j % 4][:, 0:nq * 128], in0=eb_[:, 0:nq * 128],
                                        in1=EB[:, hidx, (dlo + 1) * 128:(dlo + 1 + nq) * 128], op=ALU.mult),
                                        reads=[("e", r, j % 2), ("eb", hidx)], writes=[("pt", r, j % 4)])
                        qb = j - lo - 1
                        if 0 <= qb < nb:
                            sl = qb % 4
                            if sl == 0:
                                po = [PS.alloc(), PS.alloc()]
                            for r in range(2):
                                kv = kvc * 2 + r
                                jj_list = [jj for jj in (qb + lo - 1, qb + lo, qb + lo + 1) if jj in qrange]
                                mms = []
                                for jj in jj_list:
                                    co_ = (qb - qrange[jj][0]) * 128
                                    mms.append((psb[po[r]][:, sl * 128:(sl + 1) * 128], Vsb[:, jj, kv, :],
                                                PT[r][jj % 4][:, co_:co_ + 128]))
                                mm(mms, reads=[("v", jj) for jj in jj_list] + [("pt", r, jj % 4) for jj in jj_list],
                                   writes=[("ps", po[r])])
                            if sl == 3 or qb == nb - 1:
                                nn = (sl + 1) * 128
                                q0 = (qb - sl) * 128
                                es_ = l * 8 + qc
                                op(DVE, lambda: nc.vector.tensor_copy(out=rden[0:64, 0:nn], in_=psb[po[0]][64:128, 0:nn]),
                                   reads=[("ps", po[0])], writes=[("rden", 0)])
                                op(DVE, lambda: nc.vector.tensor_copy(out=rden[64:128, 0:nn], in_=psb[po[1]][0:64, 0:nn]),
                                   reads=[("ps", po[1])], writes=[("rden", 1)])
                                op(DVE, lambda: nc.vector.tensor_scalar(out=rden[:, 0:nn], in0=rden[:, 0:nn],
                                                                        scalar1=esink[:, es_:es_ + 1], scalar2=None, op0=ALU.add),
                                   reads=[("rden", 0), ("rden", 1), ("esink", l)], writes=[("rden", 0), ("rden", 1)])
                                op(DVE, lambda: nc.vector.reciprocal(out=rden[:, 0:nn], in_=rden[:, 0:nn]),
                                   reads=[("rden", 0), ("rden", 1)], writes=[("rden", 0), ("rden", 1)])
                                op(DVE, lambda: nc.vector.tensor_tensor(out=BM[0:64, qc, q0:q0 + nn], in0=psb[po[0]][0:64, 0:nn],
                                                                        in1=rden[0:64, 0:nn], op=ALU.mult),
                                   reads=[("ps", po[0]), ("rden", 0)], writes=kbm(qc, qb - sl, sl + 1))
                                op(DVE, lambda: nc.vector.tensor_tensor(out=BM[64:128, qc, q0:q0 + nn], in0=psb[po[1]][64:128, 0:nn],
                                                                        in1=rden[64:128, 0:nn], op=ALU.mult),
                                   reads=[("ps", po[1]), ("rden", 1)], writes=kbm(qc, qb - sl, sl + 1))
                                PS.free(po[0])
                                PS.free(po[1])

                def ymm_attn(n_, woff, py, os_, on, ob, onb):
                    wv2 = arena[:, woff:woff + 2048].rearrange("p (a k n) -> p a k n", a=2, k=8)
                    mms = [(psb[py][:, 0:on], wv2[:, 0, c, :], BM[:, c, os_:os_ + on]) for c in range(8)]
                    return mms, kbmall(range(8), ob, onb), wv2[:, 1]

                branch_out("at", False, ymm_attn)

                for n_ in range(8):
                    widx, woff, wtok = wget(f"wo{n_}")
                    wo = arena[:, woff:woff + 1024].rearrange("p (k n) -> p k n", k=8)
                    last = None
                    for (ob, onb) in own_tiles:
                        on = onb * 128
                        os_ = ob * 128
                        gs = (b0 + ob) * 128
                        px = PS.alloc()
                        last = mm([(psb[px][:, 0:on], wo[:, c, :], BM[:, 8 + c, os_:os_ + on]) for c in range(8)],
                                  reads=kbmall(range(8, 16), ob, onb), writes=[("ps", px)], extra=[wtok])
                        op(DVE, lambda: nc.vector.tensor_tensor(out=xT[:, n_, gs:gs + on], in0=xT[:, n_, gs:gs + on],
                                                                in1=psb[px][:, 0:on], op=ALU.add),
                           reads=[("ps", px)] + kx(n_, b0 + ob, onb), writes=kx(n_, b0 + ob, onb))
                        PS.free(px)
                    W.release(widx, last)

                blks = list(range(b0, b1, 2))
                for gb in blks:
                    if gb + 2 > b1:
                        gb = b1 - 2
                    rmsnorm(gb, 2, G_FFN,
                            lambda c, t0, n: hT[:, c, t0 - e0 * 128:t0 - e0 * 128 + n],
                            lambda c, blk: ("h", c, blk - e0))
                for fg in range(NFG):
                    for f in range(FPG):
                        widx, woff, wtok = wget(f"gu{fg}_{f}")
                        wg = arena[:, woff:woff + 2048].rearrange("p (a k n) -> p a k n", a=2, k=8)
                        last = None
                        for (ob, onb) in own_tiles:
                            on = onb * 128
                            os_ = ob * 128
                            hs = lo * 128 + os_
                            pg = PS.alloc()
                            pu = PS.alloc()
                            mm([(psb[pg][:, 0:on], wg[:, 0, k, :], hT[:, k, hs:hs + on]) for k in range(8)],
                               reads=khall(lo + ob, onb), writes=[("ps", pg)], extra=[wtok])
                            last = mm([(psb[pu][:, 0:on], wg[:, 1, k, :], hT[:, k, hs:hs + on]) for k in range(8)],
                                      reads=khall(lo + ob, onb), writes=[("ps", pu)])
                            f1 = nextf()
                            op(ACT, lambda: nc.scalar.activation(out=fbuf[f1][:, 0:on], in_=psb[pg][:, 0:on], func=AF.Silu),
                               reads=[("ps", pg)], writes=[("f", f1)])
                            PS.free(pg)
                            op(DVE, lambda: nc.vector.tensor_tensor(out=BM[:, f, os_:os_ + on], in0=psb[pu][:, 0:on],
                                                                    in1=fbuf[f1][:, 0:on], op=ALU.mult),
                               reads=[("ps", pu), ("f", f1)], writes=kbm(f, ob, onb))
                            PS.free(pu)
                        W.release(widx, last)
                    for n_ in range(8):
                        widx, woff, wtok = wget(f"wd{fg}_{n_}")
                        wd = arena[:, woff:woff + 1408].rearrange("p (f n) -> p f n", f=FPG)
                        last = None
                        for (ob, onb) in own_tiles:
                            on = onb * 128
                            os_ = ob * 128
                            gs = (b0 + ob) * 128
                            pd = PS.alloc()
                            last = mm([(psb[pd][:, 0:on], wd[:, f, :], BM[:, f, os_:os_ + on]) for f in range(FPG)],
                                      reads=kbmall(range(FPG), ob, onb), writes=[("ps", pd)], extra=[wtok])
                            op(DVE, lambda: nc.vector.tensor_tensor(out=xT[:, n_, gs:gs + on], in0=xT[:, n_, gs:gs + on],
                                                                    in1=psb[pd][:, 0:on], op=ALU.add),
                               reads=[("ps", pd)] + kx(n_, b0 + ob, onb), writes=kx(n_, b0 + ob, onb))
                            PS.free(pd)
                        W.release(widx, last)

        s_o = Src(sem("s_out"))
        ov = out_d.rearrange("(c p) t -> p c t", p=128)
        if do_final:
            goff = n_layers * CST_PER_L
            for gb in range(0, OUTB, 2):
                rmsnorm(gb, 2, goff,
                        lambda c, t0, n: xT[:, c, t0:t0 + n],
                        lambda c, blk: ("x", c, blk))
        for (ob, onb) in split_blocks(OUTB, 4):
            dma(SP, ov[:, :, ob * 128:(ob + onb) * 128], xT[:, :, ob * 128:(ob + onb) * 128], s_o,
                reads=kxall(ob, onb))
        nc.sync.wait_ge(s_o.sem, s_o.cnt)
        for E in (PE, ACT, DVE):
            nc.sync.wait_ge(E.src.sem, E.src.cnt)
    return nc


def _kp(mat):
    n = mat.shape[1]
    return mat.reshape(8, 128, n).transpose(1, 0, 2)


def pack_layer(w_in, w_a_out, w_pool, w_attn_out, w_o, w_gu, w_down):
    Bc, Cc, Xc, Uc, Qc, Kc, Vc, Gac, Gpc, Gtc = 0, 1024, 2048, 3072, 4096, 5120, 5376, 5632, 6656, 7680
    pcs = []

    def add(a):
        pcs.append(np.ascontiguousarray(a, dtype=np.float32).reshape(128, -1))
    for c in range(8):
        add(np.stack([_kp(w_in[:, Cc + c * 128:Cc + (c + 1) * 128]),
                      _kp(w_in[:, Xc + c * 128:Xc + (c + 1) * 128]),
                      _kp(w_in[:, Bc + c * 128:Bc + (c + 1) * 128])], axis=1))
    for n in range(8):
        add(np.stack([_kp(w_a_out[:, n * 128:(n + 1) * 128]),
                      _kp(w_in[:, Gac + n * 128:Gac + (n + 1) * 128])], axis=1))
    add(_kp(w_in[:, Uc:Uc + 512]))
    add(_kp(w_in[:, Uc + 512:Uc + 1024]))
    add(w_pool.reshape(4, 2, 128, 256).transpose(2, 0, 1, 3))
    for n in range(8):
        add(_kp(w_in[:, Gpc + n * 128:Gpc + (n + 1) * 128]))
    add(_kp(w_in[:, Kc:Kc + 256]))
    add(_kp(w_in[:, Vc:Vc + 256]))
    qcols = np.concatenate([np.arange(Qc + h * 64, Qc + (h + 1) * 64) for h in HEAD_ORDER])
    wq = w_in[:, qcols]
    for q in range(8):
        add(_kp(wq[:, q * 128:(q + 1) * 128]))
    orow = np.concatenate([np.arange(h * 64, (h + 1) * 64) for h in HEAD_ORDER])
    wao = w_attn_out[orow, :]
    for n in range(8):
        add(np.stack([_kp(wao[:, n * 128:(n + 1) * 128]),
                      _kp(w_in[:, Gtc + n * 128:Gtc + (n + 1) * 128])], axis=1))
    for n in range(8):
        add(_kp(w_o[:, n * 128:(n + 1) * 128]))
    for fg in range(NFG):
        for f in range(FPG):
            fi = fg * FPG + f
            add(np.stack([_kp(w_gu[:, fi * 128:(fi + 1) * 128]),
                          _kp(w_gu[:, DFF + fi * 128:DFF + (fi + 1) * 128])], axis=1))
        wdg = w_down[fg * FPG * 128:(fg + 1) * FPG * 128, :].reshape(FPG, 128, D).transpose(1, 0, 2)
        for n in range(8):
            add(wdg[:, :, n * 128:(n + 1) * 128])
    out = np.concatenate(pcs, axis=1)
    assert out.shape == (128, EPP), out.shape
    return out


def t5_bucket_np(rel):
    import math
    import jax
    import jax.numpy as jnp
    try:
        dev = jax.devices("cpu")[0]
    except Exception:
        dev = None
    ctx = jax.default_device(dev) if dev is not None else contextlib.nullcontext()
    with ctx:
        rel = jnp.asarray(rel, dtype=jnp.int32)
        n_buckets, max_distance = 32, 128
        half = n_buckets // 2
        max_exact = half // 2
        ret = jnp.where(rel > 0, half, 0)
        n = jnp.abs(rel)
        nf = jnp.maximum(n, 1).astype(jnp.float32)
        large = max_exact + (jnp.log(nf / max_exact) / math.log(max_distance / max_exact)
                             * (half - max_exact)).astype(jnp.int32)
        large = jnp.minimum(large, half - 1)
        return np.asarray(ret + jnp.where(n < max_exact, n, large))


def make_bias_tiles(rel_bias, sgn):
    k = np.arange(128)[:, None]
    qq = np.arange(384)[None, :]
    d = qq // 128 - 1
    q = qq % 128
    rl = k - q - 128 * d
    valid = np.abs(rl) <= 128
    bk = t5_bucket_np(sgn * rl)
    out = np.empty((128, NH, 384), np.float32)
    for hi, h in enumerate(HEAD_ORDER):
        out[:, hi, :] = np.where(valid, rel_bias[bk, h], np.float32(NEGB))
    return out


def make_bands(sgn):
    out = np.zeros((128, 16, 128), np.float32)
    tp = np.arange(128)[:, None]
    t = np.arange(128)[None, :]
    for g, w in enumerate(POOL_W):
        if sgn > 0:
            lo_o, hi_o = -(w // 2), (w - 1 - w // 2)
        else:
            lo_o, hi_o = -(w - 1 - w // 2), (w // 2)
        for d in (-1, 0, 1):
            off = 128 * d + tp - t
            m = ((off >= lo_o) & (off <= hi_o)).astype(np.float32) / np.float32(w)
            if d == 0:
                m = m - (tp == t).astype(np.float32)
            out[:, g * 3 + d + 1, :] = m
        off = tp - t
        inwin = (off >= lo_o) & (off <= hi_o)
        cnt = ((t + hi_o) - np.maximum(t + lo_o, 0) + 1).astype(np.float32)
        m = inwin.astype(np.float32) / cnt - (tp == t).astype(np.float32)
        out[:, 12 + g, :] = m
    return out.reshape(128, 16 * 128)


def make_cst(layers, conv_w, pool_scale, attn_sink, g_mix, g_ffn, g_final, sgn):
    nl = len(layers)
    out = np.zeros((128, nl * CST_PER_L + 8), np.float32)

    def pc(v):
        return v.reshape(8, 128).T
    for i, l in enumerate(layers):
        o = i * CST_PER_L
        out[:, o:o + 8] = pc(g_mix[l])
        out[:, o + 8:o + 16] = pc(g_ffn[l])
        out[:, o + 16:o + 24] = pc(pool_scale[l])
        cw = conv_w[l, :, 0, :]
        if sgn < 0:
            cw = cw[::-1]
        out[:, o + 24:o + 48] = np.stack([pc(cw[k]) for k in range(3)], axis=2).reshape(128, 24)
        for qc in range(8):
            out[0:64, o + 48 + qc] = attn_sink[l, HEAD_ORDER[2 * qc]]
            out[64:128, o + 48 + qc] = attn_sink[l, HEAD_ORDER[2 * qc + 1]]
    out[:, nl * CST_PER_L:] = pc(g_final)
    return out


FUSED = True
DBG_LAYERS = None


def kernel(x, w_in, conv_w, w_a_out, w_pool, pool_scale, w_attn_out, attn_sink, w_o,
           g_mix, g_ffn, w_gu, w_down, rel_bias, g_final):
    f = lambda a: np.asarray(a, dtype=np.float32)
    x, w_in, conv_w, w_a_out, w_pool, pool_scale = map(f, (x, w_in, conv_w, w_a_out, w_pool, pool_scale))
    w_attn_out, attn_sink, w_o, g_mix, g_ffn, w_gu, w_down, rel_bias, g_final = map(
        f, (w_attn_out, attn_sink, w_o, g_mix, g_ffn, w_gu, w_down, rel_bias, g_final))
    wst = [pack_layer(w_in[l], w_a_out[l], w_pool[l], w_attn_out[l], w_o[l], w_gu[l], w_down[l])
           for l in range(DEPTH)]
    bias_t = {s: make_bias_tiles(rel_bias, s) for s in (1, -1)}
    bands = {s: make_bands(s) for s in (1, -1)}
    half = SEQ // 2

    def run(layers, regions, tin_blk, do_final, xin):
        nc = build_program(len(layers), regions, tin_blk, do_final)
        wst_l = np.ascontiguousarray(np.stack([wst[l] for l in layers], axis=0))
        in_maps = []
        for c in range(8):
            sgn = 1 if c % 2 == 0 else -1
            in_maps.append({
                "xT": np.ascontiguousarray(xin[c].T),
                "wst": wst_l,
                "cst": make_cst(layers, conv_w, pool_scale, attn_sink, g_mix, g_ffn, g_final, sgn),
                "biasT": bias_t[sgn],
                "bands": bands[sgn],
            })
        res = run_bass_kernel_spmd(nc, in_maps, core_ids=list(range(8)))
        return [np.asarray(r["outT"]).T for r in res.results]

    def local_slices(xfull, ntok):
        outs = []
        for c in range(8):
            b, hf = c // 2, c % 2
            if hf == 0:
                outs.append(xfull[b, 0:ntok, :])
            else:
                outs.append(xfull[b, ::-1, :][0:ntok, :])
        return outs

    def assemble(parts):
        out = np.empty((BATCH, SEQ, D), np.float32)
        for c in range(8):
            b, hf = c // 2, c % 2
            if hf == 0:
                out[b, 0:half, :] = parts[c]
            else:
                out[b, half:, :] = parts[c][::-1, :]
        return out

    nrun = DEPTH if DBG_LAYERS is None else DBG_LAYERS
    fin = DBG_LAYERS is None
    if FUSED:
        regions = [OWN_BLK + (nrun - 1 - l) for l in range(nrun)]
        tin = regions[0] + 1
        parts = run(list(range(nrun)), regions, tin, fin, local_slices(x, tin * 128))
        return assemble(parts)
    else:
        cur = x
        for l in range(nrun):
            parts = run([l], [OWN_BLK], OWN_BLK + 1, fin and l == nrun - 1, local_slices(cur, (OWN_BLK + 1) * 128))
            cur = assemble(parts)
        return cur
```

```python
import numpy as np
import contextlib
import concourse.bass as bass
import concourse.mybir as mybir
from concourse.bass_utils import run_bass_kernel_spmd

F32 = mybir.dt.float32
BF16 = mybir.dt.bfloat16
AF = mybir.ActivationFunctionType
ALU = mybir.AluOpType

D = 1024
NCH = 8
SEQ = 4096
BATCH = 4
DEPTH = 4
NH = 16
HD = 64
DFF = 2816
NFC = 22
NFG = 2
FPG = 11
EPS = 1e-6
OWN_BLK = 16
MAXST = 6
MAXEXT = 8
LAG = 2
NPT = 6
TT = MAXST * 128
HEAD_ORDER = [0, 4, 1, 5, 2, 6, 3, 7, 8, 12, 9, 13, 10, 14, 11, 15]
POOL_W = (2, 4, 8, 16)
EPP = 163840
ARENA = 12288
NWSEM = 12
NEGB = -30000.0

CST_PER_L = 56


class Src:
    def __init__(self, sem):
        self.sem = sem
        self.cnt = 0


class Eng:
    def __init__(self, eng, sem, name):
        self.eng = eng
        self.src = Src(sem)
        self.name = name
        self.seen = {}

    def wait(self, tok):
        if tok is None:
            return
        src, c = tok
        if self.seen.get(id(src), 0) >= c:
            return
        self.eng.wait_ge(src.sem, c)
        self.seen[id(src)] = c

    def issue(self, inst):
        self.src.cnt += 1
        inst.then_inc(self.src.sem, 1)
        return (self.src, self.src.cnt)


class Tracker:
    def __init__(self):
        self.w = {}
        self.r = {}

    def deps(self, reads, writes):
        toks = []
        for k in reads:
            t = self.w.get(k)
            if t is not None:
                toks.append(t)
        for k in writes:
            t = self.w.get(k)
            if t is not None:
                toks.append(t)
            rr = self.r.get(k)
            if rr:
                toks.extend(rr.values())
        return toks

    def commit(self, tok, reads, writes):
        for k in reads:
            rr = self.r.setdefault(k, {})
            rr[id(tok[0])] = tok
        for k in writes:
            self.w[k] = tok
            self.r[k] = {}


class PsumAlloc:
    def __init__(self, banks):
        self.free_list = list(range(len(banks)))
        self.banks = banks

    def alloc(self):
        assert self.free_list, "out of PSUM banks"
        return self.free_list.pop(0)

    def free(self, b):
        self.free_list.append(b)


class WStream:
    def __init__(self, nc, pool, arena, sems):
        self.nc = nc
        self.pool = pool
        self.arena = arena
        self.sems = [Src(s) for s in sems]
        self.plan = []
        self.ni = 0
        self.ng = 0
        self.head = 0
        self.regions = []
        self.rel = {}
        self.info = {}

    def add(self, name, dram_ap, n):
        self.plan.append((name, dram_ap, n))

    def pump(self):
        while self.ni < len(self.plan):
            name, dap, n = self.plan[self.ni]
            idx = self.ni
            off = self.head
            if off + n > ARENA:
                off = 0
            conflicts = [rg for rg in self.regions if rg[0] < off + n and off < rg[0] + rg[1]]
            if any(rg[2] not in self.rel for rg in conflicts):
                return
            old = idx - NWSEM
            if old >= 0 and old not in self.rel:
                return
            for rg in conflicts:
                self.pool.wait(self.rel[rg[2]])
                self.regions.remove(rg)
            if old >= 0:
                self.pool.wait(self.rel[old])
            s = self.sems[idx % NWSEM]
            s.cnt += 16
            self.nc.gpsimd.dma_start(out=self.arena[:, off:off + n], in_=dap).then_inc(s.sem, 16)
            self.info[idx] = (off, n, (s, s.cnt))
            self.regions.append((off, n, idx))
            self.head = off + n
            self.ni += 1

    def get(self, name):
        self.pump()
        pname, _, n = self.plan[self.ng]
        assert pname == name, (pname, name)
        assert self.ng in self.info, f"weight arena deadlock at {name}"
        off, n, tok = self.info[self.ng]
        idx = self.ng
        self.ng += 1
        return idx, off, tok

    def release(self, idx, tok):
        self.rel[idx] = tok
        self.pump()


def split_blocks(n, maxpart):
    parts = -(-n // maxpart)
    base = n // parts
    rem = n % parts
    out = []
    s = 0
    for i in range(parts):
        m = base + (1 if i < rem else 0)
        out.append((s, m))
        s += m
    return out


def build_program(n_layers, regions, tin_blk, do_final):
    nc = bass.Bass("TRN2", target_bir_lowering=False)
    TIN = tin_blk * 128
    OUTB = regions[-1]
    ncst = n_layers * CST_PER_L + 8
    xT_d = nc.dram_tensor("xT", [D, TIN], F32, kind="ExternalInput").ap()
    wst_d = nc.dram_tensor("wst", [n_layers, 128, EPP], F32, kind="ExternalInput").ap()
    cst_d = nc.dram_tensor("cst", [128, ncst], F32, kind="ExternalInput").ap()
    bias_d = nc.dram_tensor("biasT", [128, NH, 384], F32, kind="ExternalInput").ap()
    band_d = nc.dram_tensor("bands", [128, 16 * 128], F32, kind="ExternalInput").ap()
    out_d = nc.dram_tensor("outT", [D, OUTB * 128], F32, kind="ExternalOutput").ap()

    es = contextlib.ExitStack()
    with es:
        def sb(name, shape, dt):
            return es.enter_context(nc.sbuf_tensor(name, shape, dt))

        def sem(name):
            return es.enter_context(nc.semaphore(name))

        xT = sb("xT_sb", [128, NCH, TIN], F32)
        hT = sb("hT", [128, NCH, MAXEXT * 128], BF16)
        stash = sb("stash", [128, NCH, 128], BF16)
        BM = sb("BM", [128, 16, TT], BF16)
        arena = sb("arena", [128, ARENA], BF16)
        cst = sb("cst_sb", [128, ncst], F32)
        esink = sb("esink", [128, n_layers * 8], F32)
        EB = sb("EB", [128, NH, 384], BF16)
        bands = sb("bands_sb", [128, 16, 128], BF16)
        onesM = sb("onesM", [128, 128], BF16)
        fbuf = [sb(f"fbuf{i}", [128, 512], F32) for i in range(4)]
        SCRN = 14848
        scr = sb("scr", [128, SCRN], BF16)
        sq = scr[:, 0:NCH * 256].rearrange("p (a b) -> p a b", a=NCH)
        ubuf = scr[:, 0:2 * (TT + 2)].bitcast(F32)
        ybuf = [scr[:, 2048 + i * 1024:2048 + (i + 1) * 1024].bitcast(F32) for i in range(2)]
        Usb = scr[:, 0:4096].rearrange("p (a b) -> p a b", a=4)
        _o = 0
        KT = scr[:, _o:_o + 2 * MAXEXT * 128].rearrange("p (a b) -> p a b", a=2); _o += 2 * MAXEXT * 128
        Vsb = scr[:, _o:_o + MAXEXT * 512].rearrange("p (a b c) -> p a b c", a=MAXEXT, b=4); _o += MAXEXT * 512
        QTA = scr[:, _o:_o + TT]; _o += TT
        QTB = scr[:, _o:_o + TT]; _o += TT
        Ebuf = [[None, None], [None, None]]
        for r in range(2):
            for i in range(2):
                Ebuf[r][i] = scr[:, _o:_o + 384]; _o += 384
        PT = [[None] * NPT, [None] * NPT]
        for r in range(2):
            for i in range(NPT):
                PT[r][i] = scr[:, _o:_o + 384]; _o += 384
        rden = scr[:, _o:_o + 1024].bitcast(F32); _o += 1024
        assert _o <= SCRN, _o
        psb = [es.enter_context(nc.psum_tensor(f"ps{i}", [128, 512], F32)) for i in range(8)]

        PE = Eng(nc.tensor, sem("s_pe"), "pe")
        ACT = Eng(nc.scalar, sem("s_act"), "act")
        DVE = Eng(nc.vector, sem("s_dve"), "dve")
        POOL = Eng(nc.gpsimd, sem("s_pool"), "pool")
        SP = Eng(nc.sync, sem("s_sp"), "sp")
        T = Tracker()
        PS = PsumAlloc(psb)
        W = WStream(nc, POOL, arena, [sem(f"s_w{i}") for i in range(NWSEM)])

        phase = ["init"]
        labels = {"pe": [], "act": [], "dve": [], "pool": [], "sp": []}
        LAST_LABELS.clear()
        LAST_LABELS.update(labels)

        def op(E, fn, reads=(), writes=(), extra=()):
            labels[E.name].append(phase[0])
            for t in T.deps(reads, writes):
                E.wait(t)
            for t in extra:
                E.wait(t)
            tok = E.issue(fn())
            T.commit(tok, reads, writes)
            return tok

        def mm(mms, reads=(), writes=(), extra=()):
            for t in T.deps(reads, writes):
                PE.wait(t)
            for t in extra:
                PE.wait(t)
            n = len(mms)
            inst = None
            for i, (o, l, r) in enumerate(mms):
                labels["pe"].append((phase[0], r.shape[-1]))
                inst = nc.tensor.matmul(o, l, r, start=(i == 0), stop=(i == n - 1))
            tok = PE.issue(inst)
            T.commit(tok, reads, writes)
            return tok

        def dma(E, out, in_, s, reads=(), writes=()):
            for t in T.deps(reads, writes):
                E.wait(t)
            s.cnt += 16
            E.eng.dma_start(out=out, in_=in_).then_inc(s.sem, 16)
            tok = (s, s.cnt)
            T.commit(tok, reads, writes)
            return tok

        def barrier():
            toks = [(E_.src, E_.src.cnt) for E_ in (PE, ACT, DVE, POOL)]
            for E_ in (ACT, DVE, POOL):
                for t in toks:
                    if t[0] is not E_.src and t[1] > 0:
                        E_.wait(t)

        fb_i = [0]

        def nextf():
            fb_i[0] = (fb_i[0] + 1) % len(fbuf)
            return fb_i[0]

        def kx(c, b0, nb):
            return [("x", c, b) for b in range(b0, b0 + nb)]

        def kxall(b0, nb):
            return [("x", c, b) for c in range(NCH) for b in range(b0, b0 + nb)]

        def kh(c, b0, nb):
            return [("h", c, b) for b in range(b0, b0 + nb)]

        def khall(b0, nb):
            return [("h", c, b) for c in range(NCH) for b in range(b0, b0 + nb)]

        def kbm(slot, b0, nb):
            return [("bm", slot, b) for b in range(b0, b0 + nb)]

        def kbmall(slots, b0, nb):
            return [("bm", s_, b) for s_ in slots for b in range(b0, b0 + nb)]

        s_c = Src(sem("s_cst"))
        dma(SP, cst[:, :], cst_d[:, :], s_c, writes=[("cst",)])
        xtiles = split_blocks(tin_blk, 4)
        s_x = [Src(sem(f"s_x{i}")) for i in range(len(xtiles))]
        xv = xT_d.rearrange("(c p) t -> p c t", p=128)
        for i, (b0, nb) in enumerate(xtiles):
            dma(SP, xT[:, :, b0 * 128:(b0 + nb) * 128], xv[:, :, b0 * 128:(b0 + nb) * 128],
                s_x[i], writes=kxall(b0, nb))
        s_b = Src(sem("s_band"))
        s_b.cnt += 16
        nc.gpsimd.dma_start(out=bands[:, :, :], in_=band_d.rearrange("p (a b) -> p a b", b=128)).then_inc(s_b.sem, 16)
        T.commit((s_b, s_b.cnt), [], [("bands",)])
        op(DVE, lambda: nc.vector.memset(onesM[:, :], 1.0 / D), writes=[("ones",)])
        s_bi = [Src(sem(f"s_bias{i}")) for i in range(2)]
        for h in range(NH):
            fi = h % 2
            dma(SP, fbuf[fi][:, 0:384], bias_d[:, h, :], s_bi[fi], writes=[("f", fi)])
            op(ACT, lambda: nc.scalar.activation(out=EB[:, h, :], in_=fbuf[fi][:, 0:384], func=AF.Exp),
               reads=[("f", fi)], writes=[("eb", h)])
        for l in range(n_layers):
            o = l * CST_PER_L + 48
            op(ACT, lambda: nc.scalar.activation(out=esink[:, l * 8:(l + 1) * 8], in_=cst[:, o:o + 8], func=AF.Exp),
               reads=[("cst",)], writes=[("esink", l)])

        for l in range(n_layers):
            nst = len(split_blocks(regions[l], MAXST))
            for s_i in range(nst):
                off = 0

                def addp(name, n):
                    nonlocal off
                    W.add((l, s_i, name), wst_d[l, :, off:off + n], n)
                    off += n
                for c in range(8):
                    addp(f"cv{c}", 3072)
                for n_ in range(8):
                    addp(f"ao{n_}", 2048)
                addp("wu0", 4096)
                addp("wu1", 4096)
                addp("wp", 2048)
                for n_ in range(8):
                    addp(f"gp{n_}", 1024)
                addp("wk", 2048)
                addp("wv", 2048)
                for q in range(8):
                    addp(f"wq{q}", 1024)
                for n_ in range(8):
                    addp(f"at{n_}", 2048)
                for n_ in range(8):
                    addp(f"wo{n_}", 1024)
                for fg in range(NFG):
                    for f in range(FPG):
                        addp(f"gu{fg}_{f}", 2048)
                    for n_ in range(8):
                        addp(f"wd{fg}_{n_}", 1408)
                assert off == EPP

        def rmsnorm(gb0, nb, goff, dst_fn, dst_keys_fn):
            for t0 in range(gb0 * 128, (gb0 + nb) * 128, 256):
                b = t0 // 128
                op(ACT, lambda: nc.scalar.activation(out=sq[:, :, :], in_=xT[:, :, t0:t0 + 256], func=AF.Square),
                   reads=kxall(b, 2), writes=[("sq",)])
                pb = PS.alloc()
                mm([(psb[pb][:, 0:256], onesM[:, :], sq[:, c, :]) for c in range(NCH)],
                   reads=[("sq",), ("ones",)], writes=[("ps", pb)])
                f1 = nextf()
                op(ACT, lambda: nc.scalar.activation(out=fbuf[f1][:, 0:256], in_=psb[pb][:, 0:256], func=AF.Sqrt,
                                                     bias=EPS, scale=1.0),
                   reads=[("ps", pb)], writes=[("f", f1)])
                PS.free(pb)
                op(DVE, lambda: nc.vector.reciprocal(out=fbuf[f1][:, 0:256], in_=fbuf[f1][:, 0:256]),
                   reads=[("f", f1)], writes=[("f", f1)])
                for c in range(NCH):
                    op(DVE, lambda: nc.vector.scalar_tensor_tensor(
                        out=dst_fn(c, t0, 256), in0=xT[:, c, t0:t0 + 256], scalar=cst[:, goff + c:goff + c + 1],
                        in1=fbuf[f1][:, 0:256], op0=ALU.mult, op1=ALU.mult),
                        reads=kx(c, b, 2) + [("f", f1), ("cst",)], writes=[dst_keys_fn(c, b), dst_keys_fn(c, b + 1)])

        for l in range(n_layers):
            R = regions[l]
            co = l * CST_PER_L
            G_MIX, G_FFN, PSC, CVW, SNK = co, co + 8, co + 16, co + 24, co + 48
            sts = split_blocks(R, MAXST)
            for s_i, (b0, nb) in enumerate(sts):
                b1 = b0 + nb
                e0 = b0 - 1 if s_i > 0 else 0
                e1 = b1 + 1
                ne = e1 - e0
                lo = b0 - e0
                n_own = nb * 128
                own_tiles = split_blocks(nb, 4)
                ext_tiles = split_blocks(ne, 4)

                def wget(name):
                    idx, off, tok = W.get((l, s_i, name))
                    return idx, off, tok

                phase[0] = 'norm'
                barrier()
                assert nb % 2 == 0 or True
                if s_i > 0:
                    op(DVE, lambda: nc.vector.tensor_copy(out=hT[:, :, 0:128], in_=stash[:, :, :]),
                       reads=[("stash",)], writes=khall(0, 1))
                nstart = b0 if s_i > 0 else 0
                nnb = e1 - nstart
                blks = list(range(nstart, e1, 2))
                for gb in blks:
                    if gb + 2 > e1:
                        gb = e1 - 2
                    rmsnorm(gb, 2, G_MIX,
                            lambda c, t0, n: hT[:, c, t0 - e0 * 128:t0 - e0 * 128 + n],
                            lambda c, blk: ("h", c, blk - e0))
                if s_i + 1 < len(sts):
                    sl = (b1 - 1 - e0) * 128
                    op(DVE, lambda: nc.vector.tensor_copy(out=stash[:, :, :], in_=hT[:, :, sl:sl + 128]),
                       reads=khall(b1 - 1 - e0, 1), writes=[("stash",)])

                phase[0] = 'conv'
                barrier()
                base_u = lo * 128 - 1
                cs = max(base_u, 0)
                ce = lo * 128 + n_own + 1
                ctl = []
                nct = -(-(ce - cs) // 512)
                step = -(-(ce - cs) // nct)
                t_ = cs
                while t_ < ce:
                    ctl.append((t_, min(step, ce - t_)))
                    t_ += step
                for c in range(8):
                    widx, woff, wtok = wget(f"cv{c}")
                    wv = arena[:, woff:woff + 3072].rearrange("p (a k n) -> p a k n", a=3, k=8)
                    if s_i == 0:
                        op(DVE, lambda: nc.vector.memset(ubuf[:, 0:1], 0.0), writes=[("u",)])
                    for (ts, tn) in ctl:
                        hb0 = ts // 128
                        hnb = (ts + tn - 1) // 128 - hb0 + 1
                        pc = PS.alloc()
                        px = PS.alloc()
                        mm([(psb[pc][:, 0:tn], wv[:, 0, k, :], hT[:, k, ts:ts + tn]) for k in range(8)],
                           reads=khall(hb0, hnb), writes=[("ps", pc)], extra=[wtok])
                        mm([(psb[px][:, 0:tn], wv[:, 1, k, :], hT[:, k, ts:ts + tn]) for k in range(8)],
                           reads=khall(hb0, hnb), writes=[("ps", px)])
                        f1 = nextf()
                        op(ACT, lambda: nc.scalar.activation(out=fbuf[f1][:, 0:tn], in_=psb[pc][:, 0:tn], func=AF.Copy),
                           reads=[("ps", pc)], writes=[("f", f1)])
                        PS.free(pc)
                        ui = ts - base_u
                        op(DVE, lambda: nc.vector.tensor_tensor(out=ubuf[:, ui:ui + tn], in0=psb[px][:, 0:tn],
                                                                in1=fbuf[f1][:, 0:tn], op=ALU.mult),
                           reads=[("ps", px), ("f", f1)], writes=[("u",)])
                        PS.free(px)
                    for ti_, (ob, onb) in enumerate(own_tiles):
                        on = onb * 128
                        os_ = ob * 128
                        pbk = PS.alloc()
                        hs = lo * 128 + os_
                        tokb = mm([(psb[pbk][:, 0:on], wv[:, 2, k, :], hT[:, k, hs:hs + on]) for k in range(8)],
                                  reads=khall(lo + ob, onb), writes=[("ps", pbk)])
                        yi = ti_ % 2
                        yb = ybuf[yi]
                        cw = CVW + c * 3
                        op(DVE, lambda: nc.vector.tensor_scalar(out=yb[:, 0:on], in0=ubuf[:, os_ + 1:os_ + 1 + on],
                                                                scalar1=cst[:, cw + 1:cw + 2], scalar2=None, op0=ALU.mult),
                           reads=[("u",), ("cst",)], writes=[("y", yi)])
                        op(DVE, lambda: nc.vector.scalar_tensor_tensor(out=yb[:, 0:on], in0=ubuf[:, os_:os_ + on],
                                                                       scalar=cst[:, cw:cw + 1], in1=yb[:, 0:on],
                                                                       op0=ALU.mult, op1=ALU.add),
                           reads=[("u",), ("y", yi)], writes=[("y", yi)])
                        op(DVE, lambda: nc.vector.scalar_tensor_tensor(out=yb[:, 0:on], in0=ubuf[:, os_ + 2:os_ + 2 + on],
                                                                       scalar=cst[:, cw + 2:cw + 3], in1=yb[:, 0:on],
                                                                       op0=ALU.mult, op1=ALU.add),
                           reads=[("u",), ("y", yi)], writes=[("y", yi)])
                        op(DVE, lambda: nc.vector.tensor_tensor(out=BM[:, c, os_:os_ + on], in0=psb[pbk][:, 0:on],
                                                                in1=yb[:, 0:on], op=ALU.mult),
                           reads=[("ps", pbk), ("y", yi)], writes=kbm(c, ob, onb))
                        PS.free(pbk)
                    W.release(widx, tokb)

                def branch_out(prefix, first, ymm_fn, post_scale=None):
                    for n_ in range(8):
                        widx, woff, wtok = wget(f"{prefix}{n_}")
                        last = None
                        for (ob, onb) in own_tiles:
                            on = onb * 128
                            os_ = ob * 128
                            hs = lo * 128 + os_
                            py = PS.alloc()
                            pg = PS.alloc()
                            mms, rk, gw = ymm_fn(n_, woff, py, os_, on, ob, onb)
                            mm(mms, reads=rk, writes=[("ps", py)], extra=[wtok])
                            last = mm([(psb[pg][:, 0:on], gw[:, k, :], hT[:, k, hs:hs + on]) for k in range(8)],
                                      reads=khall(lo + ob, onb), writes=[("ps", pg)])
                            f1 = nextf()
                            op(ACT, lambda: nc.scalar.activation(out=fbuf[f1][:, 0:on], in_=psb[pg][:, 0:on], func=AF.Sigmoid),
                               reads=[("ps", pg)], writes=[("f", f1)])
                            PS.free(pg)
                            if first:
                                op(DVE, lambda: nc.vector.tensor_tensor(out=BM[:, 8 + n_, os_:os_ + on], in0=psb[py][:, 0:on],
                                                                        in1=fbuf[f1][:, 0:on], op=ALU.mult),
                                   reads=[("ps", py), ("f", f1)], writes=kbm(8 + n_, ob, onb))
                            else:
                                if post_scale is None:
                                    op(DVE, lambda: nc.vector.tensor_tensor(out=fbuf[f1][:, 0:on], in0=psb[py][:, 0:on],
                                                                            in1=fbuf[f1][:, 0:on], op=ALU.mult),
                                       reads=[("ps", py), ("f", f1)], writes=[("f", f1)])
                                else:
                                    sc = post_scale + n_
                                    op(DVE, lambda: nc.vector.scalar_tensor_tensor(
                                        out=fbuf[f1][:, 0:on], in0=psb[py][:, 0:on], scalar=cst[:, sc:sc + 1],
                                        in1=fbuf[f1][:, 0:on], op0=ALU.mult, op1=ALU.mult),
                                        reads=[("ps", py), ("f", f1), ("cst",)], writes=[("f", f1)])
                                op(DVE, lambda: nc.vector.tensor_tensor(out=BM[:, 8 + n_, os_:os_ + on], in0=BM[:, 8 + n_, os_:os_ + on],
                                                                        in1=fbuf[f1][:, 0:on], op=ALU.add),
                                   reads=kbm(8 + n_, ob, onb) + [("f", f1)], writes=kbm(8 + n_, ob, onb))
                            PS.free(py)
                        W.release(widx, last)

                def ymm_full(n_, woff, py, os_, on, ob, onb):
                    wv2 = arena[:, woff:woff + 2048].rearrange("p (a k n) -> p a k n", a=2, k=8)
                    mms = [(psb[py][:, 0:on], wv2[:, 0, c, :], BM[:, c, os_:os_ + on]) for c in range(8)]
                    return mms, kbmall(range(8), ob, onb), wv2[:, 1]

                phase[0] = 'ao'
                branch_out("ao", True, ymm_full)

                phase[0] = 'poolU'
                barrier()
                i0, o0, t0_ = wget("wu0")
                i1, o1, t1_ = wget("wu1")
                ip, op_, tp_ = wget("wp")
                wu = [arena[:, o0:o0 + 4096].rearrange("p (k n) -> p k n", k=8),
                      arena[:, o1:o1 + 4096].rearrange("p (k n) -> p k n", k=8)]
                wp = arena[:, op_:op_ + 2048].rearrange("p (g k n) -> p g k n", g=4, k=2)
                lastu = None
                for i in range(ne):
                    slot = i % 4
                    for hf in range(2):
                        pu = PS.alloc()
                        lastu = mm([(psb[pu][:, :], hT[:, k, i * 128:(i + 1) * 128], wu[hf][:, k, :]) for k in range(8)],
                                   reads=khall(i, 1), writes=[("ps", pu)], extra=[t0_, t1_])
                        if hf == 0:
                            op(ACT, lambda: nc.scalar.activation(out=Usb[:, slot, 0:512], in_=psb[pu][:, :], func=AF.Copy),
                               reads=[("ps", pu)], writes=[("usb", slot, 0)])
                        else:
                            op(DVE, lambda: nc.vector.tensor_copy(out=Usb[:, slot, 512:1024], in_=psb[pu][:, :]),
                               reads=[("ps", pu)], writes=[("usb", slot, 1)])
                        PS.free(pu)
                    j = i - 1
                    if j >= lo and j < lo + nb:
                        gj = e0 + j
                        srcs = [d for d in (-1, 0, 1) if 0 <= j + d < ne]
                        for half in range(2):
                            pp = PS.alloc()
                            for cc in range(4):
                                c = half * 4 + cc
                                g = c // 2
                                mms = []
                                for d in srcs:
                                    if gj == 0:
                                        bnd = bands[:, 12 + g, :] if d == 0 else bands[:, g * 3 + 2, :]
                                    else:
                                        bnd = bands[:, g * 3 + (d + 1), :]
                                    mms.append((psb[pp][:, cc * 128:(cc + 1) * 128],
                                                Usb[:, (j + d) % 4, c * 128:(c + 1) * 128], bnd))
                                mm(mms, reads=[("usb", (j + d) % 4, c // 4) for d in srcs] + [("bands",)],
                                   writes=[("ps", pp)])
                            ob_ = j - lo
                            if half == 0:
                                op(ACT, lambda: nc.scalar.activation(
                                    out=BM[:, 0:4, ob_ * 128:(ob_ + 1) * 128],
                                    in_=psb[pp][:, :].rearrange("p (a b) -> p a b", a=4), func=AF.Copy),
                                    reads=[("ps", pp)], writes=kbmall(range(0, 4), ob_, 1))
                            else:
                                op(DVE, lambda: nc.vector.tensor_copy(
                                    out=BM[:, 4:8, ob_ * 128:(ob_ + 1) * 128],
                                    in_=psb[pp][:, :].rearrange("p (a b) -> p a b", a=4)),
                                    reads=[("ps", pp)], writes=kbmall(range(4, 8), ob_, 1))
                            PS.free(pp)
                W.release(i0, lastu)
                W.release(i1, lastu)

                def ymm_pool(n_, woff, py, os_, on, ob, onb):
                    g = n_ // 2
                    gw = arena[:, woff:woff + 1024].rearrange("p (k n) -> p k n", k=8)
                    mms = [(psb[py][:, 0:on], wp[:, g, kk, (n_ % 2) * 128:(n_ % 2) * 128 + 128],
                            BM[:, 2 * g + kk, os_:os_ + on]) for kk in range(2)]
                    return mms, kbmall([2 * g, 2 * g + 1], ob, onb), gw

                phase[0] = 'poolY'
                PE.wait(tp_)
                branch_out("gp", False, ymm_pool, post_scale=PSC)
                W.release(ip, (PE.src, PE.src.cnt))

                phase[0] = 'kv'
                barrier()
                op(DVE, lambda: nc.vector.memset(Vsb[:, :, :, :], 1.0), writes=[("v", i) for i in range(MAXEXT)])
                op(DVE, lambda: nc.vector.memset(QTA[:, :], 0.0), writes=[("qta",)])
                op(DVE, lambda: nc.vector.memset(QTB[:, :], 0.0), writes=[("qtb",)])
                ik, ok_, tk_ = wget("wk")
                iv, ov_, tv_ = wget("wv")
                wk = arena[:, ok_:ok_ + 2048].rearrange("p (k n) -> p k n", k=8)
                wvv = arena[:, ov_:ov_ + 2048].rearrange("p (k n) -> p k n", k=8)
                lastk = None
                for kc in range(2):
                    for (eb, enb) in ext_tiles:
                        en = enb * 128
                        pk = PS.alloc()
                        lastk = mm([(psb[pk][:, 0:en], wk[:, k, kc * 128:(kc + 1) * 128], hT[:, k, eb * 128:eb * 128 + en])
                                    for k in range(8)], reads=khall(eb, enb), writes=[("ps", pk)], extra=[tk_])
                        op(ACT, lambda: nc.scalar.activation(out=KT[:, kc, eb * 128:eb * 128 + en], in_=psb[pk][:, 0:en], func=AF.Copy),
                           reads=[("ps", pk)], writes=[("kt", kc, b) for b in range(eb, eb + enb)])
                        PS.free(pk)
                W.release(ik, lastk)
                lastv = None
                for i in range(ne):
                    pv = PS.alloc()
                    lastv = mm([(psb[pv][:, 0:256], hT[:, k, i * 128:(i + 1) * 128], wvv[:, k, :]) for k in range(8)],
                               reads=khall(i, 1), writes=[("ps", pv)], extra=[tv_])
                    pvv = psb[pv][:, 0:256].rearrange("p (a b c) -> p a b c", a=2, b=2)
                    op(DVE, lambda: nc.vector.tensor_copy(out=Vsb[:, i, 0::2, 0:64], in_=pvv[:, :, 0, :]),
                       reads=[("ps", pv)], writes=[("v", i)])
                    op(DVE, lambda: nc.vector.tensor_copy(out=Vsb[:, i, 1::2, 64:128], in_=pvv[:, :, 1, :]),
                       reads=[("ps", pv)], writes=[("v", i)])
                    PS.free(pv)
                W.release(iv, lastv)

                phase[0] = 'attn'
                for qc in range(8):
                    iq, oq, tq = wget(f"wq{qc}")
                    wq = arena[:, oq:oq + 1024].rearrange("p (k n) -> p k n", k=8)
                    kvc = qc // 4
                    lastq = None
                    for (ob, onb) in own_tiles:
                        on = onb * 128
                        os_ = ob * 128
                        hs = lo * 128 + os_
                        pq = PS.alloc()
                        lastq = mm([(psb[pq][:, 0:on], wq[:, k, :], hT[:, k, hs:hs + on]) for k in range(8)],
                                   reads=khall(lo + ob, onb), writes=[("ps", pq)], extra=[tq])
                        op(ACT, lambda: nc.scalar.activation(out=QTA[0:64, os_:os_ + on], in_=psb[pq][0:64, 0:on],
                                                             func=AF.Copy, scale=0.125),
                           reads=[("ps", pq)], writes=[("qta",)])
                        op(ACT, lambda: nc.scalar.activation(out=QTB[64:128, os_:os_ + on], in_=psb[pq][64:128, 0:on],
                                                             func=AF.Copy, scale=0.125),
                           reads=[("ps", pq)], writes=[("qtb",)])
                        PS.free(pq)
                    W.release(iq, lastq)
                    QT = [QTA, QTB]
                    qrange = {}
                    po = [None, None]
                    for j in range(ne + LAG + 1):
                        if j < ne:
                            qlo = max(j - lo - 1, 0)
                            qhi = min(j - lo + 1, nb - 1)
                            if qlo <= qhi:
                                nq = qhi - qlo + 1
                                qrange[j] = (qlo, qhi)
                                dlo = qlo + lo - j
                                for r in range(2):
                                    pss = PS.alloc()
                                    mm([(psb[pss][:, 0:nq * 128], KT[:, kvc, j * 128:(j + 1) * 128],
                                         QT[r][:, qlo * 128:(qhi + 1) * 128])],
                                       reads=[("kt", kvc, j), ("qta",) if r == 0 else ("qtb",)], writes=[("ps", pss)])
                                    eb_ = Ebuf[r][j % 2]
                                    op(ACT, lambda: nc.scalar.activation(out=eb_[:, 0:nq * 128], in_=psb[pss][:, 0:nq * 128], func=AF.Exp),
                                       reads=[("ps", pss)], writes=[("e", r, j % 2)])
                                    PS.free(pss)
                                    hidx = 2 * qc + r
                                    op(POOL, lambda: nc.gpsimd.tensor_tensor(
                                        out=PT[r][j % NPT][:, 0:nq * 128], in0=eb_[:, 0:nq * 128],
                                        in1=EB[:, hidx, (dlo + 1) * 128:(dlo + 1 + nq) * 128], op=ALU.mult),
                                        reads=[("e", r, j % 2), ("eb", hidx)], writes=[("pt", r, j % NPT)])
                        qb = j - LAG - lo - 1
                        if 0 <= qb < nb:
                            sl = qb % 4
                            if sl == 0:
                                po = [PS.alloc(), PS.alloc()]
                            for r in range(2):
                                kv = kvc * 2 + r
                                jj_list = [jj for jj in (qb + lo - 1, qb + lo, qb + lo + 1) if jj in qrange]
                                mms = []
                                for jj in jj_list:
                                    co_ = (qb - qrange[jj][0]) * 128
                                    mms.append((psb[po[r]][:, sl * 128:(sl + 1) * 128], Vsb[:, jj, kv, :],
                                                PT[r][jj % NPT][:, co_:co_ + 128]))
                                mm(mms, reads=[("v", jj) for jj in jj_list] + [("pt", r, jj % NPT) for jj in jj_list],
                                   writes=[("ps", po[r])])
                            if sl == 3 or qb == nb - 1:
                                nn = (sl + 1) * 128
                                q0 = (qb - sl) * 128
                                es_ = l * 8 + qc
                                op(DVE, lambda: nc.vector.tensor_copy(out=rden[0:64, 0:nn], in_=psb[po[0]][64:128, 0:nn]),
                                   reads=[("ps", po[0])], writes=[("rden", 0)])
                                op(DVE, lambda: nc.vector.tensor_copy(out=rden[64:128, 0:nn], in_=psb[po[1]][0:64, 0:nn]),
                                   reads=[("ps", po[1])], writes=[("rden", 1)])
                                op(DVE, lambda: nc.vector.tensor_scalar(out=rden[:, 0:nn], in0=rden[:, 0:nn],
                                                                        scalar1=esink[:, es_:es_ + 1], scalar2=None, op0=ALU.add),
                                   reads=[("rden", 0), ("rden", 1), ("esink", l)], writes=[("rden", 0), ("rden", 1)])
                                op(DVE, lambda: nc.vector.reciprocal(out=rden[:, 0:nn], in_=rden[:, 0:nn]),
                                   reads=[("rden", 0), ("rden", 1)], writes=[("rden", 0), ("rden", 1)])
                                op(DVE, lambda: nc.vector.tensor_tensor(out=BM[0:64, qc, q0:q0 + nn], in0=psb[po[0]][0:64, 0:nn],
                                                                        in1=rden[0:64, 0:nn], op=ALU.mult),
                                   reads=[("ps", po[0]), ("rden", 0)], writes=kbm(qc, qb - sl, sl + 1))
                                op(DVE, lambda: nc.vector.tensor_tensor(out=BM[64:128, qc, q0:q0 + nn], in0=psb[po[1]][64:128, 0:nn],
                                                                        in1=rden[64:128, 0:nn], op=ALU.mult),
                                   reads=[("ps", po[1]), ("rden", 1)], writes=kbm(qc, qb - sl, sl + 1))
                                PS.free(po[0])
                                PS.free(po[1])

                def ymm_attn(n_, woff, py, os_, on, ob, onb):
                    wv2 = arena[:, woff:woff + 2048].rearrange("p (a k n) -> p a k n", a=2, k=8)
                    mms = [(psb[py][:, 0:on], wv2[:, 0, c, :], BM[:, c, os_:os_ + on]) for c in range(8)]
                    return mms, kbmall(range(8), ob, onb), wv2[:, 1]

                phase[0] = 'at'
                branch_out("at", False, ymm_attn)

                phase[0] = 'wo'
                for n_ in range(8):
                    widx, woff, wtok = wget(f"wo{n_}")
                    wo = arena[:, woff:woff + 1024].rearrange("p (k n) -> p k n", k=8)
                    last = None
                    for (ob, onb) in own_tiles:
                        on = onb * 128
                        os_ = ob * 128
                        gs = (b0 + ob) * 128
                        px = PS.alloc()
                        last = mm([(psb[px][:, 0:on], wo[:, c, :], BM[:, 8 + c, os_:os_ + on]) for c in range(8)],
                                  reads=kbmall(range(8, 16), ob, onb), writes=[("ps", px)], extra=[wtok])
                        op(DVE, lambda: nc.vector.tensor_tensor(out=xT[:, n_, gs:gs + on], in0=xT[:, n_, gs:gs + on],
                                                                in1=psb[px][:, 0:on], op=ALU.add),
                           reads=[("ps", px)] + kx(n_, b0 + ob, onb), writes=kx(n_, b0 + ob, onb))
                        PS.free(px)
                    W.release(widx, last)

                phase[0] = 'ffn_norm'
                barrier()
                blks = list(range(b0, b1, 2))
                for gb in blks:
                    if gb + 2 > b1:
                        gb = b1 - 2
                    rmsnorm(gb, 2, G_FFN,
                            lambda c, t0, n: hT[:, c, t0 - e0 * 128:t0 - e0 * 128 + n],
                            lambda c, blk: ("h", c, blk - e0))
                for fg in range(NFG):
                    phase[0] = 'ffn_gu'
                    for f in range(FPG):
                        widx, woff, wtok = wget(f"gu{fg}_{f}")
                        wg = arena[:, woff:woff + 2048].rearrange("p (a k n) -> p a k n", a=2, k=8)
                        last = None
                        for (ob, onb) in own_tiles:
                            on = onb * 128
                            os_ = ob * 128
                            hs = lo * 128 + os_
                            pg = PS.alloc()
                            pu = PS.alloc()
                            mm([(psb[pg][:, 0:on], wg[:, 0, k, :], hT[:, k, hs:hs + on]) for k in range(8)],
                               reads=khall(lo + ob, onb), writes=[("ps", pg)], extra=[wtok])
                            last = mm([(psb[pu][:, 0:on], wg[:, 1, k, :], hT[:, k, hs:hs + on]) for k in range(8)],
                                      reads=khall(lo + ob, onb), writes=[("ps", pu)])
                            f1 = nextf()
                            op(ACT, lambda: nc.scalar.activation(out=fbuf[f1][:, 0:on], in_=psb[pg][:, 0:on], func=AF.Silu),
                               reads=[("ps", pg)], writes=[("f", f1)])
                            PS.free(pg)
                            op(DVE, lambda: nc.vector.tensor_tensor(out=BM[:, f, os_:os_ + on], in0=psb[pu][:, 0:on],
                                                                    in1=fbuf[f1][:, 0:on], op=ALU.mult),
                               reads=[("ps", pu), ("f", f1)], writes=kbm(f, ob, onb))
                            PS.free(pu)
                        W.release(widx, last)
                    phase[0] = 'ffn_d'
                    for n_ in range(8):
                        widx, woff, wtok = wget(f"wd{fg}_{n_}")
                        wd = arena[:, woff:woff + 1408].rearrange("p (f n) -> p f n", f=FPG)
                        last = None
                        for (ob, onb) in own_tiles:
                            on = onb * 128
                            os_ = ob * 128
                            gs = (b0 + ob) * 128
                            pd = PS.alloc()
                            last = mm([(psb[pd][:, 0:on], wd[:, f, :], BM[:, f, os_:os_ + on]) for f in range(FPG)],
                                      reads=kbmall(range(FPG), ob, onb), writes=[("ps", pd)], extra=[wtok])
                            op(DVE, lambda: nc.vector.tensor_tensor(out=xT[:, n_, gs:gs + on], in0=xT[:, n_, gs:gs + on],
                                                                    in1=psb[pd][:, 0:on], op=ALU.add),
                               reads=[("ps", pd)] + kx(n_, b0 + ob, onb), writes=kx(n_, b0 + ob, onb))
                            PS.free(pd)
                        W.release(widx, last)

        phase[0] = 'final'
        barrier()
        s_o = Src(sem("s_out"))
        ov = out_d.rearrange("(c p) t -> p c t", p=128)
        if do_final:
            goff = n_layers * CST_PER_L
            for gb in range(0, OUTB, 2):
                rmsnorm(gb, 2, goff,
                        lambda c, t0, n: xT[:, c, t0:t0 + n],
                        lambda c, blk: ("x", c, blk))
        for (ob, onb) in split_blocks(OUTB, 4):
            dma(SP, ov[:, :, ob * 128:(ob + onb) * 128], xT[:, :, ob * 128:(ob + onb) * 128], s_o,
                reads=kxall(ob, onb))
        nc.sync.wait_ge(s_o.sem, s_o.cnt)
        for E in (PE, ACT, DVE):
            nc.sync.wait_ge(E.src.sem, E.src.cnt)
    return nc


def _kp(mat):
    n = mat.shape[1]
    return mat.reshape(8, 128, n).transpose(1, 0, 2)


def pack_layer(w_in, w_a_out, w_pool, w_attn_out, w_o, w_gu, w_down):
    Bc, Cc, Xc, Uc, Qc, Kc, Vc, Gac, Gpc, Gtc = 0, 1024, 2048, 3072, 4096, 5120, 5376, 5632, 6656, 7680
    pcs = []

    def add(a):
        pcs.append(np.ascontiguousarray(a, dtype=np.float32).reshape(128, -1))
    for c in range(8):
        add(np.stack([_kp(w_in[:, Cc + c * 128:Cc + (c + 1) * 128]),
                      _kp(w_in[:, Xc + c * 128:Xc + (c + 1) * 128]),
                      _kp(w_in[:, Bc + c * 128:Bc + (c + 1) * 128])], axis=1))
    for n in range(8):
        add(np.stack([_kp(w_a_out[:, n * 128:(n + 1) * 128]),
                      _kp(w_in[:, Gac + n * 128:Gac + (n + 1) * 128])], axis=1))
    add(_kp(w_in[:, Uc:Uc + 512]))
    add(_kp(w_in[:, Uc + 512:Uc + 1024]))
    add(w_pool.reshape(4, 2, 128, 256).transpose(2, 0, 1, 3))
    for n in range(8):
        add(_kp(w_in[:, Gpc + n * 128:Gpc + (n + 1) * 128]))
    add(_kp(w_in[:, Kc:Kc + 256]))
    add(_kp(w_in[:, Vc:Vc + 256]))
    qcols = np.concatenate([np.arange(Qc + h * 64, Qc + (h + 1) * 64) for h in HEAD_ORDER])
    wq = w_in[:, qcols]
    for q in range(8):
        add(_kp(wq[:, q * 128:(q + 1) * 128]))
    orow = np.concatenate([np.arange(h * 64, (h + 1) * 64) for h in HEAD_ORDER])
    wao = w_attn_out[orow, :]
    for n in range(8):
        add(np.stack([_kp(wao[:, n * 128:(n + 1) * 128]),
                      _kp(w_in[:, Gtc + n * 128:Gtc + (n + 1) * 128])], axis=1))
    for n in range(8):
        add(_kp(w_o[:, n * 128:(n + 1) * 128]))
    for fg in range(NFG):
        for f in range(FPG):
            fi = fg * FPG + f
            add(np.stack([_kp(w_gu[:, fi * 128:(fi + 1) * 128]),
                          _kp(w_gu[:, DFF + fi * 128:DFF + (fi + 1) * 128])], axis=1))
        wdg = w_down[fg * FPG * 128:(fg + 1) * FPG * 128, :].reshape(FPG, 128, D).transpose(1, 0, 2)
        for n in range(8):
            add(wdg[:, :, n * 128:(n + 1) * 128])
    out = np.concatenate(pcs, axis=1)
    assert out.shape == (128, EPP), out.shape
    return out


def t5_bucket_np(rel):
    import math
    import jax
    import jax.numpy as jnp
    try:
        dev = jax.devices("cpu")[0]
    except Exception:
        dev = None
    ctx = jax.default_device(dev) if dev is not None else contextlib.nullcontext()
    with ctx:
        rel = jnp.asarray(rel, dtype=jnp.int32)
        n_buckets, max_distance = 32, 128
        half = n_buckets // 2
        max_exact = half // 2
        ret = jnp.where(rel > 0, half, 0)
        n = jnp.abs(rel)
        nf = jnp.maximum(n, 1).astype(jnp.float32)
        large = max_exact + (jnp.log(nf / max_exact) / math.log(max_distance / max_exact)
                             * (half - max_exact)).astype(jnp.int32)
        large = jnp.minimum(large, half - 1)
        return np.asarray(ret + jnp.where(n < max_exact, n, large))


def make_bias_tiles(rel_bias, sgn):
    k = np.arange(128)[:, None]
    qq = np.arange(384)[None, :]
    d = qq // 128 - 1
    q = qq % 128
    rl = k - q - 128 * d
    valid = np.abs(rl) <= 128
    bk = t5_bucket_np(sgn * rl)
    out = np.empty((128, NH, 384), np.float32)
    for hi, h in enumerate(HEAD_ORDER):
        out[:, hi, :] = np.where(valid, rel_bias[bk, h], np.float32(NEGB))
    return out


def make_bands(sgn):
    out = np.zeros((128, 16, 128), np.float32)
    tp = np.arange(128)[:, None]
    t = np.arange(128)[None, :]
    for g, w in enumerate(POOL_W):
        if sgn > 0:
            lo_o, hi_o = -(w // 2), (w - 1 - w // 2)
        else:
            lo_o, hi_o = -(w - 1 - w // 2), (w // 2)
        for d in (-1, 0, 1):
            off = 128 * d + tp - t
            m = ((off >= lo_o) & (off <= hi_o)).astype(np.float32) / np.float32(w)
            if d == 0:
                m = m - (tp == t).astype(np.float32)
            out[:, g * 3 + d + 1, :] = m
        off = tp - t
        inwin = (off >= lo_o) & (off <= hi_o)
        cnt = ((t + hi_o) - np.maximum(t + lo_o, 0) + 1).astype(np.float32)
        m = inwin.astype(np.float32) / cnt - (tp == t).astype(np.float32)
        out[:, 12 + g, :] = m
    return out.reshape(128, 16 * 128)


def make_cst(layers, conv_w, pool_scale, attn_sink, g_mix, g_ffn, g_final, sgn):
    nl = len(layers)
    out = np.zeros((128, nl * CST_PER_L + 8), np.float32)

    def pc(v):
        return v.reshape(8, 128).T
    for i, l in enumerate(layers):
        o = i * CST_PER_L
        out[:, o:o + 8] = pc(g_mix[l])
        out[:, o + 8:o + 16] = pc(g_ffn[l])
        out[:, o + 16:o + 24] = pc(pool_scale[l])
        cw = conv_w[l, :, 0, :]
        if sgn < 0:
            cw = cw[::-1]
        out[:, o + 24:o + 48] = np.stack([pc(cw[k]) for k in range(3)], axis=2).reshape(128, 24)
        for qc in range(8):
            out[0:64, o + 48 + qc] = attn_sink[l, HEAD_ORDER[2 * qc]]
            out[64:128, o + 48 + qc] = attn_sink[l, HEAD_ORDER[2 * qc + 1]]
    out[:, nl * CST_PER_L:] = pc(g_final)
    return out


FUSED = True
LAST_LABELS = {}
DBG_LAYERS = None


def kernel(x, w_in, conv_w, w_a_out, w_pool, pool_scale, w_attn_out, attn_sink, w_o,
           g_mix, g_ffn, w_gu, w_down, rel_bias, g_final):
    f = lambda a: np.asarray(a, dtype=np.float32)
    x, w_in, conv_w, w_a_out, w_pool, pool_scale = map(f, (x, w_in, conv_w, w_a_out, w_pool, pool_scale))
    w_attn_out, attn_sink, w_o, g_mix, g_ffn, w_gu, w_down, rel_bias, g_final = map(
        f, (w_attn_out, attn_sink, w_o, g_mix, g_ffn, w_gu, w_down, rel_bias, g_final))
    wst = [pack_layer(w_in[l], w_a_out[l], w_pool[l], w_attn_out[l], w_o[l], w_gu[l], w_down[l])
           for l in range(DEPTH)]
    bias_t = {s: make_bias_tiles(rel_bias, s) for s in (1, -1)}
    bands = {s: make_bands(s) for s in (1, -1)}
    half = SEQ // 2

    def run(layers, regions, tin_blk, do_final, xin):
        nc = build_program(len(layers), regions, tin_blk, do_final)
        wst_l = np.ascontiguousarray(np.stack([wst[l] for l in layers], axis=0))
        in_maps = []
        for c in range(8):
            sgn = 1 if c % 2 == 0 else -1
            in_maps.append({
                "xT": np.ascontiguousarray(xin[c].T),
                "wst": wst_l,
                "cst": make_cst(layers, conv_w, pool_scale, attn_sink, g_mix, g_ffn, g_final, sgn),
                "biasT": bias_t[sgn],
                "bands": bands[sgn],
            })
        res = run_bass_kernel_spmd(nc, in_maps, core_ids=list(range(8)))
        return [np.asarray(r["outT"]).T for r in res.results]

    def local_slices(xfull, ntok):
        outs = []
        for c in range(8):
            b, hf = c // 2, c % 2
            if hf == 0:
                outs.append(xfull[b, 0:ntok, :])
            else:
                outs.append(xfull[b, ::-1, :][0:ntok, :])
        return outs

    def assemble(parts):
        out = np.empty((BATCH, SEQ, D), np.float32)
        for c in range(8):
            b, hf = c // 2, c % 2
            if hf == 0:
                out[b, 0:half, :] = parts[c]
            else:
                out[b, half:, :] = parts[c][::-1, :]
        return out

    nrun = DEPTH if DBG_LAYERS is None else DBG_LAYERS
    fin = DBG_LAYERS is None
    if FUSED:
        regions = [OWN_BLK + (nrun - 1 - l) for l in range(nrun)]
        tin = regions[0] + 1
        parts = run(list(range(nrun)), regions, tin, fin, local_slices(x, tin * 128))
        return assemble(parts)
    else:
        cur = x
        for l in range(nrun):
            parts = run([l], [OWN_BLK], OWN_BLK + 1, fin and l == nrun - 1, local_slices(cur, (OWN_BLK + 1) * 128))
            cur = assemble(parts)
        return cur
```

```python
import numpy as np
import contextlib
import concourse.bass as bass
import concourse.mybir as mybir
from concourse.bass_utils import run_bass_kernel_spmd

F32 = mybir.dt.float32
BF16 = mybir.dt.bfloat16
AF = mybir.ActivationFunctionType
ALU = mybir.AluOpType

D = 1024
NCH = 8
SEQ = 4096
BATCH = 4
DEPTH = 4
NH = 16
HD = 64
DFF = 2816
NFC = 22
NFG = 2
FPG = 11
EPS = 1e-6
OWN_BLK = 16
MAXST = 6
MAXEXT = 8
LAG = 2
NPT = 6
TT = MAXST * 128
HEAD_ORDER = [0, 4, 1, 5, 2, 6, 3, 7, 8, 12, 9, 13, 10, 14, 11, 15]
POOL_W = (2, 4, 8, 16)
EPP = 163840
ARENA = 12288
NWSEM = 12
NEGB = -30000.0

CST_PER_L = 56


class Src:
    def __init__(self, sem):
        self.sem = sem
        self.cnt = 0


class Eng:
    def __init__(self, eng, sem, name):
        self.eng = eng
        self.src = Src(sem)
        self.name = name
        self.seen = {}

    def wait(self, tok):
        if tok is None:
            return
        src, c = tok
        if self.seen.get(id(src), 0) >= c:
            return
        self.eng.wait_ge(src.sem, c)
        self.seen[id(src)] = c

    def issue(self, inst):
        self.src.cnt += 1
        inst.then_inc(self.src.sem, 1)
        return (self.src, self.src.cnt)


class Tracker:
    def __init__(self):
        self.w = {}
        self.r = {}

    def deps(self, reads, writes):
        toks = []
        for k in reads:
            t = self.w.get(k)
            if t is not None:
                toks.append(t)
        for k in writes:
            t = self.w.get(k)
            if t is not None:
                toks.append(t)
            rr = self.r.get(k)
            if rr:
                toks.extend(rr.values())
        return toks

    def commit(self, tok, reads, writes):
        for k in reads:
            rr = self.r.setdefault(k, {})
            rr[id(tok[0])] = tok
        for k in writes:
            self.w[k] = tok
            self.r[k] = {}


class PsumAlloc:
    def __init__(self, banks):
        self.free_list = list(range(len(banks)))
        self.banks = banks

    def alloc(self):
        assert self.free_list, "out of PSUM banks"
        return self.free_list.pop(0)

    def free(self, b):
        self.free_list.append(b)


class WStream:
    def __init__(self, nc, pool, arena, sems):
        self.nc = nc
        self.pool = pool
        self.arena = arena
        self.sems = [Src(s) for s in sems]
        self.plan = []
        self.ni = 0
        self.ng = 0
        self.head = 0
        self.regions = []
        self.rel = {}
        self.info = {}

    def add(self, name, dram_ap, n):
        self.plan.append((name, dram_ap, n))

    def pump(self):
        while self.ni < len(self.plan):
            name, dap, n = self.plan[self.ni]
            idx = self.ni
            off = self.head
            if off + n > ARENA:
                off = 0
            conflicts = [rg for rg in self.regions if rg[0] < off + n and off < rg[0] + rg[1]]
            if any(rg[2] not in self.rel for rg in conflicts):
                return
            old = idx - NWSEM
            if old >= 0 and old not in self.rel:
                return
            for rg in conflicts:
                self.pool.wait(self.rel[rg[2]])
                self.regions.remove(rg)
            if old >= 0:
                self.pool.wait(self.rel[old])
            s = self.sems[idx % NWSEM]
            s.cnt += 16
            self.nc.gpsimd.dma_start(out=self.arena[:, off:off + n], in_=dap).then_inc(s.sem, 16)
            self.info[idx] = (off, n, (s, s.cnt))
            self.regions.append((off, n, idx))
            self.head = off + n
            self.ni += 1

    def get(self, name):
        self.pump()
        pname, _, n = self.plan[self.ng]
        assert pname == name, (pname, name)
        assert self.ng in self.info, f"weight arena deadlock at {name}"
        off, n, tok = self.info[self.ng]
        idx = self.ng
        self.ng += 1
        return idx, off, tok

    def release(self, idx, tok):
        self.rel[idx] = tok
        self.pump()


def split_blocks(n, maxpart):
    parts = -(-n // maxpart)
    base = n // parts
    rem = n % parts
    out = []
    s = 0
    for i in range(parts):
        m = base + (1 if i < rem else 0)
        out.append((s, m))
        s += m
    return out


def build_program(n_layers, regions, tin_blk, do_final):
    nc = bass.Bass("TRN2", target_bir_lowering=False)
    TIN = tin_blk * 128
    OUTB = regions[-1]
    ncst = n_layers * CST_PER_L + 8
    xT_d = nc.dram_tensor("xT", [D, TIN], F32, kind="ExternalInput").ap()
    wst_d = nc.dram_tensor("wst", [n_layers, 128, EPP], F32, kind="ExternalInput").ap()
    cst_d = nc.dram_tensor("cst", [128, ncst], F32, kind="ExternalInput").ap()
    bias_d = nc.dram_tensor("biasT", [128, NH, 384], F32, kind="ExternalInput").ap()
    band_d = nc.dram_tensor("bands", [128, 16 * 128], F32, kind="ExternalInput").ap()
    out_d = nc.dram_tensor("outT", [D, OUTB * 128], F32, kind="ExternalOutput").ap()

    es = contextlib.ExitStack()
    with es:
        def sb(name, shape, dt):
            return es.enter_context(nc.sbuf_tensor(name, shape, dt))

        def sem(name):
            return es.enter_context(nc.semaphore(name))

        xT = sb("xT_sb", [128, NCH, TIN], F32)
        hT = sb("hT", [128, NCH, MAXEXT * 128], BF16)
        stash = sb("stash", [128, NCH, 128], BF16)
        BM = sb("BM", [128, 16, TT], BF16)
        arena = sb("arena", [128, ARENA], BF16)
        cst = sb("cst_sb", [128, ncst], F32)
        esink = sb("esink", [128, n_layers * 8], F32)
        EB = sb("EB", [128, NH, 384], BF16)
        bands = sb("bands_sb", [128, 16, 128], BF16)
        onesM = sb("onesM", [128, 128], BF16)
        fbuf = [sb(f"fbuf{i}", [128, 512], F32) for i in range(4)]
        gtmp = [sb(f"gtmp{i}", [128, 256], F32) for i in range(2)]
        SCRN = 14848
        scr = sb("scr", [128, SCRN], BF16)
        sq = scr[:, 0:NCH * 256].rearrange("p (a b) -> p a b", a=NCH)
        ubuf = scr[:, 0:2 * (TT + 2)].bitcast(F32)
        ybuf = [scr[:, 2048 + i * 1024:2048 + (i + 1) * 1024].bitcast(F32) for i in range(2)]
        Usb = scr[:, 0:4096].rearrange("p (a b) -> p a b", a=4)
        _o = 0
        KT = scr[:, _o:_o + 2 * MAXEXT * 128].rearrange("p (a b) -> p a b", a=2); _o += 2 * MAXEXT * 128
        Vsb = scr[:, _o:_o + MAXEXT * 512].rearrange("p (a b c) -> p a b c", a=MAXEXT, b=4); _o += MAXEXT * 512
        QTA = scr[:, _o:_o + TT]; _o += TT
        QTB = scr[:, _o:_o + TT]; _o += TT
        Ebuf = [[None, None], [None, None]]
        for r in range(2):
            for i in range(2):
                Ebuf[r][i] = scr[:, _o:_o + 384]; _o += 384
        PT = [[None] * NPT, [None] * NPT]
        for r in range(2):
            for i in range(NPT):
                PT[r][i] = scr[:, _o:_o + 384]; _o += 384
        rden = scr[:, _o:_o + 1024].bitcast(F32); _o += 1024
        assert _o <= SCRN, _o
        psb = [es.enter_context(nc.psum_tensor(f"ps{i}", [128, 512], F32)) for i in range(8)]

        PE = Eng(nc.tensor, sem("s_pe"), "pe")
        ACT = Eng(nc.scalar, sem("s_act"), "act")
        DVE = Eng(nc.vector, sem("s_dve"), "dve")
        POOL = Eng(nc.gpsimd, sem("s_pool"), "pool")
        SP = Eng(nc.sync, sem("s_sp"), "sp")
        T = Tracker()
        PS = PsumAlloc(psb)
        W = WStream(nc, POOL, arena, [sem(f"s_w{i}") for i in range(NWSEM)])

        phase = ["init"]
        labels = {"pe": [], "act": [], "dve": [], "pool": [], "sp": []}
        LAST_LABELS.clear()
        LAST_LABELS.update(labels)

        def op(E, fn, reads=(), writes=(), extra=()):
            labels[E.name].append(phase[0])
            for t in T.deps(reads, writes):
                E.wait(t)
            for t in extra:
                E.wait(t)
            tok = E.issue(fn())
            T.commit(tok, reads, writes)
            return tok

        def mm(mms, reads=(), writes=(), extra=()):
            for t in T.deps(reads, writes):
                PE.wait(t)
            for t in extra:
                PE.wait(t)
            n = len(mms)
            inst = None
            for i, (o, l, r) in enumerate(mms):
                labels["pe"].append((phase[0], r.shape[-1]))
                inst = nc.tensor.matmul(o, l, r, start=(i == 0), stop=(i == n - 1))
            tok = PE.issue(inst)
            T.commit(tok, reads, writes)
            return tok

        def dma(E, out, in_, s, reads=(), writes=()):
            for t in T.deps(reads, writes):
                E.wait(t)
            s.cnt += 16
            E.eng.dma_start(out=out, in_=in_).then_inc(s.sem, 16)
            tok = (s, s.cnt)
            T.commit(tok, reads, writes)
            return tok

        def barrier():
            toks = [(E_.src, E_.src.cnt) for E_ in (PE, ACT, DVE, POOL)]
            for E_ in (ACT, DVE, POOL):
                for t in toks:
                    if t[0] is not E_.src and t[1] > 0:
                        E_.wait(t)

        fb_i = [0]

        def nextf():
            fb_i[0] = (fb_i[0] + 1) % len(fbuf)
            return fb_i[0]

        def kx(c, b0, nb):
            return [("x", c, b) for b in range(b0, b0 + nb)]

        def kxall(b0, nb):
            return [("x", c, b) for c in range(NCH) for b in range(b0, b0 + nb)]

        def kh(c, b0, nb):
            return [("h", c, b) for b in range(b0, b0 + nb)]

        def khall(b0, nb):
            return [("h", c, b) for c in range(NCH) for b in range(b0, b0 + nb)]

        def kbm(slot, b0, nb):
            return [("bm", slot, b) for b in range(b0, b0 + nb)]

        def kbmall(slots, b0, nb):
            return [("bm", s_, b) for s_ in slots for b in range(b0, b0 + nb)]

        s_c = Src(sem("s_cst"))
        dma(SP, cst[:, :], cst_d[:, :], s_c, writes=[("cst",)])
        xtiles = split_blocks(tin_blk, 4)
        s_x = [Src(sem(f"s_x{i}")) for i in range(len(xtiles))]
        xv = xT_d.rearrange("(c p) t -> p c t", p=128)
        for i, (b0, nb) in enumerate(xtiles):
            dma(SP, xT[:, :, b0 * 128:(b0 + nb) * 128], xv[:, :, b0 * 128:(b0 + nb) * 128],
                s_x[i], writes=kxall(b0, nb))
        s_b = Src(sem("s_band"))
        s_b.cnt += 16
        nc.gpsimd.dma_start(out=bands[:, :, :], in_=band_d.rearrange("p (a b) -> p a b", b=128)).then_inc(s_b.sem, 16)
        T.commit((s_b, s_b.cnt), [], [("bands",)])
        op(DVE, lambda: nc.vector.memset(onesM[:, :], 1.0 / D), writes=[("ones",)])
        s_bi = [Src(sem(f"s_bias{i}")) for i in range(2)]
        for h in range(NH):
            fi = h % 2
            dma(SP, fbuf[fi][:, 0:384], bias_d[:, h, :], s_bi[fi], writes=[("f", fi)])
            op(ACT, lambda: nc.scalar.activation(out=EB[:, h, :], in_=fbuf[fi][:, 0:384], func=AF.Exp),
               reads=[("f", fi)], writes=[("eb", h)])
        for l in range(n_layers):
            o = l * CST_PER_L + 48
            op(ACT, lambda: nc.scalar.activation(out=esink[:, l * 8:(l + 1) * 8], in_=cst[:, o:o + 8], func=AF.Exp),
               reads=[("cst",)], writes=[("esink", l)])

        for l in range(n_layers):
            nst = len(split_blocks(regions[l], MAXST))
            for s_i in range(nst):
                off = 0

                def addp(name, n):
                    nonlocal off
                    W.add((l, s_i, name), wst_d[l, :, off:off + n], n)
                    off += n
                for c in range(8):
                    addp(f"cv{c}", 3072)
                for n_ in range(8):
                    addp(f"ao{n_}", 2048)
                addp("wu0", 4096)
                addp("wu1", 4096)
                addp("wp", 2048)
                for n_ in range(8):
                    addp(f"gp{n_}", 1024)
                addp("wk", 2048)
                addp("wv", 2048)
                for q in range(8):
                    addp(f"wq{q}", 1024)
                for n_ in range(8):
                    addp(f"at{n_}", 2048)
                for n_ in range(8):
                    addp(f"wo{n_}", 1024)
                for fg in range(NFG):
                    for f in range(FPG):
                        addp(f"gu{fg}_{f}", 2048)
                    for n_ in range(8):
                        addp(f"wd{fg}_{n_}", 1408)
                assert off == EPP

        def rmsnorm(gb0, nb, goff, dst_fn, dst_keys_fn):
            for t0 in range(gb0 * 128, (gb0 + nb) * 128, 256):
                b = t0 // 128
                op(ACT, lambda: nc.scalar.activation(out=sq[:, :, :], in_=xT[:, :, t0:t0 + 256], func=AF.Square),
                   reads=kxall(b, 2), writes=[("sq",)])
                pb = PS.alloc()
                mm([(psb[pb][:, 0:256], onesM[:, :], sq[:, c, :]) for c in range(NCH)],
                   reads=[("sq",), ("ones",)], writes=[("ps", pb)])
                f1 = nextf()
                op(ACT, lambda: nc.scalar.activation(out=fbuf[f1][:, 0:256], in_=psb[pb][:, 0:256], func=AF.Ln,
                                                     bias=EPS, scale=1.0),
                   reads=[("ps", pb)], writes=[("f", f1)])
                PS.free(pb)
                op(ACT, lambda: nc.scalar.activation(out=fbuf[f1][:, 0:256], in_=fbuf[f1][:, 0:256], func=AF.Exp,
                                                     scale=-0.5),
                   reads=[("f", f1)], writes=[("f", f1)])
                for c in range(NCH):
                    if c < 6:
                        op(DVE, lambda: nc.vector.scalar_tensor_tensor(
                            out=dst_fn(c, t0, 256), in0=xT[:, c, t0:t0 + 256], scalar=cst[:, goff + c:goff + c + 1],
                            in1=fbuf[f1][:, 0:256], op0=ALU.mult, op1=ALU.mult),
                            reads=kx(c, b, 2) + [("f", f1), ("cst",)], writes=[dst_keys_fn(c, b), dst_keys_fn(c, b + 1)])
                    else:
                        hb = 256 * (c - 6)
                        op(ACT, lambda: nc.scalar.activation(out=fbuf[f1][:, 256 + 0:512], in_=xT[:, c, t0:t0 + 256], func=AF.Copy,
                                                             scale=cst[:, goff + c:goff + c + 1]) if False else
                           nc.scalar.activation(out=gtmp[c - 6][:, 0:256], in_=xT[:, c, t0:t0 + 256], func=AF.Copy,
                                                scale=cst[:, goff + c:goff + c + 1]),
                           reads=kx(c, b, 2) + [("cst",)], writes=[("gtmp", c - 6)])
                        op(POOL, lambda: nc.gpsimd.tensor_tensor(out=dst_fn(c, t0, 256), in0=gtmp[c - 6][:, 0:256],
                                                                 in1=fbuf[f1][:, 0:256], op=ALU.mult),
                           reads=[("gtmp", c - 6), ("f", f1)], writes=[dst_keys_fn(c, b), dst_keys_fn(c, b + 1)])

        for l in range(n_layers):
            R = regions[l]
            co = l * CST_PER_L
            G_MIX, G_FFN, PSC, CVW, SNK = co, co + 8, co + 16, co + 24, co + 48
            sts = split_blocks(R, MAXST)
            for s_i, (b0, nb) in enumerate(sts):
                b1 = b0 + nb
                e0 = b0 - 1 if s_i > 0 else 0
                e1 = b1 + 1
                ne = e1 - e0
                lo = b0 - e0
                n_own = nb * 128
                own_tiles = split_blocks(nb, 4)
                ext_tiles = split_blocks(ne, 4)

                def wget(name):
                    idx, off, tok = W.get((l, s_i, name))
                    return idx, off, tok

                phase[0] = 'norm'
                barrier()
                assert nb % 2 == 0 or True
                if s_i > 0:
                    op(DVE, lambda: nc.vector.tensor_copy(out=hT[:, :, 0:128], in_=stash[:, :, :]),
                       reads=[("stash",)], writes=khall(0, 1))
                nstart = b0 if s_i > 0 else 0
                nnb = e1 - nstart
                blks = list(range(nstart, e1, 2))
                for gb in blks:
                    if gb + 2 > e1:
                        gb = e1 - 2
                    rmsnorm(gb, 2, G_MIX,
                            lambda c, t0, n: hT[:, c, t0 - e0 * 128:t0 - e0 * 128 + n],
                            lambda c, blk: ("h", c, blk - e0))
                if s_i + 1 < len(sts):
                    sl = (b1 - 1 - e0) * 128
                    op(DVE, lambda: nc.vector.tensor_copy(out=stash[:, :, :], in_=hT[:, :, sl:sl + 128]),
                       reads=khall(b1 - 1 - e0, 1), writes=[("stash",)])

                phase[0] = 'conv'
                barrier()
                base_u = lo * 128 - 1
                cs = max(base_u, 0)
                ce = lo * 128 + n_own + 1
                ctl = []
                nct = -(-(ce - cs) // 512)
                step = -(-(ce - cs) // nct)
                t_ = cs
                while t_ < ce:
                    ctl.append((t_, min(step, ce - t_)))
                    t_ += step
                for c in range(8):
                    widx, woff, wtok = wget(f"cv{c}")
                    wv = arena[:, woff:woff + 3072].rearrange("p (a k n) -> p a k n", a=3, k=8)
                    if s_i == 0:
                        op(DVE, lambda: nc.vector.memset(ubuf[:, 0:1], 0.0), writes=[("u",)])
                    for (ts, tn) in ctl:
                        hb0 = ts // 128
                        hnb = (ts + tn - 1) // 128 - hb0 + 1
                        pc = PS.alloc()
                        px = PS.alloc()
                        mm([(psb[pc][:, 0:tn], wv[:, 0, k, :], hT[:, k, ts:ts + tn]) for k in range(8)],
                           reads=khall(hb0, hnb), writes=[("ps", pc)], extra=[wtok])
                        mm([(psb[px][:, 0:tn], wv[:, 1, k, :], hT[:, k, ts:ts + tn]) for k in range(8)],
                           reads=khall(hb0, hnb), writes=[("ps", px)])
                        f1 = nextf()
                        op(ACT, lambda: nc.scalar.activation(out=fbuf[f1][:, 0:tn], in_=psb[pc][:, 0:tn], func=AF.Copy),
                           reads=[("ps", pc)], writes=[("f", f1)])
                        PS.free(pc)
                        ui = ts - base_u
                        op(DVE, lambda: nc.vector.tensor_tensor(out=ubuf[:, ui:ui + tn], in0=psb[px][:, 0:tn],
                                                                in1=fbuf[f1][:, 0:tn], op=ALU.mult),
                           reads=[("ps", px), ("f", f1)], writes=[("u",)])
                        PS.free(px)
                    for ti_, (ob, onb) in enumerate(own_tiles):
                        on = onb * 128
                        os_ = ob * 128
                        pbk = PS.alloc()
                        hs = lo * 128 + os_
                        tokb = mm([(psb[pbk][:, 0:on], wv[:, 2, k, :], hT[:, k, hs:hs + on]) for k in range(8)],
                                  reads=khall(lo + ob, onb), writes=[("ps", pbk)])
                        yi = ti_ % 2
                        yb = ybuf[yi]
                        cw = CVW + c * 3
                        op(DVE, lambda: nc.vector.tensor_scalar(out=yb[:, 0:on], in0=ubuf[:, os_ + 1:os_ + 1 + on],
                                                                scalar1=cst[:, cw + 1:cw + 2], scalar2=None, op0=ALU.mult),
                           reads=[("u",), ("cst",)], writes=[("y", yi)])
                        op(DVE, lambda: nc.vector.scalar_tensor_tensor(out=yb[:, 0:on], in0=ubuf[:, os_:os_ + on],
                                                                       scalar=cst[:, cw:cw + 1], in1=yb[:, 0:on],
                                                                       op0=ALU.mult, op1=ALU.add),
                           reads=[("u",), ("y", yi)], writes=[("y", yi)])
                        op(DVE, lambda: nc.vector.scalar_tensor_tensor(out=yb[:, 0:on], in0=ubuf[:, os_ + 2:os_ + 2 + on],
                                                                       scalar=cst[:, cw + 2:cw + 3], in1=yb[:, 0:on],
                                                                       op0=ALU.mult, op1=ALU.add),
                           reads=[("u",), ("y", yi)], writes=[("y", yi)])
                        op(DVE, lambda: nc.vector.tensor_tensor(out=BM[:, c, os_:os_ + on], in0=psb[pbk][:, 0:on],
                                                                in1=yb[:, 0:on], op=ALU.mult),
                           reads=[("ps", pbk), ("y", yi)], writes=kbm(c, ob, onb))
                        PS.free(pbk)
                    W.release(widx, tokb)

                def branch_out(prefix, first, ymm_fn, post_scale=None):
                    for n_ in range(8):
                        widx, woff, wtok = wget(f"{prefix}{n_}")
                        last = None
                        for (ob, onb) in own_tiles:
                            on = onb * 128
                            os_ = ob * 128
                            hs = lo * 128 + os_
                            py = PS.alloc()
                            pg = PS.alloc()
                            mms, rk, gw = ymm_fn(n_, woff, py, os_, on, ob, onb)
                            mm(mms, reads=rk, writes=[("ps", py)], extra=[wtok])
                            last = mm([(psb[pg][:, 0:on], gw[:, k, :], hT[:, k, hs:hs + on]) for k in range(8)],
                                      reads=khall(lo + ob, onb), writes=[("ps", pg)])
                            f1 = nextf()
                            op(ACT, lambda: nc.scalar.activation(out=fbuf[f1][:, 0:on], in_=psb[pg][:, 0:on], func=AF.Sigmoid),
                               reads=[("ps", pg)], writes=[("f", f1)])
                            PS.free(pg)
                            if first:
                                op(DVE, lambda: nc.vector.tensor_tensor(out=BM[:, 8 + n_, os_:os_ + on], in0=psb[py][:, 0:on],
                                                                        in1=fbuf[f1][:, 0:on], op=ALU.mult),
                                   reads=[("ps", py), ("f", f1)], writes=kbm(8 + n_, ob, onb))
                            else:
                                if post_scale is None:
                                    op(DVE, lambda: nc.vector.tensor_tensor(out=fbuf[f1][:, 0:on], in0=psb[py][:, 0:on],
                                                                            in1=fbuf[f1][:, 0:on], op=ALU.mult),
                                       reads=[("ps", py), ("f", f1)], writes=[("f", f1)])
                                else:
                                    sc = post_scale + n_
                                    op(DVE, lambda: nc.vector.scalar_tensor_tensor(
                                        out=fbuf[f1][:, 0:on], in0=psb[py][:, 0:on], scalar=cst[:, sc:sc + 1],
                                        in1=fbuf[f1][:, 0:on], op0=ALU.mult, op1=ALU.mult),
                                        reads=[("ps", py), ("f", f1), ("cst",)], writes=[("f", f1)])
                                op(DVE, lambda: nc.vector.tensor_tensor(out=BM[:, 8 + n_, os_:os_ + on], in0=BM[:, 8 + n_, os_:os_ + on],
                                                                        in1=fbuf[f1][:, 0:on], op=ALU.add),
                                   reads=kbm(8 + n_, ob, onb) + [("f", f1)], writes=kbm(8 + n_, ob, onb))
                            PS.free(py)
                        W.release(widx, last)

                def ymm_full(n_, woff, py, os_, on, ob, onb):
                    wv2 = arena[:, woff:woff + 2048].rearrange("p (a k n) -> p a k n", a=2, k=8)
                    mms = [(psb[py][:, 0:on], wv2[:, 0, c, :], BM[:, c, os_:os_ + on]) for c in range(8)]
                    return mms, kbmall(range(8), ob, onb), wv2[:, 1]

                phase[0] = 'ao'
                branch_out("ao", True, ymm_full)

                phase[0] = 'poolU'
                barrier()
                i0, o0, t0_ = wget("wu0")
                i1, o1, t1_ = wget("wu1")
                ip, op_, tp_ = wget("wp")
                wu = [arena[:, o0:o0 + 4096].rearrange("p (k n) -> p k n", k=8),
                      arena[:, o1:o1 + 4096].rearrange("p (k n) -> p k n", k=8)]
                wp = arena[:, op_:op_ + 2048].rearrange("p (g k n) -> p g k n", g=4, k=2)
                lastu = None
                for i in range(ne):
                    slot = i % 4
                    for hf in range(2):
                        pu = PS.alloc()
                        lastu = mm([(psb[pu][:, :], hT[:, k, i * 128:(i + 1) * 128], wu[hf][:, k, :]) for k in range(8)],
                                   reads=khall(i, 1), writes=[("ps", pu)], extra=[t0_, t1_])
                        if hf == 0:
                            op(ACT, lambda: nc.scalar.activation(out=Usb[:, slot, 0:512], in_=psb[pu][:, :], func=AF.Copy),
                               reads=[("ps", pu)], writes=[("usb", slot, 0)])
                        else:
                            op(DVE, lambda: nc.vector.tensor_copy(out=Usb[:, slot, 512:1024], in_=psb[pu][:, :]),
                               reads=[("ps", pu)], writes=[("usb", slot, 1)])
                        PS.free(pu)
                    j = i - 1
                    if j >= lo and j < lo + nb:
                        gj = e0 + j
                        srcs = [d for d in (-1, 0, 1) if 0 <= j + d < ne]
                        for half in range(2):
                            pp = PS.alloc()
                            for cc in range(4):
                                c = half * 4 + cc
                                g = c // 2
                                mms = []
                                for d in srcs:
                                    if gj == 0:
                                        bnd = bands[:, 12 + g, :] if d == 0 else bands[:, g * 3 + 2, :]
                                    else:
                                        bnd = bands[:, g * 3 + (d + 1), :]
                                    mms.append((psb[pp][:, cc * 128:(cc + 1) * 128],
                                                Usb[:, (j + d) % 4, c * 128:(c + 1) * 128], bnd))
                                mm(mms, reads=[("usb", (j + d) % 4, c // 4) for d in srcs] + [("bands",)],
                                   writes=[("ps", pp)])
                            ob_ = j - lo
                            if half == 0:
                                op(ACT, lambda: nc.scalar.activation(
                                    out=BM[:, 0:4, ob_ * 128:(ob_ + 1) * 128],
                                    in_=psb[pp][:, :].rearrange("p (a b) -> p a b", a=4), func=AF.Copy),
                                    reads=[("ps", pp)], writes=kbmall(range(0, 4), ob_, 1))
                            else:
                                op(DVE, lambda: nc.vector.tensor_copy(
                                    out=BM[:, 4:8, ob_ * 128:(ob_ + 1) * 128],
                                    in_=psb[pp][:, :].rearrange("p (a b) -> p a b", a=4)),
                                    reads=[("ps", pp)], writes=kbmall(range(4, 8), ob_, 1))
                            PS.free(pp)
                W.release(i0, lastu)
                W.release(i1, lastu)

                def ymm_pool(n_, woff, py, os_, on, ob, onb):
                    g = n_ // 2
                    gw = arena[:, woff:woff + 1024].rearrange("p (k n) -> p k n", k=8)
                    mms = [(psb[py][:, 0:on], wp[:, g, kk, (n_ % 2) * 128:(n_ % 2) * 128 + 128],
                            BM[:, 2 * g + kk, os_:os_ + on]) for kk in range(2)]
                    return mms, kbmall([2 * g, 2 * g + 1], ob, onb), gw

                phase[0] = 'poolY'
                PE.wait(tp_)
                branch_out("gp", False, ymm_pool, post_scale=PSC)
                W.release(ip, (PE.src, PE.src.cnt))

                phase[0] = 'kv'
                barrier()
                op(DVE, lambda: nc.vector.memset(Vsb[:, :, :, :], 1.0), writes=[("v", i) for i in range(MAXEXT)])
                op(DVE, lambda: nc.vector.memset(QTA[:, :], 0.0), writes=[("qta",)])
                op(DVE, lambda: nc.vector.memset(QTB[:, :], 0.0), writes=[("qtb",)])
                ik, ok_, tk_ = wget("wk")
                iv, ov_, tv_ = wget("wv")
                wk = arena[:, ok_:ok_ + 2048].rearrange("p (k n) -> p k n", k=8)
                wvv = arena[:, ov_:ov_ + 2048].rearrange("p (k n) -> p k n", k=8)
                lastk = None
                for kc in range(2):
                    for (eb, enb) in ext_tiles:
                        en = enb * 128
                        pk = PS.alloc()
                        lastk = mm([(psb[pk][:, 0:en], wk[:, k, kc * 128:(kc + 1) * 128], hT[:, k, eb * 128:eb * 128 + en])
                                    for k in range(8)], reads=khall(eb, enb), writes=[("ps", pk)], extra=[tk_])
                        op(ACT, lambda: nc.scalar.activation(out=KT[:, kc, eb * 128:eb * 128 + en], in_=psb[pk][:, 0:en], func=AF.Copy),
                           reads=[("ps", pk)], writes=[("kt", kc, b) for b in range(eb, eb + enb)])
                        PS.free(pk)
                W.release(ik, lastk)
                lastv = None
                for i in range(ne):
                    pv = PS.alloc()
                    lastv = mm([(psb[pv][:, 0:256], hT[:, k, i * 128:(i + 1) * 128], wvv[:, k, :]) for k in range(8)],
                               reads=khall(i, 1), writes=[("ps", pv)], extra=[tv_])
                    pvv = psb[pv][:, 0:256].rearrange("p (a b c) -> p a b c", a=2, b=2)
                    op(DVE, lambda: nc.vector.tensor_copy(out=Vsb[:, i, 0::2, 0:64], in_=pvv[:, :, 0, :]),
                       reads=[("ps", pv)], writes=[("v", i)])
                    op(DVE, lambda: nc.vector.tensor_copy(out=Vsb[:, i, 1::2, 64:128], in_=pvv[:, :, 1, :]),
                       reads=[("ps", pv)], writes=[("v", i)])
                    PS.free(pv)
                W.release(iv, lastv)

                phase[0] = 'attn'
                for qc in range(8):
                    iq, oq, tq = wget(f"wq{qc}")
                    wq = arena[:, oq:oq + 1024].rearrange("p (k n) -> p k n", k=8)
                    kvc = qc // 4
                    lastq = None
                    for (ob, onb) in own_tiles:
                        on = onb * 128
                        os_ = ob * 128
                        hs = lo * 128 + os_
                        pq = PS.alloc()
                        lastq = mm([(psb[pq][:, 0:on], wq[:, k, :], hT[:, k, hs:hs + on]) for k in range(8)],
                                   reads=khall(lo + ob, onb), writes=[("ps", pq)], extra=[tq])
                        op(DVE, lambda: nc.vector.tensor_scalar(out=QTA[0:64, os_:os_ + on], in0=psb[pq][0:64, 0:on],
                                                                scalar1=0.125, scalar2=None, op0=ALU.mult),
                           reads=[("ps", pq)], writes=[("qta",)])
                        op(DVE, lambda: nc.vector.tensor_scalar(out=QTB[64:128, os_:os_ + on], in0=psb[pq][64:128, 0:on],
                                                                scalar1=0.125, scalar2=None, op0=ALU.mult),
                           reads=[("ps", pq)], writes=[("qtb",)])
                        PS.free(pq)
                    W.release(iq, lastq)
                    QT = [QTA, QTB]
                    qrange = {}
                    po = [None, None]
                    for j in range(ne + LAG + 1):
                        if j < ne:
                            qlo = max(j - lo - 1, 0)
                            qhi = min(j - lo + 1, nb - 1)
                            if qlo <= qhi:
                                nq = qhi - qlo + 1
                                qrange[j] = (qlo, qhi)
                                dlo = qlo + lo - j
                                for r in range(2):
                                    pss = PS.alloc()
                                    mm([(psb[pss][:, 0:nq * 128], KT[:, kvc, j * 128:(j + 1) * 128],
                                         QT[r][:, qlo * 128:(qhi + 1) * 128])],
                                       reads=[("kt", kvc, j), ("qta",) if r == 0 else ("qtb",)], writes=[("ps", pss)])
                                    eb_ = Ebuf[r][j % 2]
                                    op(ACT, lambda: nc.scalar.activation(out=eb_[:, 0:nq * 128], in_=psb[pss][:, 0:nq * 128], func=AF.Exp),
                                       reads=[("ps", pss)], writes=[("e", r, j % 2)])
                                    PS.free(pss)
                                    hidx = 2 * qc + r
                                    EE, ee = (POOL, nc.gpsimd) if r == 0 else (DVE, nc.vector)
                                    op(EE, lambda: ee.tensor_tensor(
                                        out=PT[r][j % NPT][:, 0:nq * 128], in0=eb_[:, 0:nq * 128],
                                        in1=EB[:, hidx, (dlo + 1) * 128:(dlo + 1 + nq) * 128], op=ALU.mult),
                                        reads=[("e", r, j % 2), ("eb", hidx)], writes=[("pt", r, j % NPT)])
                        qb = j - LAG - lo - 1
                        if 0 <= qb < nb:
                            sl = qb % 4
                            if sl == 0:
                                po = [PS.alloc(), PS.alloc()]
                            for r in range(2):
                                kv = kvc * 2 + r
                                jj_list = [jj for jj in (qb + lo - 1, qb + lo, qb + lo + 1) if jj in qrange]
                                mms = []
                                for jj in jj_list:
                                    co_ = (qb - qrange[jj][0]) * 128
                                    mms.append((psb[po[r]][:, sl * 128:(sl + 1) * 128], Vsb[:, jj, kv, :],
                                                PT[r][jj % NPT][:, co_:co_ + 128]))
                                mm(mms, reads=[("v", jj) for jj in jj_list] + [("pt", r, jj % NPT) for jj in jj_list],
                                   writes=[("ps", po[r])])
                            if sl == 3 or qb == nb - 1:
                                nn = (sl + 1) * 128
                                q0 = (qb - sl) * 128
                                es_ = l * 8 + qc
                                op(ACT, lambda: nc.scalar.activation(out=rden[0:64, 0:nn], in_=psb[po[0]][64:128, 0:nn], func=AF.Ln,
                                                                     bias=esink[0:64, es_:es_ + 1], scale=1.0),
                                   reads=[("ps", po[0]), ("esink", l)], writes=[("rden", 0)])
                                op(ACT, lambda: nc.scalar.activation(out=rden[64:128, 0:nn], in_=psb[po[1]][0:64, 0:nn], func=AF.Ln,
                                                                     bias=esink[64:128, es_:es_ + 1], scale=1.0),
                                   reads=[("ps", po[1]), ("esink", l)], writes=[("rden", 1)])
                                op(ACT, lambda: nc.scalar.activation(out=rden[:, 0:nn], in_=rden[:, 0:nn], func=AF.Exp, scale=-1.0),
                                   reads=[("rden", 0), ("rden", 1)], writes=[("rden", 0), ("rden", 1)])
                                op(DVE, lambda: nc.vector.tensor_tensor(out=BM[0:64, qc, q0:q0 + nn], in0=psb[po[0]][0:64, 0:nn],
                                                                        in1=rden[0:64, 0:nn], op=ALU.mult),
                                   reads=[("ps", po[0]), ("rden", 0)], writes=kbm(qc, qb - sl, sl + 1))
                                op(DVE, lambda: nc.vector.tensor_tensor(out=BM[64:128, qc, q0:q0 + nn], in0=psb[po[1]][64:128, 0:nn],
                                                                        in1=rden[64:128, 0:nn], op=ALU.mult),
                                   reads=[("ps", po[1]), ("rden", 1)], writes=kbm(qc, qb - sl, sl + 1))
                                PS.free(po[0])
                                PS.free(po[1])

                def ymm_attn(n_, woff, py, os_, on, ob, onb):
                    wv2 = arena[:, woff:woff + 2048].rearrange("p (a k n) -> p a k n", a=2, k=8)
                    mms = [(psb[py][:, 0:on], wv2[:, 0, c, :], BM[:, c, os_:os_ + on]) for c in range(8)]
                    return mms, kbmall(range(8), ob, onb), wv2[:, 1]

                phase[0] = 'at'
                branch_out("at", False, ymm_attn)

                phase[0] = 'wo'
                for n_ in range(8):
                    widx, woff, wtok = wget(f"wo{n_}")
                    wo = arena[:, woff:woff + 1024].rearrange("p (k n) -> p k n", k=8)
                    last = None
                    for (ob, onb) in own_tiles:
                        on = onb * 128
                        os_ = ob * 128
                        gs = (b0 + ob) * 128
                        px = PS.alloc()
                        last = mm([(psb[px][:, 0:on], wo[:, c, :], BM[:, 8 + c, os_:os_ + on]) for c in range(8)],
                                  reads=kbmall(range(8, 16), ob, onb), writes=[("ps", px)], extra=[wtok])
                        op(DVE, lambda: nc.vector.tensor_tensor(out=xT[:, n_, gs:gs + on], in0=xT[:, n_, gs:gs + on],
                                                                in1=psb[px][:, 0:on], op=ALU.add),
                           reads=[("ps", px)] + kx(n_, b0 + ob, onb), writes=kx(n_, b0 + ob, onb))
                        PS.free(px)
                    W.release(widx, last)

                phase[0] = 'ffn_norm'
                barrier()
                blks = list(range(b0, b1, 2))
                for gb in blks:
                    if gb + 2 > b1:
                        gb = b1 - 2
                    rmsnorm(gb, 2, G_FFN,
                            lambda c, t0, n: hT[:, c, t0 - e0 * 128:t0 - e0 * 128 + n],
                            lambda c, blk: ("h", c, blk - e0))
                for fg in range(NFG):
                    phase[0] = 'ffn_gu'
                    for f in range(FPG):
                        widx, woff, wtok = wget(f"gu{fg}_{f}")
                        wg = arena[:, woff:woff + 2048].rearrange("p (a k n) -> p a k n", a=2, k=8)
                        last = None
                        for (ob, onb) in own_tiles:
                            on = onb * 128
                            os_ = ob * 128
                            hs = lo * 128 + os_
                            pg = PS.alloc()
                            pu = PS.alloc()
                            mm([(psb[pg][:, 0:on], wg[:, 0, k, :], hT[:, k, hs:hs + on]) for k in range(8)],
                               reads=khall(lo + ob, onb), writes=[("ps", pg)], extra=[wtok])
                            last = mm([(psb[pu][:, 0:on], wg[:, 1, k, :], hT[:, k, hs:hs + on]) for k in range(8)],
                                      reads=khall(lo + ob, onb), writes=[("ps", pu)])
                            f1 = nextf()
                            op(ACT, lambda: nc.scalar.activation(out=fbuf[f1][:, 0:on], in_=psb[pg][:, 0:on], func=AF.Silu),
                               reads=[("ps", pg)], writes=[("f", f1)])
                            PS.free(pg)
                            op(DVE, lambda: nc.vector.tensor_tensor(out=BM[:, f, os_:os_ + on], in0=psb[pu][:, 0:on],
                                                                    in1=fbuf[f1][:, 0:on], op=ALU.mult),
                               reads=[("ps", pu), ("f", f1)], writes=kbm(f, ob, onb))
                            PS.free(pu)
                        W.release(widx, last)
                    phase[0] = 'ffn_d'
                    for n_ in range(8):
                        widx, woff, wtok = wget(f"wd{fg}_{n_}")
                        wd = arena[:, woff:woff + 1408].rearrange("p (f n) -> p f n", f=FPG)
                        last = None
                        for (ob, onb) in own_tiles:
                            on = onb * 128
                            os_ = ob * 128
                            gs = (b0 + ob) * 128
                            pd = PS.alloc()
                            last = mm([(psb[pd][:, 0:on], wd[:, f, :], BM[:, f, os_:os_ + on]) for f in range(FPG)],
                                      reads=kbmall(range(FPG), ob, onb), writes=[("ps", pd)], extra=[wtok])
                            op(DVE, lambda: nc.vector.tensor_tensor(out=xT[:, n_, gs:gs + on], in0=xT[:, n_, gs:gs + on],
                                                                    in1=psb[pd][:, 0:on], op=ALU.add),
                               reads=[("ps", pd)] + kx(n_, b0 + ob, onb), writes=kx(n_, b0 + ob, onb))
                            PS.free(pd)
                        W.release(widx, last)

        phase[0] = 'final'
        barrier()
        s_o = Src(sem("s_out"))
        ov = out_d.rearrange("(c p) t -> p c t", p=128)
        if do_final:
            goff = n_layers * CST_PER_L
            for gb in range(0, OUTB, 2):
                rmsnorm(gb, 2, goff,
                        lambda c, t0, n: xT[:, c, t0:t0 + n],
                        lambda c, blk: ("x", c, blk))
        for (ob, onb) in split_blocks(OUTB, 4):
            dma(SP, ov[:, :, ob * 128:(ob + onb) * 128], xT[:, :, ob * 128:(ob + onb) * 128], s_o,
                reads=kxall(ob, onb))
        nc.sync.wait_ge(s_o.sem, s_o.cnt)
        for E in (PE, ACT, DVE):
            nc.sync.wait_ge(E.src.sem, E.src.cnt)
    return nc


def _kp(mat):
    n = mat.shape[1]
    return mat.reshape(8, 128, n).transpose(1, 0, 2)


def pack_layer(w_in, w_a_out, w_pool, w_attn_out, w_o, w_gu, w_down):
    Bc, Cc, Xc, Uc, Qc, Kc, Vc, Gac, Gpc, Gtc = 0, 1024, 2048, 3072, 4096, 5120, 5376, 5632, 6656, 7680
    pcs = []

    def add(a):
        pcs.append(np.ascontiguousarray(a, dtype=np.float32).reshape(128, -1))
    for c in range(8):
        add(np.stack([_kp(w_in[:, Cc + c * 128:Cc + (c + 1) * 128]),
                      _kp(w_in[:, Xc + c * 128:Xc + (c + 1) * 128]),
                      _kp(w_in[:, Bc + c * 128:Bc + (c + 1) * 128])], axis=1))
    for n in range(8):
        add(np.stack([_kp(w_a_out[:, n * 128:(n + 1) * 128]),
                      _kp(w_in[:, Gac + n * 128:Gac + (n + 1) * 128])], axis=1))
    add(_kp(w_in[:, Uc:Uc + 512]))
    add(_kp(w_in[:, Uc + 512:Uc + 1024]))
    add(w_pool.reshape(4, 2, 128, 256).transpose(2, 0, 1, 3))
    for n in range(8):
        add(_kp(w_in[:, Gpc + n * 128:Gpc + (n + 1) * 128]))
    add(_kp(w_in[:, Kc:Kc + 256]))
    add(_kp(w_in[:, Vc:Vc + 256]))
    qcols = np.concatenate([np.arange(Qc + h * 64, Qc + (h + 1) * 64) for h in HEAD_ORDER])
    wq = w_in[:, qcols]
    for q in range(8):
        add(_kp(wq[:, q * 128:(q + 1) * 128]))
    orow = np.concatenate([np.arange(h * 64, (h + 1) * 64) for h in HEAD_ORDER])
    wao = w_attn_out[orow, :]
    for n in range(8):
        add(np.stack([_kp(wao[:, n * 128:(n + 1) * 128]),
                      _kp(w_in[:, Gtc + n * 128:Gtc + (n + 1) * 128])], axis=1))
    for n in range(8):
        add(_kp(w_o[:, n * 128:(n + 1) * 128]))
    for fg in range(NFG):
        for f in range(FPG):
            fi = fg * FPG + f
            add(np.stack([_kp(w_gu[:, fi * 128:(fi + 1) * 128]),
                          _kp(w_gu[:, DFF + fi * 128:DFF + (fi + 1) * 128])], axis=1))
        wdg = w_down[fg * FPG * 128:(fg + 1) * FPG * 128, :].reshape(FPG, 128, D).transpose(1, 0, 2)
        for n in range(8):
            add(wdg[:, :, n * 128:(n + 1) * 128])
    out = np.concatenate(pcs, axis=1)
    assert out.shape == (128, EPP), out.shape
    return out


def t5_bucket_np(rel):
    import math
    import jax
    import jax.numpy as jnp
    try:
        dev = jax.devices("cpu")[0]
    except Exception:
        dev = None
    ctx = jax.default_device(dev) if dev is not None else contextlib.nullcontext()
    with ctx:
        rel = jnp.asarray(rel, dtype=jnp.int32)
        n_buckets, max_distance = 32, 128
        half = n_buckets // 2
        max_exact = half // 2
        ret = jnp.where(rel > 0, half, 0)
        n = jnp.abs(rel)
        nf = jnp.maximum(n, 1).astype(jnp.float32)
        large = max_exact + (jnp.log(nf / max_exact) / math.log(max_distance / max_exact)
                             * (half - max_exact)).astype(jnp.int32)
        large = jnp.minimum(large, half - 1)
        return np.asarray(ret + jnp.where(n < max_exact, n, large))


def make_bias_tiles(rel_bias, sgn):
    k = np.arange(128)[:, None]
    qq = np.arange(384)[None, :]
    d = qq // 128 - 1
    q = qq % 128
    rl = k - q - 128 * d
    valid = np.abs(rl) <= 128
    bk = t5_bucket_np(sgn * rl)
    out = np.empty((128, NH, 384), np.float32)
    for hi, h in enumerate(HEAD_ORDER):
        out[:, hi, :] = np.where(valid, rel_bias[bk, h], np.float32(NEGB))
    return out


def make_bands(sgn):
    out = np.zeros((128, 16, 128), np.float32)
    tp = np.arange(128)[:, None]
    t = np.arange(128)[None, :]
    for g, w in enumerate(POOL_W):
        if sgn > 0:
            lo_o, hi_o = -(w // 2), (w - 1 - w // 2)
        else:
            lo_o, hi_o = -(w - 1 - w // 2), (w // 2)
        for d in (-1, 0, 1):
            off = 128 * d + tp - t
            m = ((off >= lo_o) & (off <= hi_o)).astype(np.float32) / np.float32(w)
            if d == 0:
                m = m - (tp == t).astype(np.float32)
            out[:, g * 3 + d + 1, :] = m
        off = tp - t
        inwin = (off >= lo_o) & (off <= hi_o)
        cnt = ((t + hi_o) - np.maximum(t + lo_o, 0) + 1).astype(np.float32)
        m = inwin.astype(np.float32) / cnt - (tp == t).astype(np.float32)
        out[:, 12 + g, :] = m
    return out.reshape(128, 16 * 128)


def make_cst(layers, conv_w, pool_scale, attn_sink, g_mix, g_ffn, g_final, sgn):
    nl = len(layers)
    out = np.zeros((128, nl * CST_PER_L + 8), np.float32)

    def pc(v):
        return v.reshape(8, 128).T
    for i, l in enumerate(layers):
        o = i * CST_PER_L
        out[:, o:o + 8] = pc(g_mix[l])
        out[:, o + 8:o + 16] = pc(g_ffn[l])
        out[:, o + 16:o + 24] = pc(pool_scale[l])
        cw = conv_w[l, :, 0, :]
        if sgn < 0:
            cw = cw[::-1]
        out[:, o + 24:o + 48] = np.stack([pc(cw[k]) for k in range(3)], axis=2).reshape(128, 24)
        for qc in range(8):
            out[0:64, o + 48 + qc] = attn_sink[l, HEAD_ORDER[2 * qc]]
            out[64:128, o + 48 + qc] = attn_sink[l, HEAD_ORDER[2 * qc + 1]]
    out[:, nl * CST_PER_L:] = pc(g_final)
    return out


FUSED = True
LAST_LABELS = {}
DBG_LAYERS = None


def kernel(x, w_in, conv_w, w_a_out, w_pool, pool_scale, w_attn_out, attn_sink, w_o,
           g_mix, g_ffn, w_gu, w_down, rel_bias, g_final):
    f = lambda a: np.asarray(a, dtype=np.float32)
    x, w_in, conv_w, w_a_out, w_pool, pool_scale = map(f, (x, w_in, conv_w, w_a_out, w_pool, pool_scale))
    w_attn_out, attn_sink, w_o, g_mix, g_ffn, w_gu, w_down, rel_bias, g_final = map(
        f, (w_attn_out, attn_sink, w_o, g_mix, g_ffn, w_gu, w_down, rel_bias, g_final))
    wst = [pack_layer(w_in[l], w_a_out[l], w_pool[l], w_attn_out[l], w_o[l], w_gu[l], w_down[l])
           for l in range(DEPTH)]
    bias_t = {s: make_bias_tiles(rel_bias, s) for s in (1, -1)}
    bands = {s: make_bands(s) for s in (1, -1)}
    half = SEQ // 2

    def run(layers, regions, tin_blk, do_final, xin):
        nc = build_program(len(layers), regions, tin_blk, do_final)
        wst_l = np.ascontiguousarray(np.stack([wst[l] for l in layers], axis=0))
        in_maps = []
        for c in range(8):
            sgn = 1 if c % 2 == 0 else -1
            in_maps.append({
                "xT": np.ascontiguousarray(xin[c].T),
                "wst": wst_l,
                "cst": make_cst(layers, conv_w, pool_scale, attn_sink, g_mix, g_ffn, g_final, sgn),
                "biasT": bias_t[sgn],
                "bands": bands[sgn],
            })
        res = run_bass_kernel_spmd(nc, in_maps, core_ids=list(range(8)))
        return [np.asarray(r["outT"]).T for r in res.results]

    def local_slices(xfull, ntok):
        outs = []
        for c in range(8):
            b, hf = c // 2, c % 2
            if hf == 0:
                outs.append(xfull[b, 0:ntok, :])
            else:
                outs.append(xfull[b, ::-1, :][0:ntok, :])
        return outs

    def assemble(parts):
        out = np.empty((BATCH, SEQ, D), np.float32)
        for c in range(8):
            b, hf = c // 2, c % 2
            if hf == 0:
                out[b, 0:half, :] = parts[c]
            else:
                out[b, half:, :] = parts[c][::-1, :]
        return out

    nrun = DEPTH if DBG_LAYERS is None else DBG_LAYERS
    fin = DBG_LAYERS is None
    if FUSED:
        regions = [OWN_BLK + (nrun - 1 - l) for l in range(nrun)]
        tin = regions[0] + 1
        parts = run(list(range(nrun)), regions, tin, fin, local_slices(x, tin * 128))
        return assemble(parts)
    else:
        cur = x
        for l in range(nrun):
            parts = run([l], [OWN_BLK], OWN_BLK + 1, fin and l == nrun - 1, local_slices(cur, (OWN_BLK + 1) * 128))
            cur = assemble(parts)
        return cur
```

```python
import numpy as np
import contextlib
import concourse.bass as bass
import concourse.mybir as mybir
from concourse.bass_utils import run_bass_kernel_spmd

F32 = mybir.dt.float32
BF16 = mybir.dt.bfloat16
AF = mybir.ActivationFunctionType
ALU = mybir.AluOpType

D = 1024
NCH = 8
SEQ = 4096
BATCH = 4
DEPTH = 4
NH = 16
HD = 64
DFF = 2816
NFC = 22
NFG = 2
FPG = 11
EPS = 1e-6
OWN_BLK = 16
MAXST = 6
MAXEXT = 8
LAG = 2
NPT = 6
TT = MAXST * 128
HEAD_ORDER = [0, 4, 1, 5, 2, 6, 3, 7, 8, 12, 9, 13, 10, 14, 11, 15]
POOL_W = (2, 4, 8, 16)
EPP = 163840
ARENA = 12288
NWSEM = 12
NEGB = -30000.0

CST_PER_L = 56


class Src:
    def __init__(self, sem):
        self.sem = sem
        self.cnt = 0


class Eng:
    def __init__(self, eng, sem, name):
        self.eng = eng
        self.src = Src(sem)
        self.name = name
        self.seen = {}

    def wait(self, tok):
        if tok is None:
            return
        src, c = tok
        if self.seen.get(id(src), 0) >= c:
            return
        self.eng.wait_ge(src.sem, c)
        self.seen[id(src)] = c

    def issue(self, inst):
        self.src.cnt += 1
        inst.then_inc(self.src.sem, 1)
        return (self.src, self.src.cnt)


class Tracker:
    def __init__(self):
        self.w = {}
        self.r = {}

    def deps(self, reads, writes):
        toks = []
        for k in reads:
            t = self.w.get(k)
            if t is not None:
                toks.append(t)
        for k in writes:
            t = self.w.get(k)
            if t is not None:
                toks.append(t)
            rr = self.r.get(k)
            if rr:
                toks.extend(rr.values())
        return toks

    def commit(self, tok, reads, writes):
        for k in reads:
            rr = self.r.setdefault(k, {})
            rr[id(tok[0])] = tok
        for k in writes:
            self.w[k] = tok
            self.r[k] = {}


class PsumAlloc:
    def __init__(self, banks):
        self.free_list = list(range(len(banks)))
        self.banks = banks

    def alloc(self):
        assert self.free_list, "out of PSUM banks"
        return self.free_list.pop(0)

    def free(self, b):
        self.free_list.append(b)


class WStream:
    def __init__(self, nc, pool, arena, sems):
        self.nc = nc
        self.pool = pool
        self.arena = arena
        self.sems = [Src(s) for s in sems]
        self.plan = []
        self.ni = 0
        self.ng = 0
        self.head = 0
        self.regions = []
        self.rel = {}
        self.info = {}

    def add(self, name, dram_ap, n):
        self.plan.append((name, dram_ap, n))

    def pump(self):
        while self.ni < len(self.plan):
            name, dap, n = self.plan[self.ni]
            idx = self.ni
            off = self.head
            if off + n > ARENA:
                off = 0
            conflicts = [rg for rg in self.regions if rg[0] < off + n and off < rg[0] + rg[1]]
            if any(rg[2] not in self.rel for rg in conflicts):
                return
            old = idx - NWSEM
            if old >= 0 and old not in self.rel:
                return
            for rg in conflicts:
                self.pool.wait(self.rel[rg[2]])
                self.regions.remove(rg)
            if old >= 0:
                self.pool.wait(self.rel[old])
            s = self.sems[idx % NWSEM]
            s.cnt += 16
            self.nc.gpsimd.dma_start(out=self.arena[:, off:off + n], in_=dap).then_inc(s.sem, 16)
            self.info[idx] = (off, n, (s, s.cnt))
            self.regions.append((off, n, idx))
            self.head = off + n
            self.ni += 1

    def get(self, name):
        self.pump()
        pname, _, n = self.plan[self.ng]
        assert pname == name, (pname, name)
        assert self.ng in self.info, f"weight arena deadlock at {name}"
        off, n, tok = self.info[self.ng]
        idx = self.ng
        self.ng += 1
        return idx, off, tok

    def release(self, idx, tok):
        self.rel[idx] = tok
        self.pump()


def split_blocks(n, maxpart):
    parts = -(-n // maxpart)
    base = n // parts
    rem = n % parts
    out = []
    s = 0
    for i in range(parts):
        m = base + (1 if i < rem else 0)
        out.append((s, m))
        s += m
    return out


def build_program(n_layers, regions, tin_blk, do_final):
    nc = bass.Bass("TRN2", target_bir_lowering=False)
    TIN = tin_blk * 128
    OUTB = regions[-1]
    ncst = n_layers * CST_PER_L + 8
    xT_d = nc.dram_tensor("xT", [D, TIN], F32, kind="ExternalInput").ap()
    wst_d = nc.dram_tensor("wst", [n_layers, 128, EPP], F32, kind="ExternalInput").ap()
    cst_d = nc.dram_tensor("cst", [128, ncst], F32, kind="ExternalInput").ap()
    bias_d = nc.dram_tensor("biasT", [128, NH, 384], F32, kind="ExternalInput").ap()
    band_d = nc.dram_tensor("bands", [128, 16 * 128], F32, kind="ExternalInput").ap()
    out_d = nc.dram_tensor("outT", [D, OUTB * 128], F32, kind="ExternalOutput").ap()

    es = contextlib.ExitStack()
    with es:
        def sb(name, shape, dt):
            return es.enter_context(nc.sbuf_tensor(name, shape, dt))

        def sem(name):
            return es.enter_context(nc.semaphore(name))

        xT = sb("xT_sb", [128, NCH, TIN], F32)
        hT = sb("hT", [128, NCH, MAXEXT * 128], BF16)
        stash = sb("stash", [128, NCH, 128], BF16)
        BM = sb("BM", [128, 16, TT], BF16)
        arena = sb("arena", [128, ARENA], BF16)
        cst = sb("cst_sb", [128, ncst], F32)
        esink = sb("esink", [128, n_layers * 8], F32)
        EB = sb("EB", [128, NH, 384], BF16)
        bands = sb("bands_sb", [128, 16, 128], BF16)
        onesM = sb("onesM", [128, 128], BF16)
        fbuf = [sb(f"fbuf{i}", [128, 512], F32) for i in range(4)]
        SCRN = 16384
        scr = sb("scr", [128, SCRN], BF16)
        sq = scr[:, 0:NCH * 512].rearrange("p (a b) -> p a b", a=NCH)
        ubuf = scr[:, 0:2 * (TT + 2)].bitcast(F32)
        ybuf = [scr[:, 2048 + i * 1024:2048 + (i + 1) * 1024].bitcast(F32) for i in range(2)]
        Usb = scr[:, 0:5120].rearrange("p (a b) -> p a b", a=5)
        _o = 0
        KT = scr[:, _o:_o + 2 * MAXEXT * 128].rearrange("p (a b) -> p a b", a=2); _o += 2 * MAXEXT * 128
        Vsb = scr[:, _o:_o + MAXEXT * 512].rearrange("p (a b c) -> p a b c", a=MAXEXT, b=4); _o += MAXEXT * 512
        QTA = scr[:, _o:_o + TT]; _o += TT
        QTB = scr[:, _o:_o + TT]; _o += TT
        QTA2 = scr[:, _o:_o + TT]; _o += TT
        QTB2 = scr[:, _o:_o + TT]; _o += TT
        Ebuf = [[None, None], [None, None]]
        for r in range(2):
            for i in range(2):
                Ebuf[r][i] = scr[:, _o:_o + 384]; _o += 384
        PT = [[None] * NPT, [None] * NPT]
        for r in range(2):
            for i in range(NPT):
                PT[r][i] = scr[:, _o:_o + 384]; _o += 384
        rden = scr[:, _o:_o + 1024].bitcast(F32); _o += 1024
        assert _o <= SCRN, _o
        psb = [es.enter_context(nc.psum_tensor(f"ps{i}", [128, 512], F32)) for i in range(8)]

        PE = Eng(nc.tensor, sem("s_pe"), "pe")
        ACT = Eng(nc.scalar, sem("s_act"), "act")
        DVE = Eng(nc.vector, sem("s_dve"), "dve")
        POOL = Eng(nc.gpsimd, sem("s_pool"), "pool")
        SP = Eng(nc.sync, sem("s_sp"), "sp")
        T = Tracker()
        PS = PsumAlloc(psb)
        W = WStream(nc, POOL, arena, [sem(f"s_w{i}") for i in range(NWSEM)])

        phase = ["init"]
        labels = {"pe": [], "act": [], "dve": [], "pool": [], "sp": []}
        LAST_LABELS.clear()
        LAST_LABELS.update(labels)

        def op(E, fn, reads=(), writes=(), extra=()):
            labels[E.name].append(phase[0])
            for t in T.deps(reads, writes):
                E.wait(t)
            for t in extra:
                E.wait(t)
            tok = E.issue(fn())
            T.commit(tok, reads, writes)
            return tok

        def mm(mms, reads=(), writes=(), extra=()):
            for t in T.deps(reads, writes):
                PE.wait(t)
            for t in extra:
                PE.wait(t)
            n = len(mms)
            inst = None
            for i, (o, l, r) in enumerate(mms):
                labels["pe"].append((phase[0], r.shape[-1]))
                inst = nc.tensor.matmul(o, l, r, start=(i == 0), stop=(i == n - 1))
            tok = PE.issue(inst)
            T.commit(tok, reads, writes)
            return tok

        def dma(E, out, in_, s, reads=(), writes=()):
            for t in T.deps(reads, writes):
                E.wait(t)
            s.cnt += 16
            E.eng.dma_start(out=out, in_=in_).then_inc(s.sem, 16)
            tok = (s, s.cnt)
            T.commit(tok, reads, writes)
            return tok

        def barrier():
            toks = [(E_.src, E_.src.cnt) for E_ in (PE, ACT, DVE, POOL)]
            for E_ in (ACT, DVE, POOL):
                for t in toks:
                    if t[0] is not E_.src and t[1] > 0:
                        E_.wait(t)

        fb_i = [0]

        def nextf():
            fb_i[0] = (fb_i[0] + 1) % len(fbuf)
            return fb_i[0]

        def kx(c, b0, nb):
            return [("x", c, b) for b in range(b0, b0 + nb)]

        def kxall(b0, nb):
            return [("x", c, b) for c in range(NCH) for b in range(b0, b0 + nb)]

        def kh(c, b0, nb):
            return [("h", c, b) for b in range(b0, b0 + nb)]

        def khall(b0, nb):
            return [("h", c, b) for c in range(NCH) for b in range(b0, b0 + nb)]

        def kbm(slot, b0, nb):
            return [("bm", slot, b) for b in range(b0, b0 + nb)]

        def kbmall(slots, b0, nb):
            return [("bm", s_, b) for s_ in slots for b in range(b0, b0 + nb)]

        s_c = Src(sem("s_cst"))
        dma(SP, cst[:, :], cst_d[:, :], s_c, writes=[("cst",)])
        xtiles = split_blocks(tin_blk, 4)
        s_x = [Src(sem(f"s_x{i}")) for i in range(len(xtiles))]
        xv = xT_d.rearrange("(c p) t -> p c t", p=128)
        for i, (b0, nb) in enumerate(xtiles):
            dma(SP, xT[:, :, b0 * 128:(b0 + nb) * 128], xv[:, :, b0 * 128:(b0 + nb) * 128],
                s_x[i], writes=kxall(b0, nb))
        s_b = Src(sem("s_band"))
        s_b.cnt += 16
        nc.gpsimd.dma_start(out=bands[:, :, :], in_=band_d.rearrange("p (a b) -> p a b", b=128)).then_inc(s_b.sem, 16)
        T.commit((s_b, s_b.cnt), [], [("bands",)])
        op(DVE, lambda: nc.vector.memset(onesM[:, :], 1.0 / D), writes=[("ones",)])
        s_bi = [Src(sem(f"s_bias{i}")) for i in range(2)]
        for h in range(NH):
            fi = h % 2
            dma(SP, fbuf[fi][:, 0:384], bias_d[:, h, :], s_bi[fi], writes=[("f", fi)])
            op(ACT, lambda: nc.scalar.activation(out=EB[:, h, :], in_=fbuf[fi][:, 0:384], func=AF.Exp),
               reads=[("f", fi)], writes=[("eb", h)])
        for l in range(n_layers):
            o = l * CST_PER_L + 48
            op(ACT, lambda: nc.scalar.activation(out=esink[:, l * 8:(l + 1) * 8], in_=cst[:, o:o + 8], func=AF.Exp),
               reads=[("cst",)], writes=[("esink", l)])

        for l in range(n_layers):
            nst = len(split_blocks(regions[l], MAXST))
            for s_i in range(nst):
                off = 0

                def addp(name, n):
                    nonlocal off
                    W.add((l, s_i, name), wst_d[l, :, off:off + n], n)
                    off += n
                for c in range(8):
                    addp(f"cv{c}", 3072)
                for n_ in range(8):
                    addp(f"ao{n_}", 2048)
                addp("wu0", 4096)
                addp("wu1", 4096)
                addp("wp", 2048)
                for n_ in range(8):
                    addp(f"gp{n_}", 1024)
                addp("wk", 2048)
                addp("wv", 2048)
                for q in range(8):
                    addp(f"wq{q}", 1024)
                for n_ in range(8):
                    addp(f"at{n_}", 2048)
                for n_ in range(8):
                    addp(f"wo{n_}", 1024)
                for fg in range(NFG):
                    for f in range(FPG):
                        addp(f"gu{fg}_{f}", 2048)
                    for n_ in range(8):
                        addp(f"wd{fg}_{n_}", 1408)
                assert off == EPP

        def rmsnorm(gb0, nblk, goff, dst_fn, dst_keys_fn):
            t0 = gb0 * 128
            n = nblk * 128
            b = gb0
            op(ACT, lambda: nc.scalar.activation(out=sq[:, :, 0:n], in_=xT[:, :, t0:t0 + n], func=AF.Square),
               reads=kxall(b, nblk), writes=[("sq",)])
            pb = PS.alloc()
            mm([(psb[pb][:, 0:n], onesM[:, :], sq[:, c, 0:n]) for c in range(NCH)],
               reads=[("sq",), ("ones",)], writes=[("ps", pb)])
            f1 = nextf()
            op(ACT, lambda: nc.scalar.activation(out=fbuf[f1][:, 0:n], in_=psb[pb][:, 0:n], func=AF.Ln,
                                                 bias=EPS, scale=1.0),
               reads=[("ps", pb)], writes=[("f", f1)])
            PS.free(pb)
            op(ACT, lambda: nc.scalar.activation(out=fbuf[f1][:, 0:n], in_=fbuf[f1][:, 0:n], func=AF.Exp,
                                                 scale=-0.5),
               reads=[("f", f1)], writes=[("f", f1)])
            for c in range(NCH):
                op(DVE, lambda: nc.vector.scalar_tensor_tensor(
                    out=dst_fn(c, t0, n), in0=xT[:, c, t0:t0 + n], scalar=cst[:, goff + c:goff + c + 1],
                    in1=fbuf[f1][:, 0:n], op0=ALU.mult, op1=ALU.mult),
                    reads=kx(c, b, nblk) + [("f", f1), ("cst",)],
                    writes=[dst_keys_fn(c, bb) for bb in range(b, b + nblk)])

        def st_geom(l, s_i):
            sts_ = split_blocks(regions[l], MAXST)
            b0_, nb_ = sts_[s_i]
            e0_ = b0_ - 1 if s_i > 0 else 0
            return sts_, b0_, nb_, e0_, b0_ + nb_ + 1

        def phase0(l, s_i):
            sts_, b0_, nb_, e0_, e1_ = st_geom(l, s_i)
            b1_ = b0_ + nb_
            phase[0] = 'norm'
            barrier()
            if s_i > 0:
                op(DVE, lambda: nc.vector.tensor_copy(out=hT[:, :, 0:128], in_=stash[:, :, :]),
                   reads=[("stash",)], writes=khall(0, 1))
            nstart = b0_ if s_i > 0 else 0
            for (tb, tnb) in split_blocks(e1_ - nstart, 3):
                rmsnorm(nstart + tb, tnb, l * CST_PER_L,
                        lambda c, t0, n: hT[:, c, t0 - e0_ * 128:t0 - e0_ * 128 + n],
                        lambda c, blk: ("h", c, blk - e0_))
            if s_i + 1 < len(sts_):
                sl_ = (b1_ - 1 - e0_) * 128
                op(DVE, lambda: nc.vector.tensor_copy(out=stash[:, :, :], in_=hT[:, :, sl_:sl_ + 128]),
                   reads=khall(b1_ - 1 - e0_, 1), writes=[("stash",)])

        order = [(l_, s_) for l_ in range(n_layers) for s_ in range(len(split_blocks(regions[l_], MAXST)))]
        phase0(*order[0])

        for l in range(n_layers):
            R = regions[l]
            co = l * CST_PER_L
            G_MIX, G_FFN, PSC, CVW, SNK = co, co + 8, co + 16, co + 24, co + 48
            sts = split_blocks(R, MAXST)
            for s_i, (b0, nb) in enumerate(sts):
                b1 = b0 + nb
                e0 = b0 - 1 if s_i > 0 else 0
                e1 = b1 + 1
                ne = e1 - e0
                lo = b0 - e0
                n_own = nb * 128
                own_tiles = split_blocks(nb, 4)
                ext_tiles = split_blocks(ne, 4)

                def wget(name):
                    idx, off, tok = W.get((l, s_i, name))
                    return idx, off, tok

                phase[0] = 'conv'
                barrier()
                base_u = lo * 128 - 1
                cs = max(base_u, 0)
                ce = lo * 128 + n_own + 1
                ctl = []
                nct = -(-(ce - cs) // 512)
                step = -(-(ce - cs) // nct)
                t_ = cs
                while t_ < ce:
                    ctl.append((t_, min(step, ce - t_)))
                    t_ += step
                for c in range(8):
                    widx, woff, wtok = wget(f"cv{c}")
                    wv = arena[:, woff:woff + 3072].rearrange("p (a k n) -> p a k n", a=3, k=8)
                    if s_i == 0:
                        op(DVE, lambda: nc.vector.memset(ubuf[:, 0:1], 0.0), writes=[("u",)])
                    for (ts, tn) in ctl:
                        hb0 = ts // 128
                        hnb = (ts + tn - 1) // 128 - hb0 + 1
                        pc = PS.alloc()
                        px = PS.alloc()
                        mm([(psb[pc][:, 0:tn], wv[:, 0, k, :], hT[:, k, ts:ts + tn]) for k in range(8)],
                           reads=khall(hb0, hnb), writes=[("ps", pc)], extra=[wtok])
                        mm([(psb[px][:, 0:tn], wv[:, 1, k, :], hT[:, k, ts:ts + tn]) for k in range(8)],
                           reads=khall(hb0, hnb), writes=[("ps", px)])
                        f1 = nextf()
                        op(ACT, lambda: nc.scalar.activation(out=fbuf[f1][:, 0:tn], in_=psb[pc][:, 0:tn], func=AF.Copy),
                           reads=[("ps", pc)], writes=[("f", f1)])
                        PS.free(pc)
                        ui = ts - base_u
                        op(DVE, lambda: nc.vector.tensor_tensor(out=ubuf[:, ui:ui + tn], in0=psb[px][:, 0:tn],
                                                                in1=fbuf[f1][:, 0:tn], op=ALU.mult),
                           reads=[("ps", px), ("f", f1)], writes=[("u",)])
                        PS.free(px)
                    for ti_, (ob, onb) in enumerate(own_tiles):
                        on = onb * 128
                        os_ = ob * 128
                        pbk = PS.alloc()
                        hs = lo * 128 + os_
                        tokb = mm([(psb[pbk][:, 0:on], wv[:, 2, k, :], hT[:, k, hs:hs + on]) for k in range(8)],
                                  reads=khall(lo + ob, onb), writes=[("ps", pbk)])
                        yi = ti_ % 2
                        yb = ybuf[yi]
                        cw = CVW + c * 3
                        op(DVE, lambda: nc.vector.tensor_scalar(out=yb[:, 0:on], in0=ubuf[:, os_ + 1:os_ + 1 + on],
                                                                scalar1=cst[:, cw + 1:cw + 2], scalar2=None, op0=ALU.mult),
                           reads=[("u",), ("cst",)], writes=[("y", yi)])
                        op(DVE, lambda: nc.vector.scalar_tensor_tensor(out=yb[:, 0:on], in0=ubuf[:, os_:os_ + on],
                                                                       scalar=cst[:, cw:cw + 1], in1=yb[:, 0:on],
                                                                       op0=ALU.mult, op1=ALU.add),
                           reads=[("u",), ("y", yi)], writes=[("y", yi)])
                        op(DVE, lambda: nc.vector.scalar_tensor_tensor(out=yb[:, 0:on], in0=ubuf[:, os_ + 2:os_ + 2 + on],
                                                                       scalar=cst[:, cw + 2:cw + 3], in1=yb[:, 0:on],
                                                                       op0=ALU.mult, op1=ALU.add),
                           reads=[("u",), ("y", yi)], writes=[("y", yi)])
                        op(DVE, lambda: nc.vector.tensor_tensor(out=BM[:, c, os_:os_ + on], in0=psb[pbk][:, 0:on],
                                                                in1=yb[:, 0:on], op=ALU.mult),
                           reads=[("ps", pbk), ("y", yi)], writes=kbm(c, ob, onb))
                        PS.free(pbk)
                    W.release(widx, tokb)

                def branch_out(prefix, first, ymm_fn, post_scale=None):
                    for n_ in range(8):
                        widx, woff, wtok = wget(f"{prefix}{n_}")
                        last = None
                        for (ob, onb) in own_tiles:
                            on = onb * 128
                            os_ = ob * 128
                            hs = lo * 128 + os_
                            py = PS.alloc()
                            pg = PS.alloc()
                            mms, rk, gw = ymm_fn(n_, woff, py, os_, on, ob, onb)
                            mm(mms, reads=rk, writes=[("ps", py)], extra=[wtok])
                            last = mm([(psb[pg][:, 0:on], gw[:, k, :], hT[:, k, hs:hs + on]) for k in range(8)],
                                      reads=khall(lo + ob, onb), writes=[("ps", pg)])
                            f1 = nextf()
                            op(ACT, lambda: nc.scalar.activation(out=fbuf[f1][:, 0:on], in_=psb[pg][:, 0:on], func=AF.Sigmoid),
                               reads=[("ps", pg)], writes=[("f", f1)])
                            PS.free(pg)
                            if first:
                                op(DVE, lambda: nc.vector.tensor_tensor(out=BM[:, 8 + n_, os_:os_ + on], in0=psb[py][:, 0:on],
                                                                        in1=fbuf[f1][:, 0:on], op=ALU.mult),
                                   reads=[("ps", py), ("f", f1)], writes=kbm(8 + n_, ob, onb))
                            else:
                                if post_scale is None:
                                    op(DVE, lambda: nc.vector.tensor_tensor(out=fbuf[f1][:, 0:on], in0=psb[py][:, 0:on],
                                                                            in1=fbuf[f1][:, 0:on], op=ALU.mult),
                                       reads=[("ps", py), ("f", f1)], writes=[("f", f1)])
                                else:
                                    sc = post_scale + n_
                                    op(DVE, lambda: nc.vector.scalar_tensor_tensor(
                                        out=fbuf[f1][:, 0:on], in0=psb[py][:, 0:on], scalar=cst[:, sc:sc + 1],
                                        in1=fbuf[f1][:, 0:on], op0=ALU.mult, op1=ALU.mult),
                                        reads=[("ps", py), ("f", f1), ("cst",)], writes=[("f", f1)])
                                op(DVE, lambda: nc.vector.tensor_tensor(out=BM[:, 8 + n_, os_:os_ + on], in0=BM[:, 8 + n_, os_:os_ + on],
                                                                        in1=fbuf[f1][:, 0:on], op=ALU.add),
                                   reads=kbm(8 + n_, ob, onb) + [("f", f1)], writes=kbm(8 + n_, ob, onb))
                            PS.free(py)
                        W.release(widx, last)

                def ymm_full(n_, woff, py, os_, on, ob, onb):
                    wv2 = arena[:, woff:woff + 2048].rearrange("p (a k n) -> p a k n", a=2, k=8)
                    mms = [(psb[py][:, 0:on], wv2[:, 0, c, :], BM[:, c, os_:os_ + on]) for c in range(8)]
                    return mms, kbmall(range(8), ob, onb), wv2[:, 1]

                phase[0] = 'ao'
                branch_out("ao", True, ymm_full)

                phase[0] = 'poolU'
                barrier()
                i0, o0, t0_ = wget("wu0")
                i1, o1, t1_ = wget("wu1")
                ip, op_, tp_ = wget("wp")
                wu = [arena[:, o0:o0 + 4096].rearrange("p (k n) -> p k n", k=8),
                      arena[:, o1:o1 + 4096].rearrange("p (k n) -> p k n", k=8)]
                wp = arena[:, op_:op_ + 2048].rearrange("p (g k n) -> p g k n", g=4, k=2)
                lastu = None
                for i in range(ne + 1):
                    if i < ne:
                        slot = i % 5
                        for hf in range(2):
                            pu = PS.alloc()
                            lastu = mm([(psb[pu][:, :], hT[:, k, i * 128:(i + 1) * 128], wu[hf][:, k, :]) for k in range(8)],
                                       reads=khall(i, 1), writes=[("ps", pu)], extra=[t0_, t1_])
                            if hf == 0:
                                op(ACT, lambda: nc.scalar.activation(out=Usb[:, slot, 0:512], in_=psb[pu][:, :], func=AF.Copy),
                                   reads=[("ps", pu)], writes=[("usb", slot, 0)])
                            else:
                                op(DVE, lambda: nc.vector.tensor_copy(out=Usb[:, slot, 512:1024], in_=psb[pu][:, :]),
                                   reads=[("ps", pu)], writes=[("usb", slot, 1)])
                            PS.free(pu)
                    j = i - 2
                    if j >= lo and j < lo + nb:
                        gj = e0 + j
                        srcs = [d for d in (-1, 0, 1) if 0 <= j + d < ne]
                        for half in range(2):
                            pp = PS.alloc()
                            for cc in range(4):
                                c = half * 4 + cc
                                g = c // 2
                                mms = []
                                for d in srcs:
                                    if gj == 0:
                                        bnd = bands[:, 12 + g, :] if d == 0 else bands[:, g * 3 + 2, :]
                                    else:
                                        bnd = bands[:, g * 3 + (d + 1), :]
                                    mms.append((psb[pp][:, cc * 128:(cc + 1) * 128],
                                                Usb[:, (j + d) % 5, c * 128:(c + 1) * 128], bnd))
                                mm(mms, reads=[("usb", (j + d) % 5, c // 4) for d in srcs] + [("bands",)],
                                   writes=[("ps", pp)])
                            ob_ = j - lo
                            if half == 0:
                                op(ACT, lambda: nc.scalar.activation(
                                    out=BM[:, 0:4, ob_ * 128:(ob_ + 1) * 128],
                                    in_=psb[pp][:, :].rearrange("p (a b) -> p a b", a=4), func=AF.Copy),
                                    reads=[("ps", pp)], writes=kbmall(range(0, 4), ob_, 1))
                            else:
                                op(DVE, lambda: nc.vector.tensor_copy(
                                    out=BM[:, 4:8, ob_ * 128:(ob_ + 1) * 128],
                                    in_=psb[pp][:, :].rearrange("p (a b) -> p a b", a=4)),
                                    reads=[("ps", pp)], writes=kbmall(range(4, 8), ob_, 1))
                            PS.free(pp)
                W.release(i0, lastu)
                W.release(i1, lastu)

                def ymm_pool(n_, woff, py, os_, on, ob, onb):
                    g = n_ // 2
                    gw = arena[:, woff:woff + 1024].rearrange("p (k n) -> p k n", k=8)
                    mms = [(psb[py][:, 0:on], wp[:, g, kk, (n_ % 2) * 128:(n_ % 2) * 128 + 128],
                            BM[:, 2 * g + kk, os_:os_ + on]) for kk in range(2)]
                    return mms, kbmall([2 * g, 2 * g + 1], ob, onb), gw

                phase[0] = 'poolY'
                PE.wait(tp_)
                branch_out("gp", False, ymm_pool, post_scale=PSC)
                W.release(ip, (PE.src, PE.src.cnt))

                phase[0] = 'kv'
                barrier()
                op(DVE, lambda: nc.vector.memset(Vsb[:, :, :, :], 1.0), writes=[("v", i) for i in range(MAXEXT)])
                op(DVE, lambda: nc.vector.memset(QTA[:, :], 0.0), writes=[("qta", 0)])
                op(DVE, lambda: nc.vector.memset(QTB[:, :], 0.0), writes=[("qtb", 0)])
                op(DVE, lambda: nc.vector.memset(QTA2[:, :], 0.0), writes=[("qta", 1)])
                op(DVE, lambda: nc.vector.memset(QTB2[:, :], 0.0), writes=[("qtb", 1)])
                ik, ok_, tk_ = wget("wk")
                iv, ov_, tv_ = wget("wv")
                wk = arena[:, ok_:ok_ + 2048].rearrange("p (k n) -> p k n", k=8)
                wvv = arena[:, ov_:ov_ + 2048].rearrange("p (k n) -> p k n", k=8)
                lastk = None
                for kc in range(2):
                    for (eb, enb) in ext_tiles:
                        en = enb * 128
                        pk = PS.alloc()
                        lastk = mm([(psb[pk][:, 0:en], wk[:, k, kc * 128:(kc + 1) * 128], hT[:, k, eb * 128:eb * 128 + en])
                                    for k in range(8)], reads=khall(eb, enb), writes=[("ps", pk)], extra=[tk_])
                        op(ACT, lambda: nc.scalar.activation(out=KT[:, kc, eb * 128:eb * 128 + en], in_=psb[pk][:, 0:en], func=AF.Copy),
                           reads=[("ps", pk)], writes=[("kt", kc, b) for b in range(eb, eb + enb)])
                        PS.free(pk)
                W.release(ik, lastk)
                lastv = None
                for i in range(ne):
                    pv = PS.alloc()
                    lastv = mm([(psb[pv][:, 0:256], hT[:, k, i * 128:(i + 1) * 128], wvv[:, k, :]) for k in range(8)],
                               reads=khall(i, 1), writes=[("ps", pv)], extra=[tv_])
                    pvv = psb[pv][:, 0:256].rearrange("p (a b c) -> p a b c", a=2, b=2)
                    op(DVE, lambda: nc.vector.tensor_copy(out=Vsb[:, i, 0::2, 0:64], in_=pvv[:, :, 0, :]),
                       reads=[("ps", pv)], writes=[("v", i)])
                    op(DVE, lambda: nc.vector.tensor_copy(out=Vsb[:, i, 1::2, 64:128], in_=pvv[:, :, 1, :]),
                       reads=[("ps", pv)], writes=[("v", i)])
                    PS.free(pv)
                W.release(iv, lastv)

                phase[0] = 'attn'
                QTAb = [QTA, QTA2]
                QTBb = [QTB, QTB2]

                def qproj(qc):
                    iq, oq, tq = wget(f"wq{qc}")
                    wq = arena[:, oq:oq + 1024].rearrange("p (k n) -> p k n", k=8)
                    lastq = None
                    qa, qb_ = QTAb[qc % 2], QTBb[qc % 2]
                    for (ob, onb) in own_tiles:
                        on = onb * 128
                        os_ = ob * 128
                        hs = lo * 128 + os_
                        pq = PS.alloc()
                        lastq = mm([(psb[pq][:, 0:on], wq[:, k, :], hT[:, k, hs:hs + on]) for k in range(8)],
                                   reads=khall(lo + ob, onb), writes=[("ps", pq)], extra=[tq])
                        op(DVE, lambda: nc.vector.tensor_scalar(out=qa[0:64, os_:os_ + on], in0=psb[pq][0:64, 0:on],
                                                                scalar1=0.125, scalar2=None, op0=ALU.mult),
                           reads=[("ps", pq)], writes=[("qta", qc % 2)])
                        op(DVE, lambda: nc.vector.tensor_scalar(out=qb_[64:128, os_:os_ + on], in0=psb[pq][64:128, 0:on],
                                                                scalar1=0.125, scalar2=None, op0=ALU.mult),
                           reads=[("ps", pq)], writes=[("qtb", qc % 2)])
                        PS.free(pq)
                    W.release(iq, lastq)

                items = []
                for qc in range(8):
                    for j in range(ne):
                        qlo = max(j - lo - 1, 0)
                        qhi = min(j - lo + 1, nb - 1)
                        if qlo <= qhi:
                            items.append((qc, j, qlo, qhi))
                first_of = {}
                for t_i, it in enumerate(items):
                    first_of.setdefault(it[0], t_i)
                QLEAD = 3
                qdone = set()
                info = {}
                pend = []
                pobanks = {}

                def pv_stage(qc, qb):
                    kvc = qc // 4
                    sl = qb % 4
                    grp = (qc, qb // 4)
                    if sl == 0:
                        pobanks[grp] = [PS.alloc(), PS.alloc()]
                    po = pobanks[grp]
                    for r in range(2):
                        kv = kvc * 2 + r
                        jj_list = [jj for jj in (qb + lo - 1, qb + lo, qb + lo + 1) if (qc, jj) in info]
                        mms = []
                        for jj in jj_list:
                            slot_, qlo_, _ = info[(qc, jj)]
                            co_ = (qb - qlo_) * 128
                            mms.append((psb[po[r]][:, sl * 128:(sl + 1) * 128], Vsb[:, jj, kv, :],
                                        PT[r][slot_][:, co_:co_ + 128]))
                        mm(mms, reads=[("v", jj) for jj in jj_list] + [("pt", r, info[(qc, jj)][0]) for jj in jj_list],
                           writes=[("ps", po[r])])
                    if sl == 3 or qb == nb - 1:
                        nn = (sl + 1) * 128
                        q0 = (qb - sl) * 128
                        es_ = l * 8 + qc
                        op(ACT, lambda: nc.scalar.activation(out=rden[0:64, 0:nn], in_=psb[po[0]][64:128, 0:nn], func=AF.Ln,
                                                             bias=esink[0:64, es_:es_ + 1], scale=1.0),
                           reads=[("ps", po[0]), ("esink", l)], writes=[("rden", 0)])
                        op(ACT, lambda: nc.scalar.activation(out=rden[64:128, 0:nn], in_=psb[po[1]][0:64, 0:nn], func=AF.Ln,
                                                             bias=esink[64:128, es_:es_ + 1], scale=1.0),
                           reads=[("ps", po[1]), ("esink", l)], writes=[("rden", 1)])
                        op(ACT, lambda: nc.scalar.activation(out=rden[:, 0:nn], in_=rden[:, 0:nn], func=AF.Exp, scale=-1.0),
                           reads=[("rden", 0), ("rden", 1)], writes=[("rden", 0), ("rden", 1)])
                        op(DVE, lambda: nc.vector.tensor_tensor(out=BM[0:64, qc, q0:q0 + nn], in0=psb[po[0]][0:64, 0:nn],
                                                                in1=rden[0:64, 0:nn], op=ALU.mult),
                           reads=[("ps", po[0]), ("rden", 0)], writes=kbm(qc, qb - sl, sl + 1))
                        op(DVE, lambda: nc.vector.tensor_tensor(out=BM[64:128, qc, q0:q0 + nn], in0=psb[po[1]][64:128, 0:nn],
                                                                in1=rden[64:128, 0:nn], op=ALU.mult),
                           reads=[("ps", po[1]), ("rden", 1)], writes=kbm(qc, qb - sl, sl + 1))
                        PS.free(po[0])
                        PS.free(po[1])
                        del pobanks[grp]

                qproj(0)
                qdone.add(0)
                for t_i, (qc, j, qlo, qhi) in enumerate(items):
                    nq_ = qc + 1
                    if nq_ < 8 and nq_ not in qdone and t_i >= first_of[nq_] - QLEAD:
                        qproj(nq_)
                        qdone.add(nq_)
                    kvc = qc // 4
                    nq = qhi - qlo + 1
                    dlo = qlo + lo - j
                    slot = t_i % NPT
                    info[(qc, j)] = (slot, qlo, qhi)
                    QT = [QTAb[qc % 2], QTBb[qc % 2]]
                    for r in range(2):
                        pss = PS.alloc()
                        mm([(psb[pss][:, 0:nq * 128], KT[:, kvc, j * 128:(j + 1) * 128],
                             QT[r][:, qlo * 128:(qhi + 1) * 128])],
                           reads=[("kt", kvc, j), ("qta", qc % 2) if r == 0 else ("qtb", qc % 2)], writes=[("ps", pss)])
                        eb_ = Ebuf[r][t_i % 2]
                        op(ACT, lambda: nc.scalar.activation(out=eb_[:, 0:nq * 128], in_=psb[pss][:, 0:nq * 128], func=AF.Exp),
                           reads=[("ps", pss)], writes=[("e", r, t_i % 2)])
                        PS.free(pss)
                        hidx = 2 * qc + r
                        EE, ee = (POOL, nc.gpsimd) if r == 0 else (DVE, nc.vector)
                        op(EE, lambda: ee.tensor_tensor(
                            out=PT[r][slot][:, 0:nq * 128], in0=eb_[:, 0:nq * 128],
                            in1=EB[:, hidx, (dlo + 1) * 128:(dlo + 1 + nq) * 128], op=ALU.mult),
                            reads=[("e", r, t_i % 2), ("eb", hidx)], writes=[("pt", r, slot)])
                    qb = j - lo - 1
                    if 0 <= qb < nb:
                        pend.append((t_i + LAG, qc, qb))
                    while pend and pend[0][0] <= t_i:
                        _, qc_p, qb_p = pend.pop(0)
                        pv_stage(qc_p, qb_p)
                while pend:
                    _, qc_p, qb_p = pend.pop(0)
                    pv_stage(qc_p, qb_p)

                def ymm_attn(n_, woff, py, os_, on, ob, onb):
                    wv2 = arena[:, woff:woff + 2048].rearrange("p (a k n) -> p a k n", a=2, k=8)
                    mms = [(psb[py][:, 0:on], wv2[:, 0, c, :], BM[:, c, os_:os_ + on]) for c in range(8)]
                    return mms, kbmall(range(8), ob, onb), wv2[:, 1]

                phase[0] = 'at'
                branch_out("at", False, ymm_attn)

                phase[0] = 'wo'
                wops = [wget(f"wo{n_}") for n_ in range(8)]
                first_norm = True
                last = None
                for (ob, onb) in own_tiles:
                    on = onb * 128
                    os_ = ob * 128
                    gs = (b0 + ob) * 128
                    phase[0] = 'wo'
                    for n_ in range(8):
                        widx, woff, wtok = wops[n_]
                        wo = arena[:, woff:woff + 1024].rearrange("p (k n) -> p k n", k=8)
                        px = PS.alloc()
                        last = mm([(psb[px][:, 0:on], wo[:, c, :], BM[:, 8 + c, os_:os_ + on]) for c in range(8)],
                                  reads=kbmall(range(8, 16), ob, onb), writes=[("ps", px)], extra=[wtok])
                        op(DVE, lambda: nc.vector.tensor_tensor(out=xT[:, n_, gs:gs + on], in0=xT[:, n_, gs:gs + on],
                                                                in1=psb[px][:, 0:on], op=ALU.add),
                           reads=[("ps", px)] + kx(n_, b0 + ob, onb), writes=kx(n_, b0 + ob, onb))
                        PS.free(px)
                    phase[0] = 'ffn_norm'
                    if first_norm:
                        barrier()
                        first_norm = False
                    rmsnorm(b0 + ob, onb, G_FFN,
                            lambda c, t0, n: hT[:, c, t0 - e0 * 128:t0 - e0 * 128 + n],
                            lambda c, blk: ("h", c, blk - e0))
                for n_ in range(8):
                    W.release(wops[n_][0], last)

                for fg in range(NFG):
                    phase[0] = 'ffn_gu'
                    for f in range(FPG):
                        widx, woff, wtok = wget(f"gu{fg}_{f}")
                        wg = arena[:, woff:woff + 2048].rearrange("p (a k n) -> p a k n", a=2, k=8)
                        last = None
                        for (ob, onb) in own_tiles:
                            on = onb * 128
                            os_ = ob * 128
                            hs = lo * 128 + os_
                            pg = PS.alloc()
                            pu = PS.alloc()
                            mm([(psb[pg][:, 0:on], wg[:, 0, k, :], hT[:, k, hs:hs + on]) for k in range(8)],
                               reads=khall(lo + ob, onb), writes=[("ps", pg)], extra=[wtok])
                            last = mm([(psb[pu][:, 0:on], wg[:, 1, k, :], hT[:, k, hs:hs + on]) for k in range(8)],
                                      reads=khall(lo + ob, onb), writes=[("ps", pu)])
                            f1 = nextf()
                            op(ACT, lambda: nc.scalar.activation(out=fbuf[f1][:, 0:on], in_=psb[pg][:, 0:on], func=AF.Silu),
                               reads=[("ps", pg)], writes=[("f", f1)])
                            PS.free(pg)
                            op(DVE, lambda: nc.vector.tensor_tensor(out=BM[:, f, os_:os_ + on], in0=psb[pu][:, 0:on],
                                                                    in1=fbuf[f1][:, 0:on], op=ALU.mult),
                               reads=[("ps", pu), ("f", f1)], writes=kbm(f, ob, onb))
                            PS.free(pu)
                        W.release(widx, last)
                    if fg == NFG - 1:
                        oi_ = order.index((l, s_i))
                        if oi_ + 1 < len(order):
                            phase0(*order[oi_ + 1])
                    phase[0] = 'ffn_d'
                    for n_ in range(8):
                        widx, woff, wtok = wget(f"wd{fg}_{n_}")
                        wd = arena[:, woff:woff + 1408].rearrange("p (f n) -> p f n", f=FPG)
                        last = None
                        for (ob, onb) in own_tiles:
                            on = onb * 128
                            os_ = ob * 128
                            gs = (b0 + ob) * 128
                            pd = PS.alloc()
                            last = mm([(psb[pd][:, 0:on], wd[:, f, :], BM[:, f, os_:os_ + on]) for f in range(FPG)],
                                      reads=kbmall(range(FPG), ob, onb), writes=[("ps", pd)], extra=[wtok])
                            op(DVE, lambda: nc.vector.tensor_tensor(out=xT[:, n_, gs:gs + on], in0=xT[:, n_, gs:gs + on],
                                                                    in1=psb[pd][:, 0:on], op=ALU.add),
                               reads=[("ps", pd)] + kx(n_, b0 + ob, onb), writes=kx(n_, b0 + ob, onb))
                            PS.free(pd)
                        W.release(widx, last)

        phase[0] = 'final'
        barrier()
        s_o = Src(sem("s_out"))
        ov = out_d.rearrange("(c p) t -> p c t", p=128)
        if do_final:
            goff = n_layers * CST_PER_L
            for gb in range(0, OUTB, 2):
                rmsnorm(gb, 2, goff,
                        lambda c, t0, n: xT[:, c, t0:t0 + n],
                        lambda c, blk: ("x", c, blk))
        for (ob, onb) in split_blocks(OUTB, 4):
            dma(SP, ov[:, :, ob * 128:(ob + onb) * 128], xT[:, :, ob * 128:(ob + onb) * 128], s_o,
                reads=kxall(ob, onb))
        nc.sync.wait_ge(s_o.sem, s_o.cnt)
        for E in (PE, ACT, DVE):
            nc.sync.wait_ge(E.src.sem, E.src.cnt)
    return nc


def _kp(mat):
    n = mat.shape[1]
    return mat.reshape(8, 128, n).transpose(1, 0, 2)


def pack_layer(w_in, w_a_out, w_pool, w_attn_out, w_o, w_gu, w_down):
    Bc, Cc, Xc, Uc, Qc, Kc, Vc, Gac, Gpc, Gtc = 0, 1024, 2048, 3072, 4096, 5120, 5376, 5632, 6656, 7680
    pcs = []

    def add(a):
        pcs.append(np.ascontiguousarray(a, dtype=np.float32).reshape(128, -1))
    for c in range(8):
        add(np.stack([_kp(w_in[:, Cc + c * 128:Cc + (c + 1) * 128]),
                      _kp(w_in[:, Xc + c * 128:Xc + (c + 1) * 128]),
                      _kp(w_in[:, Bc + c * 128:Bc + (c + 1) * 128])], axis=1))
    for n in range(8):
        add(np.stack([_kp(w_a_out[:, n * 128:(n + 1) * 128]),
                      _kp(w_in[:, Gac + n * 128:Gac + (n + 1) * 128])], axis=1))
    add(_kp(w_in[:, Uc:Uc + 512]))
    add(_kp(w_in[:, Uc + 512:Uc + 1024]))
    add(w_pool.reshape(4, 2, 128, 256).transpose(2, 0, 1, 3))
    for n in range(8):
        add(_kp(w_in[:, Gpc + n * 128:Gpc + (n + 1) * 128]))
    add(_kp(w_in[:, Kc:Kc + 256]))
    add(_kp(w_in[:, Vc:Vc + 256]))
    qcols = np.concatenate([np.arange(Qc + h * 64, Qc + (h + 1) * 64) for h in HEAD_ORDER])
    wq = w_in[:, qcols]
    for q in range(8):
        add(_kp(wq[:, q * 128:(q + 1) * 128]))
    orow = np.concatenate([np.arange(h * 64, (h + 1) * 64) for h in HEAD_ORDER])
    wao = w_attn_out[orow, :]
    for n in range(8):
        add(np.stack([_kp(wao[:, n * 128:(n + 1) * 128]),
                      _kp(w_in[:, Gtc + n * 128:Gtc + (n + 1) * 128])], axis=1))
    for n in range(8):
        add(_kp(w_o[:, n * 128:(n + 1) * 128]))
    for fg in range(NFG):
        for f in range(FPG):
            fi = fg * FPG + f
            add(np.stack([_kp(w_gu[:, fi * 128:(fi + 1) * 128]),
                          _kp(w_gu[:, DFF + fi * 128:DFF + (fi + 1) * 128])], axis=1))
        wdg = w_down[fg * FPG * 128:(fg + 1) * FPG * 128, :].reshape(FPG, 128, D).transpose(1, 0, 2)
        for n in range(8):
            add(wdg[:, :, n * 128:(n + 1) * 128])
    out = np.concatenate(pcs, axis=1)
    assert out.shape == (128, EPP), out.shape
    return out


def t5_bucket_np(rel):
    import math
    import jax
    import jax.numpy as jnp
    try:
        dev = jax.devices("cpu")[0]
    except Exception:
        dev = None
    ctx = jax.default_device(dev) if dev is not None else contextlib.nullcontext()
    with ctx:
        rel = jnp.asarray(rel, dtype=jnp.int32)
        n_buckets, max_distance = 32, 128
        half = n_buckets // 2
        max_exact = half // 2
        ret = jnp.where(rel > 0, half, 0)
        n = jnp.abs(rel)
        nf = jnp.maximum(n, 1).astype(jnp.float32)
        large = max_exact + (jnp.log(nf / max_exact) / math.log(max_distance / max_exact)
                             * (half - max_exact)).astype(jnp.int32)
        large = jnp.minimum(large, half - 1)
        return np.asarray(ret + jnp.where(n < max_exact, n, large))


def make_bias_tiles(rel_bias, sgn):
    k = np.arange(128)[:, None]
    qq = np.arange(384)[None, :]
    d = qq // 128 - 1
    q = qq % 128
    rl = k - q - 128 * d
    valid = np.abs(rl) <= 128
    bk = t5_bucket_np(sgn * rl)
    out = np.empty((128, NH, 384), np.float32)
    for hi, h in enumerate(HEAD_ORDER):
        out[:, hi, :] = np.where(valid, rel_bias[bk, h], np.float32(NEGB))
    return out


def make_bands(sgn):
    out = np.zeros((128, 16, 128), np.float32)
    tp = np.arange(128)[:, None]
    t = np.arange(128)[None, :]
    for g, w in enumerate(POOL_W):
        if sgn > 0:
            lo_o, hi_o = -(w // 2), (w - 1 - w // 2)
        else:
            lo_o, hi_o = -(w - 1 - w // 2), (w // 2)
        for d in (-1, 0, 1):
            off = 128 * d + tp - t
            m = ((off >= lo_o) & (off <= hi_o)).astype(np.float32) / np.float32(w)
            if d == 0:
                m = m - (tp == t).astype(np.float32)
            out[:, g * 3 + d + 1, :] = m
        off = tp - t
        inwin = (off >= lo_o) & (off <= hi_o)
        cnt = ((t + hi_o) - np.maximum(t + lo_o, 0) + 1).astype(np.float32)
        m = inwin.astype(np.float32) / cnt - (tp == t).astype(np.float32)
        out[:, 12 + g, :] = m
    return out.reshape(128, 16 * 128)


def make_cst(layers, conv_w, pool_scale, attn_sink, g_mix, g_ffn, g_final, sgn):
    nl = len(layers)
    out = np.zeros((128, nl * CST_PER_L + 8), np.float32)

    def pc(v):
        return v.reshape(8, 128).T
    for i, l in enumerate(layers):
        o = i * CST_PER_L
        out[:, o:o + 8] = pc(g_mix[l])
        out[:, o + 8:o + 16] = pc(g_ffn[l])
        out[:, o + 16:o + 24] = pc(pool_scale[l])
        cw = conv_w[l, :, 0, :]
        if sgn < 0:
            cw = cw[::-1]
        out[:, o + 24:o + 48] = np.stack([pc(cw[k]) for k in range(3)], axis=2).reshape(128, 24)
        for qc in range(8):
            out[0:64, o + 48 + qc] = attn_sink[l, HEAD_ORDER[2 * qc]]
            out[64:128, o + 48 + qc] = attn_sink[l, HEAD_ORDER[2 * qc + 1]]
    out[:, nl * CST_PER_L:] = pc(g_final)
    return out


FUSED = True
LAST_LABELS = {}
DBG_LAYERS = None


def kernel(x, w_in, conv_w, w_a_out, w_pool, pool_scale, w_attn_out, attn_sink, w_o,
           g_mix, g_ffn, w_gu, w_down, rel_bias, g_final):
    f = lambda a: np.asarray(a, dtype=np.float32)
    x, w_in, conv_w, w_a_out, w_pool, pool_scale = map(f, (x, w_in, conv_w, w_a_out, w_pool, pool_scale))
    w_attn_out, attn_sink, w_o, g_mix, g_ffn, w_gu, w_down, rel_bias, g_final = map(
        f, (w_attn_out, attn_sink, w_o, g_mix, g_ffn, w_gu, w_down, rel_bias, g_final))
    wst = [pack_layer(w_in[l], w_a_out[l], w_pool[l], w_attn_out[l], w_o[l], w_gu[l], w_down[l])
           for l in range(DEPTH)]
    bias_t = {s: make_bias_tiles(rel_bias, s) for s in (1, -1)}
    bands = {s: make_bands(s) for s in (1, -1)}
    half = SEQ // 2

    def run(layers, regions, tin_blk, do_final, xin):
        nc = build_program(len(layers), regions, tin_blk, do_final)
        wst_l = np.ascontiguousarray(np.stack([wst[l] for l in layers], axis=0))
        in_maps = []
        for c in range(8):
            sgn = 1 if c % 2 == 0 else -1
            in_maps.append({
                "xT": np.ascontiguousarray(xin[c].T),
                "wst": wst_l,
                "cst": make_cst(layers, conv_w, pool_scale, attn_sink, g_mix, g_ffn, g_final, sgn),
                "biasT": bias_t[sgn],
                "bands": bands[sgn],
            })
        res = run_bass_kernel_spmd(nc, in_maps, core_ids=list(range(8)))
        return [np.asarray(r["outT"]).T for r in res.results]

    def local_slices(xfull, ntok):
        outs = []
        for c in range(8):
            b, hf = c // 2, c % 2
            if hf == 0:
                outs.append(xfull[b, 0:ntok, :])
            else:
                outs.append(xfull[b, ::-1, :][0:ntok, :])
        return outs

    def assemble(parts):
        out = np.empty((BATCH, SEQ, D), np.float32)
        for c in range(8):
            b, hf = c // 2, c % 2
            if hf == 0:
                out[b, 0:half, :] = parts[c]
            else:
                out[b, half:, :] = parts[c][::-1, :]
        return out

    nrun = DEPTH if DBG_LAYERS is None else DBG_LAYERS
    fin = DBG_LAYERS is None
    if FUSED:
        regions = [OWN_BLK + (nrun - 1 - l) for l in range(nrun)]
        tin = regions[0] + 1
        parts = run(list(range(nrun)), regions, tin, fin, local_slices(x, tin * 128))
        return assemble(parts)
    else:
        cur = x
        for l in range(nrun):
            parts = run([l], [OWN_BLK], OWN_BLK + 1, fin and l == nrun - 1, local_slices(cur, (OWN_BLK + 1) * 128))
            cur = assemble(parts)
        return cur
```

```python
import numpy as np
import contextlib
import concourse.bass as bass
import concourse.mybir as mybir
from concourse.bass_utils import run_bass_kernel_spmd

F32 = mybir.dt.float32
BF16 = mybir.dt.bfloat16
AF = mybir.ActivationFunctionType
ALU = mybir.AluOpType

D = 1024
NCH = 8
SEQ = 4096
BATCH = 4
DEPTH = 4
NH = 16
HD = 64
DFF = 2816
NFC = 22
NFG = 2
FPG = 11
EPS = 1e-6
OWN_BLK = 16
MAXST = 6
MAXEXT = 8
LAG = 3
NPT = 7
TT = MAXST * 128
HEAD_ORDER = [0, 4, 1, 5, 2, 6, 3, 7, 8, 12, 9, 13, 10, 14, 11, 15]
POOL_W = (2, 4, 8, 16)
EPP = 163840
ARENA = 12288
NWSEM = 12
NEGB = -30000.0

CST_PER_L = 56


class Src:
    def __init__(self, sem):
        self.sem = sem
        self.cnt = 0


class Eng:
    def __init__(self, eng, sem, name):
        self.eng = eng
        self.src = Src(sem)
        self.name = name
        self.seen = {}

    def wait(self, tok):
        if tok is None:
            return
        src, c = tok
        if self.seen.get(id(src), 0) >= c:
            return
        self.eng.wait_ge(src.sem, c)
        self.seen[id(src)] = c

    def issue(self, inst):
        self.src.cnt += 1
        inst.then_inc(self.src.sem, 1)
        return (self.src, self.src.cnt)


class Tracker:
    def __init__(self):
        self.w = {}
        self.r = {}

    def deps(self, reads, writes):
        toks = []
        for k in reads:
            t = self.w.get(k)
            if t is not None:
                toks.append(t)
        for k in writes:
            t = self.w.get(k)
            if t is not None:
                toks.append(t)
            rr = self.r.get(k)
            if rr:
                toks.extend(rr.values())
        return toks

    def commit(self, tok, reads, writes):
        for k in reads:
            rr = self.r.setdefault(k, {})
            rr[id(tok[0])] = tok
        for k in writes:
            self.w[k] = tok
            self.r[k] = {}


class PsumAlloc:
    def __init__(self, banks):
        self.free_list = list(range(len(banks)))
        self.banks = banks

    def alloc(self):
        assert self.free_list, "out of PSUM banks"
        return self.free_list.pop(0)

    def free(self, b):
        self.free_list.append(b)


class WStream:
    def __init__(self, nc, pool, arena, sems):
        self.nc = nc
        self.pool = pool
        self.arena = arena
        self.sems = [Src(s) for s in sems]
        self.plan = []
        self.ni = 0
        self.ng = 0
        self.head = 0
        self.regions = []
        self.rel = {}
        self.info = {}

    def add(self, name, dram_ap, n):
        self.plan.append((name, dram_ap, n))

    def pump(self):
        while self.ni < len(self.plan):
            name, dap, n = self.plan[self.ni]
            idx = self.ni
            off = self.head
            if off + n > ARENA:
                off = 0
            conflicts = [rg for rg in self.regions if rg[0] < off + n and off < rg[0] + rg[1]]
            if any(rg[2] not in self.rel for rg in conflicts):
                return
            old = idx - NWSEM
            if old >= 0 and old not in self.rel:
                return
            for rg in conflicts:
                self.pool.wait(self.rel[rg[2]])
                self.regions.remove(rg)
            if old >= 0:
                self.pool.wait(self.rel[old])
            s = self.sems[idx % NWSEM]
            s.cnt += 16
            self.nc.gpsimd.dma_start(out=self.arena[:, off:off + n], in_=dap).then_inc(s.sem, 16)
            self.info[idx] = (off, n, (s, s.cnt))
            self.regions.append((off, n, idx))
            self.head = off + n
            self.ni += 1

    def get(self, name):
        self.pump()
        pname, _, n = self.plan[self.ng]
        assert pname == name, (pname, name)
        assert self.ng in self.info, f"weight arena deadlock at {name}"
        off, n, tok = self.info[self.ng]
        idx = self.ng
        self.ng += 1
        return idx, off, tok

    def release(self, idx, tok):
        self.rel[idx] = tok
        self.pump()


def split_blocks(n, maxpart):
    parts = -(-n // maxpart)
    base = n // parts
    rem = n % parts
    out = []
    s = 0
    for i in range(parts):
        m = base + (1 if i < rem else 0)
        out.append((s, m))
        s += m
    return out


def build_program(n_layers, regions, tin_blk, do_final):
    nc = bass.Bass("TRN2", target_bir_lowering=False)
    TIN = tin_blk * 128
    OUTB = regions[-1]
    ncst = n_layers * CST_PER_L + 8
    xT_d = nc.dram_tensor("xT", [D, TIN], F32, kind="ExternalInput").ap()
    wst_d = nc.dram_tensor("wst", [n_layers, 128, EPP], F32, kind="ExternalInput").ap()
    cst_d = nc.dram_tensor("cst", [128, ncst], F32, kind="ExternalInput").ap()
    bias_d = nc.dram_tensor("biasT", [128, NH, 384], F32, kind="ExternalInput").ap()
    band_d = nc.dram_tensor("bands", [128, 16 * 128], F32, kind="ExternalInput").ap()
    out_d = nc.dram_tensor("outT", [D, OUTB * 128], F32, kind="ExternalOutput").ap()

    es = contextlib.ExitStack()
    with es:
        def sb(name, shape, dt):
            return es.enter_context(nc.sbuf_tensor(name, shape, dt))

        def sem(name):
            return es.enter_context(nc.semaphore(name))

        xT = sb("xT_sb", [128, NCH, TIN], F32)
        hT = sb("hT", [128, NCH, MAXEXT * 128], BF16)
        stash = sb("stash", [128, NCH, 128], BF16)
        BM = sb("BM", [128, 16, TT], BF16)
        arena = sb("arena", [128, ARENA], BF16)
        cst = sb("cst_sb", [128, ncst], F32)
        esink = sb("esink", [128, n_layers * 8], F32)
        EB = sb("EB", [128, NH, 384], BF16)
        bands = sb("bands_sb", [128, 16, 128], BF16)
        onesM = sb("onesM", [128, 128], BF16)
        fbuf = [sb(f"fbuf{i}", [128, 512], F32) for i in range(4)]
        SCRN = 17152
        scr = sb("scr", [128, SCRN], BF16)
        sq = scr[:, 0:NCH * 512].rearrange("p (a b) -> p a b", a=NCH)
        ubuf = scr[:, 0:2 * (TT + 2)].bitcast(F32)
        ybuf = [scr[:, 2048 + i * 1024:2048 + (i + 1) * 1024].bitcast(F32) for i in range(2)]
        Usb = scr[:, 0:5120].rearrange("p (a b) -> p a b", a=5)
        _o = 0
        KT = scr[:, _o:_o + 2 * MAXEXT * 128].rearrange("p (a b) -> p a b", a=2); _o += 2 * MAXEXT * 128
        Vsb = scr[:, _o:_o + MAXEXT * 512].rearrange("p (a b c) -> p a b c", a=MAXEXT, b=4); _o += MAXEXT * 512
        QTA = scr[:, _o:_o + TT]; _o += TT
        QTB = scr[:, _o:_o + TT]; _o += TT
        QTA2 = scr[:, _o:_o + TT]; _o += TT
        QTB2 = scr[:, _o:_o + TT]; _o += TT
        Ebuf = [[None, None], [None, None]]
        for r in range(2):
            for i in range(2):
                Ebuf[r][i] = scr[:, _o:_o + 384]; _o += 384
        PT = [[None] * NPT, [None] * NPT]
        for r in range(2):
            for i in range(NPT):
                PT[r][i] = scr[:, _o:_o + 384]; _o += 384
        rden = scr[:, _o:_o + 1024].bitcast(F32); _o += 1024
        assert _o <= SCRN, _o
        psb = [es.enter_context(nc.psum_tensor(f"ps{i}", [128, 512], F32)) for i in range(8)]

        PE = Eng(nc.tensor, sem("s_pe"), "pe")
        ACT = Eng(nc.scalar, sem("s_act"), "act")
        DVE = Eng(nc.vector, sem("s_dve"), "dve")
        POOL = Eng(nc.gpsimd, sem("s_pool"), "pool")
        SP = Eng(nc.sync, sem("s_sp"), "sp")
        T = Tracker()
        PS = PsumAlloc(psb)
        W = WStream(nc, POOL, arena, [sem(f"s_w{i}") for i in range(NWSEM)])

        phase = ["init"]
        labels = {"pe": [], "act": [], "dve": [], "pool": [], "sp": []}
        LAST_LABELS.clear()
        LAST_LABELS.update(labels)

        def op(E, fn, reads=(), writes=(), extra=()):
            labels[E.name].append(phase[0])
            for t in T.deps(reads, writes):
                E.wait(t)
            for t in extra:
                E.wait(t)
            tok = E.issue(fn())
            T.commit(tok, reads, writes)
            return tok

        def mm(mms, reads=(), writes=(), extra=()):
            for t in T.deps(reads, writes):
                if t[0] is PE.src:
                    continue
                PE.wait(t)
            for t in extra:
                PE.wait(t)
            n = len(mms)
            inst = None
            for i, (o, l, r) in enumerate(mms):
                labels["pe"].append((phase[0], r.shape[-1]))
                inst = nc.tensor.matmul(o, l, r, start=(i == 0), stop=(i == n - 1))
            tok = PE.issue(inst)
            T.commit(tok, reads, writes)
            return tok

        def dma(E, out, in_, s, reads=(), writes=()):
            for t in T.deps(reads, writes):
                E.wait(t)
            s.cnt += 16
            E.eng.dma_start(out=out, in_=in_).then_inc(s.sem, 16)
            tok = (s, s.cnt)
            T.commit(tok, reads, writes)
            return tok

        def barrier():
            toks = [(E_.src, E_.src.cnt) for E_ in (PE, ACT, DVE, POOL)]
            for E_ in (ACT, DVE, POOL):
                for t in toks:
                    if t[0] is not E_.src and t[1] > 0:
                        E_.wait(t)

        fb_i = [0]

        def nextf():
            fb_i[0] = (fb_i[0] + 1) % len(fbuf)
            return fb_i[0]

        def kx(c, b0, nb):
            return [("x", c, b) for b in range(b0, b0 + nb)]

        def kxall(b0, nb):
            return [("x", c, b) for c in range(NCH) for b in range(b0, b0 + nb)]

        def kh(c, b0, nb):
            return [("h", c, b) for b in range(b0, b0 + nb)]

        def khall(b0, nb):
            return [("h", c, b) for c in range(NCH) for b in range(b0, b0 + nb)]

        def kbm(slot, b0, nb):
            return [("bm", slot, b) for b in range(b0, b0 + nb)]

        def kbmall(slots, b0, nb):
            return [("bm", s_, b) for s_ in slots for b in range(b0, b0 + nb)]

        s_c = Src(sem("s_cst"))
        dma(SP, cst[:, :], cst_d[:, :], s_c, writes=[("cst",)])
        xtiles = split_blocks(tin_blk, 4)
        s_x = [Src(sem(f"s_x{i}")) for i in range(len(xtiles))]
        xv = xT_d.rearrange("(c p) t -> p c t", p=128)
        for i, (b0, nb) in enumerate(xtiles):
            dma(SP, xT[:, :, b0 * 128:(b0 + nb) * 128], xv[:, :, b0 * 128:(b0 + nb) * 128],
                s_x[i], writes=kxall(b0, nb))
        s_b = Src(sem("s_band"))
        s_b.cnt += 16
        nc.gpsimd.dma_start(out=bands[:, :, :], in_=band_d.rearrange("p (a b) -> p a b", b=128)).then_inc(s_b.sem, 16)
        T.commit((s_b, s_b.cnt), [], [("bands",)])
        op(DVE, lambda: nc.vector.memset(onesM[:, :], 1.0 / D), writes=[("ones",)])
        s_bi = [Src(sem(f"s_bias{i}")) for i in range(2)]
        for h in range(NH):
            fi = h % 2
            dma(SP, fbuf[fi][:, 0:384], bias_d[:, h, :], s_bi[fi], writes=[("f", fi)])
            op(ACT, lambda: nc.scalar.activation(out=EB[:, h, :], in_=fbuf[fi][:, 0:384], func=AF.Exp),
               reads=[("f", fi)], writes=[("eb", h)])
        for l in range(n_layers):
            o = l * CST_PER_L + 48
            op(ACT, lambda: nc.scalar.activation(out=esink[:, l * 8:(l + 1) * 8], in_=cst[:, o:o + 8], func=AF.Exp),
               reads=[("cst",)], writes=[("esink", l)])

        for l in range(n_layers):
            nst = len(split_blocks(regions[l], MAXST))
            for s_i in range(nst):
                off = 0

                def addp(name, n):
                    nonlocal off
                    W.add((l, s_i, name), wst_d[l, :, off:off + n], n)
                    off += n
                for c in range(8):
                    addp(f"cv{c}", 3072)
                for n_ in range(8):
                    addp(f"ao{n_}", 2048)
                addp("wu0", 4096)
                addp("wu1", 4096)
                addp("wp", 2048)
                for n_ in range(8):
                    addp(f"gp{n_}", 1024)
                addp("wk", 2048)
                addp("wv", 2048)
                for q in range(8):
                    addp(f"wq{q}", 1024)
                for n_ in range(8):
                    addp(f"at{n_}", 2048)
                for n_ in range(8):
                    addp(f"wo{n_}", 1024)
                for fg in range(NFG):
                    for f in range(FPG):
                        addp(f"gu{fg}_{f}", 2048)
                    for n_ in range(8):
                        addp(f"wd{fg}_{n_}", 1408)
                assert off == EPP

        def rmsnorm(gb0, nblk, goff, dst_fn, dst_keys_fn):
            rms_part1(gb0, nblk)
            rms_part2(gb0, nblk, goff, dst_fn, dst_keys_fn)

        def rms_part1(gb0, nblk):
            t0 = gb0 * 128
            n = nblk * 128
            op(ACT, lambda: nc.scalar.activation(out=sq[:, :, 0:n], in_=xT[:, :, t0:t0 + n], func=AF.Square),
               reads=kxall(gb0, nblk), writes=[("sq",)])

        def rms_part2(gb0, nblk, goff, dst_fn, dst_keys_fn):
            t0 = gb0 * 128
            n = nblk * 128
            b = gb0
            pb = PS.alloc()
            mm([(psb[pb][:, 0:n], onesM[:, :], sq[:, c, 0:n]) for c in range(NCH)],
               reads=[("sq",), ("ones",)], writes=[("ps", pb)])
            f1 = nextf()
            op(ACT, lambda: nc.scalar.activation(out=fbuf[f1][:, 0:n], in_=psb[pb][:, 0:n], func=AF.Ln,
                                                 bias=EPS, scale=1.0),
               reads=[("ps", pb)], writes=[("f", f1)])
            PS.free(pb)
            op(ACT, lambda: nc.scalar.activation(out=fbuf[f1][:, 0:n], in_=fbuf[f1][:, 0:n], func=AF.Exp,
                                                 scale=-0.5),
               reads=[("f", f1)], writes=[("f", f1)])
            for c in range(NCH):
                op(DVE, lambda: nc.vector.scalar_tensor_tensor(
                    out=dst_fn(c, t0, n), in0=xT[:, c, t0:t0 + n], scalar=cst[:, goff + c:goff + c + 1],
                    in1=fbuf[f1][:, 0:n], op0=ALU.mult, op1=ALU.mult),
                    reads=kx(c, b, nblk) + [("f", f1), ("cst",)],
                    writes=[dst_keys_fn(c, bb) for bb in range(b, b + nblk)])

        def st_geom(l, s_i):
            sts_ = split_blocks(regions[l], MAXST)
            b0_, nb_ = sts_[s_i]
            e0_ = b0_ - 1 if s_i > 0 else 0
            return sts_, b0_, nb_, e0_, b0_ + nb_ + 1

        def phase0_stages(l, s_i):
            sts_, b0_, nb_, e0_, e1_ = st_geom(l, s_i)
            b1_ = b0_ + nb_
            nstart = b0_ if s_i > 0 else 0
            tiles_ = split_blocks(e1_ - nstart, 3)
            dst_fn = lambda c, t0, n: hT[:, c, t0 - e0_ * 128:t0 - e0_ * 128 + n]
            dkey = lambda c, blk: ("h", c, blk - e0_)
            stages = []

            def mk(k):
                def st():
                    prev = phase[0]
                    phase[0] = 'norm'
                    if k == 0:
                        barrier()
                        if s_i > 0:
                            op(DVE, lambda: nc.vector.tensor_copy(out=hT[:, :, 0:128], in_=stash[:, :, :]),
                               reads=[("stash",)], writes=khall(0, 1))
                    if k > 0:
                        tb, tnb = tiles_[k - 1]
                        rms_part2(nstart + tb, tnb, l * CST_PER_L, dst_fn, dkey)
                    if k < len(tiles_):
                        tb, tnb = tiles_[k]
                        rms_part1(nstart + tb, tnb)
                    if k == len(tiles_) and s_i + 1 < len(sts_):
                        sl_ = (b1_ - 1 - e0_) * 128
                        op(DVE, lambda: nc.vector.tensor_copy(out=stash[:, :, :], in_=hT[:, :, sl_:sl_ + 128]),
                           reads=khall(b1_ - 1 - e0_, 1), writes=[("stash",)])
                    phase[0] = prev
                return st
            for k in range(len(tiles_) + 1):
                stages.append(mk(k))
            return stages

        def phase0(l, s_i):
            for st in phase0_stages(l, s_i):
                st()

        order = [(l_, s_) for l_ in range(n_layers) for s_ in range(len(split_blocks(regions[l_], MAXST)))]
        phase0(*order[0])

        for l in range(n_layers):
            R = regions[l]
            co = l * CST_PER_L
            G_MIX, G_FFN, PSC, CVW, SNK = co, co + 8, co + 16, co + 24, co + 48
            sts = split_blocks(R, MAXST)
            for s_i, (b0, nb) in enumerate(sts):
                b1 = b0 + nb
                e0 = b0 - 1 if s_i > 0 else 0
                e1 = b1 + 1
                ne = e1 - e0
                lo = b0 - e0
                n_own = nb * 128
                own_tiles = split_blocks(nb, 4)
                ext_tiles = split_blocks(ne, 4)

                def wget(name):
                    idx, off, tok = W.get((l, s_i, name))
                    return idx, off, tok

                phase[0] = 'conv'
                barrier()
                base_u = lo * 128 - 1
                cs = max(base_u, 0)
                ce = lo * 128 + n_own + 1
                ctl = []
                nct = -(-(ce - cs) // 512)
                step = -(-(ce - cs) // nct)
                t_ = cs
                while t_ < ce:
                    ctl.append((t_, min(step, ce - t_)))
                    t_ += step
                for c in range(8):
                    widx, woff, wtok = wget(f"cv{c}")
                    wv = arena[:, woff:woff + 3072].rearrange("p (a k n) -> p a k n", a=3, k=8)
                    if s_i == 0:
                        op(DVE, lambda: nc.vector.memset(ubuf[:, 0:1], 0.0), writes=[("u",)])
                    for (ts, tn) in ctl:
                        hb0 = ts // 128
                        hnb = (ts + tn - 1) // 128 - hb0 + 1
                        pc = PS.alloc()
                        px = PS.alloc()
                        mm([(psb[pc][:, 0:tn], wv[:, 0, k, :], hT[:, k, ts:ts + tn]) for k in range(8)],
                           reads=khall(hb0, hnb), writes=[("ps", pc)], extra=[wtok])
                        mm([(psb[px][:, 0:tn], wv[:, 1, k, :], hT[:, k, ts:ts + tn]) for k in range(8)],
                           reads=khall(hb0, hnb), writes=[("ps", px)])
                        f1 = nextf()
                        op(ACT, lambda: nc.scalar.activation(out=fbuf[f1][:, 0:tn], in_=psb[pc][:, 0:tn], func=AF.Copy),
                           reads=[("ps", pc)], writes=[("f", f1)])
                        PS.free(pc)
                        ui = ts - base_u
                        op(DVE, lambda: nc.vector.tensor_tensor(out=ubuf[:, ui:ui + tn], in0=psb[px][:, 0:tn],
                                                                in1=fbuf[f1][:, 0:tn], op=ALU.mult),
                           reads=[("ps", px), ("f", f1)], writes=[("u",)])
                        PS.free(px)
                    for ti_, (ob, onb) in enumerate(own_tiles):
                        on = onb * 128
                        os_ = ob * 128
                        pbk = PS.alloc()
                        hs = lo * 128 + os_
                        tokb = mm([(psb[pbk][:, 0:on], wv[:, 2, k, :], hT[:, k, hs:hs + on]) for k in range(8)],
                                  reads=khall(lo + ob, onb), writes=[("ps", pbk)])
                        yi = ti_ % 2
                        yb = ybuf[yi]
                        cw = CVW + c * 3
                        op(DVE, lambda: nc.vector.tensor_scalar(out=yb[:, 0:on], in0=ubuf[:, os_ + 1:os_ + 1 + on],
                                                                scalar1=cst[:, cw + 1:cw + 2], scalar2=None, op0=ALU.mult),
                           reads=[("u",), ("cst",)], writes=[("y", yi)])
                        op(DVE, lambda: nc.vector.scalar_tensor_tensor(out=yb[:, 0:on], in0=ubuf[:, os_:os_ + on],
                                                                       scalar=cst[:, cw:cw + 1], in1=yb[:, 0:on],
                                                                       op0=ALU.mult, op1=ALU.add),
                           reads=[("u",), ("y", yi)], writes=[("y", yi)])
                        op(DVE, lambda: nc.vector.scalar_tensor_tensor(out=yb[:, 0:on], in0=ubuf[:, os_ + 2:os_ + 2 + on],
                                                                       scalar=cst[:, cw + 2:cw + 3], in1=yb[:, 0:on],
                                                                       op0=ALU.mult, op1=ALU.add),
                           reads=[("u",), ("y", yi)], writes=[("y", yi)])
                        op(DVE, lambda: nc.vector.tensor_tensor(out=BM[:, c, os_:os_ + on], in0=psb[pbk][:, 0:on],
                                                                in1=yb[:, 0:on], op=ALU.mult),
                           reads=[("ps", pbk), ("y", yi)], writes=kbm(c, ob, onb))
                        PS.free(pbk)
                    W.release(widx, tokb)

                def branch_out(prefix, first, ymm_fn, post_scale=None):
                    for n_ in range(8):
                        widx, woff, wtok = wget(f"{prefix}{n_}")
                        last = None
                        for (ob, onb) in own_tiles:
                            on = onb * 128
                            os_ = ob * 128
                            hs = lo * 128 + os_
                            py = PS.alloc()
                            pg = PS.alloc()
                            mms, rk, gw = ymm_fn(n_, woff, py, os_, on, ob, onb)
                            mm(mms, reads=rk, writes=[("ps", py)], extra=[wtok])
                            last = mm([(psb[pg][:, 0:on], gw[:, k, :], hT[:, k, hs:hs + on]) for k in range(8)],
                                      reads=khall(lo + ob, onb), writes=[("ps", pg)])
                            f1 = nextf()
                            op(ACT, lambda: nc.scalar.activation(out=fbuf[f1][:, 0:on], in_=psb[pg][:, 0:on], func=AF.Sigmoid),
                               reads=[("ps", pg)], writes=[("f", f1)])
                            PS.free(pg)
                            if first:
                                op(DVE, lambda: nc.vector.tensor_tensor(out=BM[:, 8 + n_, os_:os_ + on], in0=psb[py][:, 0:on],
                                                                        in1=fbuf[f1][:, 0:on], op=ALU.mult),
                                   reads=[("ps", py), ("f", f1)], writes=kbm(8 + n_, ob, onb))
                            else:
                                if post_scale is None:
                                    op(DVE, lambda: nc.vector.tensor_tensor(out=fbuf[f1][:, 0:on], in0=psb[py][:, 0:on],
                                                                            in1=fbuf[f1][:, 0:on], op=ALU.mult),
                                       reads=[("ps", py), ("f", f1)], writes=[("f", f1)])
                                else:
                                    sc = post_scale + n_
                                    op(DVE, lambda: nc.vector.scalar_tensor_tensor(
                                        out=fbuf[f1][:, 0:on], in0=psb[py][:, 0:on], scalar=cst[:, sc:sc + 1],
                                        in1=fbuf[f1][:, 0:on], op0=ALU.mult, op1=ALU.mult),
                                        reads=[("ps", py), ("f", f1), ("cst",)], writes=[("f", f1)])
                                op(DVE, lambda: nc.vector.tensor_tensor(out=BM[:, 8 + n_, os_:os_ + on], in0=BM[:, 8 + n_, os_:os_ + on],
                                                                        in1=fbuf[f1][:, 0:on], op=ALU.add),
                                   reads=kbm(8 + n_, ob, onb) + [("f", f1)], writes=kbm(8 + n_, ob, onb))
                            PS.free(py)
                        W.release(widx, last)

                def ymm_full(n_, woff, py, os_, on, ob, onb):
                    wv2 = arena[:, woff:woff + 2048].rearrange("p (a k n) -> p a k n", a=2, k=8)
                    mms = [(psb[py][:, 0:on], wv2[:, 0, c, :], BM[:, c, os_:os_ + on]) for c in range(8)]
                    return mms, kbmall(range(8), ob, onb), wv2[:, 1]

                phase[0] = 'ao'
                branch_out("ao", True, ymm_full)

                phase[0] = 'poolU'
                barrier()
                i0, o0, t0_ = wget("wu0")
                i1, o1, t1_ = wget("wu1")
                ip, op_, tp_ = wget("wp")
                wu = [arena[:, o0:o0 + 4096].rearrange("p (k n) -> p k n", k=8),
                      arena[:, o1:o1 + 4096].rearrange("p (k n) -> p k n", k=8)]
                wp = arena[:, op_:op_ + 2048].rearrange("p (g k n) -> p g k n", g=4, k=2)
                lastu = None
                for i in range(ne + 1):
                    if i < ne:
                        slot = i % 5
                        for hf in range(2):
                            pu = PS.alloc()
                            lastu = mm([(psb[pu][:, :], hT[:, k, i * 128:(i + 1) * 128], wu[hf][:, k, :]) for k in range(8)],
                                       reads=khall(i, 1), writes=[("ps", pu)], extra=[t0_, t1_])
                            if hf == 0:
                                op(ACT, lambda: nc.scalar.activation(out=Usb[:, slot, 0:512], in_=psb[pu][:, :], func=AF.Copy),
                                   reads=[("ps", pu)], writes=[("usb", slot, 0)])
                            else:
                                op(DVE, lambda: nc.vector.tensor_copy(out=Usb[:, slot, 512:1024], in_=psb[pu][:, :]),
                                   reads=[("ps", pu)], writes=[("usb", slot, 1)])
                            PS.free(pu)
                    j = i - 2
                    if j >= lo and j < lo + nb:
                        gj = e0 + j
                        srcs = [d for d in (-1, 0, 1) if 0 <= j + d < ne]
                        for half in range(2):
                            pp = PS.alloc()
                            for cc in range(4):
                                c = half * 4 + cc
                                g = c // 2
                                mms = []
                                for d in srcs:
                                    if gj == 0:
                                        bnd = bands[:, 12 + g, :] if d == 0 else bands[:, g * 3 + 2, :]
                                    else:
                                        bnd = bands[:, g * 3 + (d + 1), :]
                                    mms.append((psb[pp][:, cc * 128:(cc + 1) * 128],
                                                Usb[:, (j + d) % 5, c * 128:(c + 1) * 128], bnd))
                                mm(mms, reads=[("usb", (j + d) % 5, c // 4) for d in srcs] + [("bands",)],
                                   writes=[("ps", pp)])
                            ob_ = j - lo
                            if half == 0:
                                op(ACT, lambda: nc.scalar.activation(
                                    out=BM[:, 0:4, ob_ * 128:(ob_ + 1) * 128],
                                    in_=psb[pp][:, :].rearrange("p (a b) -> p a b", a=4), func=AF.Copy),
                                    reads=[("ps", pp)], writes=kbmall(range(0, 4), ob_, 1))
                            else:
                                op(DVE, lambda: nc.vector.tensor_copy(
                                    out=BM[:, 4:8, ob_ * 128:(ob_ + 1) * 128],
                                    in_=psb[pp][:, :].rearrange("p (a b) -> p a b", a=4)),
                                    reads=[("ps", pp)], writes=kbmall(range(4, 8), ob_, 1))
                            PS.free(pp)
                W.release(i0, lastu)
                W.release(i1, lastu)

                def ymm_pool(n_, woff, py, os_, on, ob, onb):
                    g = n_ // 2
                    gw = arena[:, woff:woff + 1024].rearrange("p (k n) -> p k n", k=8)
                    mms = [(psb[py][:, 0:on], wp[:, g, kk, (n_ % 2) * 128:(n_ % 2) * 128 + 128],
                            BM[:, 2 * g + kk, os_:os_ + on]) for kk in range(2)]
                    return mms, kbmall([2 * g, 2 * g + 1], ob, onb), gw

                phase[0] = 'poolY'
                PE.wait(tp_)
                branch_out("gp", False, ymm_pool, post_scale=PSC)
                W.release(ip, (PE.src, PE.src.cnt))

                phase[0] = 'kv'
                barrier()
                op(DVE, lambda: nc.vector.memset(Vsb[:, :, :, :], 1.0), writes=[("v", i) for i in range(MAXEXT)])
                op(DVE, lambda: nc.vector.memset(QTA[:, :], 0.0), writes=[("qta", 0)])
                op(DVE, lambda: nc.vector.memset(QTB[:, :], 0.0), writes=[("qtb", 0)])
                op(DVE, lambda: nc.vector.memset(QTA2[:, :], 0.0), writes=[("qta", 1)])
                op(DVE, lambda: nc.vector.memset(QTB2[:, :], 0.0), writes=[("qtb", 1)])
                ik, ok_, tk_ = wget("wk")
                iv, ov_, tv_ = wget("wv")
                wk = arena[:, ok_:ok_ + 2048].rearrange("p (k n) -> p k n", k=8)
                wvv = arena[:, ov_:ov_ + 2048].rearrange("p (k n) -> p k n", k=8)
                lastk = None
                for kc in range(2):
                    for (eb, enb) in ext_tiles:
                        en = enb * 128
                        pk = PS.alloc()
                        lastk = mm([(psb[pk][:, 0:en], wk[:, k, kc * 128:(kc + 1) * 128], hT[:, k, eb * 128:eb * 128 + en])
                                    for k in range(8)], reads=khall(eb, enb), writes=[("ps", pk)], extra=[tk_])
                        op(ACT, lambda: nc.scalar.activation(out=KT[:, kc, eb * 128:eb * 128 + en], in_=psb[pk][:, 0:en], func=AF.Copy),
                           reads=[("ps", pk)], writes=[("kt", kc, b) for b in range(eb, eb + enb)])
                        PS.free(pk)
                W.release(ik, lastk)
                lastv = None
                for i in range(ne):
                    pv = PS.alloc()
                    lastv = mm([(psb[pv][:, 0:256], hT[:, k, i * 128:(i + 1) * 128], wvv[:, k, :]) for k in range(8)],
                               reads=khall(i, 1), writes=[("ps", pv)], extra=[tv_])
                    pvv = psb[pv][:, 0:256].rearrange("p (a b c) -> p a b c", a=2, b=2)
                    op(DVE, lambda: nc.vector.tensor_copy(out=Vsb[:, i, 0::2, 0:64], in_=pvv[:, :, 0, :]),
                       reads=[("ps", pv)], writes=[("v", i)])
                    op(DVE, lambda: nc.vector.tensor_copy(out=Vsb[:, i, 1::2, 64:128], in_=pvv[:, :, 1, :]),
                       reads=[("ps", pv)], writes=[("v", i)])
                    PS.free(pv)
                W.release(iv, lastv)

                phase[0] = 'attn'
                QTAb = [QTA, QTA2]
                QTBb = [QTB, QTB2]

                def qproj(qc):
                    iq, oq, tq = wget(f"wq{qc}")
                    wq = arena[:, oq:oq + 1024].rearrange("p (k n) -> p k n", k=8)
                    lastq = None
                    qa, qb_ = QTAb[qc % 2], QTBb[qc % 2]
                    for (ob, onb) in own_tiles:
                        on = onb * 128
                        os_ = ob * 128
                        hs = lo * 128 + os_
                        pq = PS.alloc()
                        lastq = mm([(psb[pq][:, 0:on], wq[:, k, :], hT[:, k, hs:hs + on]) for k in range(8)],
                                   reads=khall(lo + ob, onb), writes=[("ps", pq)], extra=[tq])
                        op(DVE, lambda: nc.vector.tensor_scalar(out=qa[0:64, os_:os_ + on], in0=psb[pq][0:64, 0:on],
                                                                scalar1=0.125, scalar2=None, op0=ALU.mult),
                           reads=[("ps", pq)], writes=[("qta", qc % 2)])
                        op(DVE, lambda: nc.vector.tensor_scalar(out=qb_[64:128, os_:os_ + on], in0=psb[pq][64:128, 0:on],
                                                                scalar1=0.125, scalar2=None, op0=ALU.mult),
                           reads=[("ps", pq)], writes=[("qtb", qc % 2)])
                        PS.free(pq)
                    W.release(iq, lastq)

                items = []
                for qc in range(8):
                    for j in range(ne):
                        qlo = max(j - lo - 1, 0)
                        qhi = min(j - lo + 1, nb - 1)
                        if qlo <= qhi:
                            items.append((qc, j, qlo, qhi))
                first_of = {}
                for t_i, it in enumerate(items):
                    first_of.setdefault(it[0], t_i)
                QLEAD = 3
                qdone = set()
                info = {}
                pend = []
                pobanks = {}

                def pv_stage(qc, qb):
                    kvc = qc // 4
                    sl = qb % 4
                    grp = (qc, qb // 4)
                    if sl == 0:
                        pobanks[grp] = [PS.alloc(), PS.alloc()]
                    po = pobanks[grp]
                    for r in range(2):
                        kv = kvc * 2 + r
                        jj_list = [jj for jj in (qb + lo - 1, qb + lo, qb + lo + 1) if (qc, jj) in info]
                        mms = []
                        for jj in jj_list:
                            slot_, qlo_, _ = info[(qc, jj)]
                            co_ = (qb - qlo_) * 128
                            mms.append((psb[po[r]][:, sl * 128:(sl + 1) * 128], Vsb[:, jj, kv, :],
                                        PT[r][slot_][:, co_:co_ + 128]))
                        mm(mms, reads=[("v", jj) for jj in jj_list] + [("pt", r, info[(qc, jj)][0]) for jj in jj_list],
                           writes=[("ps", po[r])])
                    if sl == 3 or qb == nb - 1:
                        nn = (sl + 1) * 128
                        q0 = (qb - sl) * 128
                        es_ = l * 8 + qc
                        op(ACT, lambda: nc.scalar.activation(out=rden[0:64, 0:nn], in_=psb[po[0]][64:128, 0:nn], func=AF.Ln,
                                                             bias=esink[0:64, es_:es_ + 1], scale=1.0),
                           reads=[("ps", po[0]), ("esink", l)], writes=[("rden", 0)])
                        op(ACT, lambda: nc.scalar.activation(out=rden[64:128, 0:nn], in_=psb[po[1]][0:64, 0:nn], func=AF.Ln,
                                                             bias=esink[64:128, es_:es_ + 1], scale=1.0),
                           reads=[("ps", po[1]), ("esink", l)], writes=[("rden", 1)])
                        op(ACT, lambda: nc.scalar.activation(out=rden[:, 0:nn], in_=rden[:, 0:nn], func=AF.Exp, scale=-1.0),
                           reads=[("rden", 0), ("rden", 1)], writes=[("rden", 0), ("rden", 1)])
                        op(DVE, lambda: nc.vector.tensor_tensor(out=BM[0:64, qc, q0:q0 + nn], in0=psb[po[0]][0:64, 0:nn],
                                                                in1=rden[0:64, 0:nn], op=ALU.mult),
                           reads=[("ps", po[0]), ("rden", 0)], writes=kbm(qc, qb - sl, sl + 1))
                        op(DVE, lambda: nc.vector.tensor_tensor(out=BM[64:128, qc, q0:q0 + nn], in0=psb[po[1]][64:128, 0:nn],
                                                                in1=rden[64:128, 0:nn], op=ALU.mult),
                           reads=[("ps", po[1]), ("rden", 1)], writes=kbm(qc, qb - sl, sl + 1))
                        PS.free(po[0])
                        PS.free(po[1])
                        del pobanks[grp]

                qproj(0)
                qdone.add(0)
                for t_i, (qc, j, qlo, qhi) in enumerate(items):
                    nq_ = qc + 1
                    if nq_ < 8 and nq_ not in qdone and t_i >= first_of[nq_] - QLEAD:
                        qproj(nq_)
                        qdone.add(nq_)
                    kvc = qc // 4
                    nq = qhi - qlo + 1
                    dlo = qlo + lo - j
                    slot = t_i % NPT
                    info[(qc, j)] = (slot, qlo, qhi)
                    QT = [QTAb[qc % 2], QTBb[qc % 2]]
                    for r in range(2):
                        pss = PS.alloc()
                        mm([(psb[pss][:, 0:nq * 128], KT[:, kvc, j * 128:(j + 1) * 128],
                             QT[r][:, qlo * 128:(qhi + 1) * 128])],
                           reads=[("kt", kvc, j), ("qta", qc % 2) if r == 0 else ("qtb", qc % 2)], writes=[("ps", pss)])
                        eb_ = Ebuf[r][t_i % 2]
                        op(ACT, lambda: nc.scalar.activation(out=eb_[:, 0:nq * 128], in_=psb[pss][:, 0:nq * 128], func=AF.Exp),
                           reads=[("ps", pss)], writes=[("e", r, t_i % 2)])
                        PS.free(pss)
                        hidx = 2 * qc + r
                        EE, ee = (POOL, nc.gpsimd) if r == 0 else (DVE, nc.vector)
                        op(EE, lambda: ee.tensor_tensor(
                            out=PT[r][slot][:, 0:nq * 128], in0=eb_[:, 0:nq * 128],
                            in1=EB[:, hidx, (dlo + 1) * 128:(dlo + 1 + nq) * 128], op=ALU.mult),
                            reads=[("e", r, t_i % 2), ("eb", hidx)], writes=[("pt", r, slot)])
                    qb = j - lo - 1
                    if 0 <= qb < nb:
                        pend.append((t_i + LAG, qc, qb))
                    while pend and pend[0][0] <= t_i:
                        _, qc_p, qb_p = pend.pop(0)
                        pv_stage(qc_p, qb_p)
                while pend:
                    _, qc_p, qb_p = pend.pop(0)
                    pv_stage(qc_p, qb_p)

                def ymm_attn(n_, woff, py, os_, on, ob, onb):
                    wv2 = arena[:, woff:woff + 2048].rearrange("p (a k n) -> p a k n", a=2, k=8)
                    mms = [(psb[py][:, 0:on], wv2[:, 0, c, :], BM[:, c, os_:os_ + on]) for c in range(8)]
                    return mms, kbmall(range(8), ob, onb), wv2[:, 1]

                phase[0] = 'at'
                branch_out("at", False, ymm_attn)

                phase[0] = 'wo'
                wops = [wget(f"wo{n_}") for n_ in range(8)]
                first_norm = True
                last = None
                for (ob, onb) in own_tiles:
                    on = onb * 128
                    os_ = ob * 128
                    gs = (b0 + ob) * 128
                    phase[0] = 'wo'
                    for n_ in range(8):
                        widx, woff, wtok = wops[n_]
                        wo = arena[:, woff:woff + 1024].rearrange("p (k n) -> p k n", k=8)
                        px = PS.alloc()
                        last = mm([(psb[px][:, 0:on], wo[:, c, :], BM[:, 8 + c, os_:os_ + on]) for c in range(8)],
                                  reads=kbmall(range(8, 16), ob, onb), writes=[("ps", px)], extra=[wtok])
                        op(DVE, lambda: nc.vector.tensor_tensor(out=xT[:, n_, gs:gs + on], in0=xT[:, n_, gs:gs + on],
                                                                in1=psb[px][:, 0:on], op=ALU.add),
                           reads=[("ps", px)] + kx(n_, b0 + ob, onb), writes=kx(n_, b0 + ob, onb))
                        PS.free(px)
                    phase[0] = 'ffn_norm'
                    if first_norm:
                        barrier()
                        first_norm = False
                    rmsnorm(b0 + ob, onb, G_FFN,
                            lambda c, t0, n: hT[:, c, t0 - e0 * 128:t0 - e0 * 128 + n],
                            lambda c, blk: ("h", c, blk - e0))
                for n_ in range(8):
                    W.release(wops[n_][0], last)

                for fg in range(NFG):
                    phase[0] = 'ffn_gu'
                    for f in range(FPG):
                        widx, woff, wtok = wget(f"gu{fg}_{f}")
                        wg = arena[:, woff:woff + 2048].rearrange("p (a k n) -> p a k n", a=2, k=8)
                        last = None
                        for (ob, onb) in own_tiles:
                            on = onb * 128
                            os_ = ob * 128
                            hs = lo * 128 + os_
                            pg = PS.alloc()
                            pu = PS.alloc()
                            mm([(psb[pg][:, 0:on], wg[:, 0, k, :], hT[:, k, hs:hs + on]) for k in range(8)],
                               reads=khall(lo + ob, onb), writes=[("ps", pg)], extra=[wtok])
                            last = mm([(psb[pu][:, 0:on], wg[:, 1, k, :], hT[:, k, hs:hs + on]) for k in range(8)],
                                      reads=khall(lo + ob, onb), writes=[("ps", pu)])
                            f1 = nextf()
                            op(ACT, lambda: nc.scalar.activation(out=fbuf[f1][:, 0:on], in_=psb[pg][:, 0:on], func=AF.Silu),
                               reads=[("ps", pg)], writes=[("f", f1)])
                            PS.free(pg)
                            op(DVE, lambda: nc.vector.tensor_tensor(out=BM[:, f, os_:os_ + on], in0=psb[pu][:, 0:on],
                                                                    in1=fbuf[f1][:, 0:on], op=ALU.mult),
                               reads=[("ps", pu), ("f", f1)], writes=kbm(f, ob, onb))
                            PS.free(pu)
                        W.release(widx, last)
                    stages = []
                    if fg == NFG - 1:
                        oi_ = order.index((l, s_i))
                        if oi_ + 1 < len(order):
                            stages = phase0_stages(*order[oi_ + 1])
                    phase[0] = 'ffn_d'
                    if stages:
                        stages.pop(0)()
                    for n_ in range(8):
                        if stages and n_ >= 1:
                            stages.pop(0)()
                        widx, woff, wtok = wget(f"wd{fg}_{n_}")
                        wd = arena[:, woff:woff + 1408].rearrange("p (f n) -> p f n", f=FPG)
                        last = None
                        for (ob, onb) in own_tiles:
                            on = onb * 128
                            os_ = ob * 128
                            gs = (b0 + ob) * 128
                            pd = PS.alloc()
                            last = mm([(psb[pd][:, 0:on], wd[:, f, :], BM[:, f, os_:os_ + on]) for f in range(FPG)],
                                      reads=kbmall(range(FPG), ob, onb), writes=[("ps", pd)], extra=[wtok])
                            op(DVE, lambda: nc.vector.tensor_tensor(out=xT[:, n_, gs:gs + on], in0=xT[:, n_, gs:gs + on],
                                                                    in1=psb[pd][:, 0:on], op=ALU.add),
                               reads=[("ps", pd)] + kx(n_, b0 + ob, onb), writes=kx(n_, b0 + ob, onb))
                            PS.free(pd)
                        W.release(widx, last)
                    while stages:
                        stages.pop(0)()

        phase[0] = 'final'
        barrier()
        s_o = Src(sem("s_out"))
        ov = out_d.rearrange("(c p) t -> p c t", p=128)
        if do_final:
            goff = n_layers * CST_PER_L
            for gb in range(0, OUTB, 2):
                rmsnorm(gb, 2, goff,
                        lambda c, t0, n: xT[:, c, t0:t0 + n],
                        lambda c, blk: ("x", c, blk))
        for (ob, onb) in split_blocks(OUTB, 4):
            dma(SP, ov[:, :, ob * 128:(ob + onb) * 128], xT[:, :, ob * 128:(ob + onb) * 128], s_o,
                reads=kxall(ob, onb))
        nc.sync.wait_ge(s_o.sem, s_o.cnt)
        for E in (PE, ACT, DVE):
            nc.sync.wait_ge(E.src.sem, E.src.cnt)
    return nc


def _kp(mat):
    n = mat.shape[1]
    return mat.reshape(8, 128, n).transpose(1, 0, 2)


def pack_layer(w_in, w_a_out, w_pool, w_attn_out, w_o, w_gu, w_down):
    Bc, Cc, Xc, Uc, Qc, Kc, Vc, Gac, Gpc, Gtc = 0, 1024, 2048, 3072, 4096, 5120, 5376, 5632, 6656, 7680
    pcs = []

    def add(a):
        pcs.append(np.ascontiguousarray(a, dtype=np.float32).reshape(128, -1))
    for c in range(8):
        add(np.stack([_kp(w_in[:, Cc + c * 128:Cc + (c + 1) * 128]),
                      _kp(w_in[:, Xc + c * 128:Xc + (c + 1) * 128]),
                      _kp(w_in[:, Bc + c * 128:Bc + (c + 1) * 128])], axis=1))
    for n in range(8):
        add(np.stack([_kp(w_a_out[:, n * 128:(n + 1) * 128]),
                      _kp(w_in[:, Gac + n * 128:Gac + (n + 1) * 128])], axis=1))
    add(_kp(w_in[:, Uc:Uc + 512]))
    add(_kp(w_in[:, Uc + 512:Uc + 1024]))
    add(w_pool.reshape(4, 2, 128, 256).transpose(2, 0, 1, 3))
    for n in range(8):
        add(_kp(w_in[:, Gpc + n * 128:Gpc + (n + 1) * 128]))
    add(_kp(w_in[:, Kc:Kc + 256]))
    add(_kp(w_in[:, Vc:Vc + 256]))
    qcols = np.concatenate([np.arange(Qc + h * 64, Qc + (h + 1) * 64) for h in HEAD_ORDER])
    wq = w_in[:, qcols]
    for q in range(8):
        add(_kp(wq[:, q * 128:(q + 1) * 128]))
    orow = np.concatenate([np.arange(h * 64, (h + 1) * 64) for h in HEAD_ORDER])
    wao = w_attn_out[orow, :]
    for n in range(8):
        add(np.stack([_kp(wao[:, n * 128:(n + 1) * 128]),
                      _kp(w_in[:, Gtc + n * 128:Gtc + (n + 1) * 128])], axis=1))
    for n in range(8):
        add(_kp(w_o[:, n * 128:(n + 1) * 128]))
    for fg in range(NFG):
        for f in range(FPG):
            fi = fg * FPG + f
            add(np.stack([_kp(w_gu[:, fi * 128:(fi + 1) * 128]),
                          _kp(w_gu[:, DFF + fi * 128:DFF + (fi + 1) * 128])], axis=1))
        wdg = w_down[fg * FPG * 128:(fg + 1) * FPG * 128, :].reshape(FPG, 128, D).transpose(1, 0, 2)
        for n in range(8):
            add(wdg[:, :, n * 128:(n + 1) * 128])
    out = np.concatenate(pcs, axis=1)
    assert out.shape == (128, EPP), out.shape
    return out


def t5_bucket_np(rel):
    import math
    import jax
    import jax.numpy as jnp
    try:
        dev = jax.devices("cpu")[0]
    except Exception:
        dev = None
    ctx = jax.default_device(dev) if dev is not None else contextlib.nullcontext()
    with ctx:
        rel = jnp.asarray(rel, dtype=jnp.int32)
        n_buckets, max_distance = 32, 128
        half = n_buckets // 2
        max_exact = half // 2
        ret = jnp.where(rel > 0, half, 0)
        n = jnp.abs(rel)
        nf = jnp.maximum(n, 1).astype(jnp.float32)
        large = max_exact + (jnp.log(nf / max_exact) / math.log(max_distance / max_exact)
                             * (half - max_exact)).astype(jnp.int32)
        large = jnp.minimum(large, half - 1)
        return np.asarray(ret + jnp.where(n < max_exact, n, large))


def make_bias_tiles(rel_bias, sgn):
    k = np.arange(128)[:, None]
    qq = np.arange(384)[None, :]
    d = qq // 128 - 1
    q = qq % 128
    rl = k - q - 128 * d
    valid = np.abs(rl) <= 128
    bk = t5_bucket_np(sgn * rl)
    out = np.empty((128, NH, 384), np.float32)
    for hi, h in enumerate(HEAD_ORDER):
        out[:, hi, :] = np.where(valid, rel_bias[bk, h], np.float32(NEGB))
    return out


def make_bands(sgn):
    out = np.zeros((128, 16, 128), np.float32)
    tp = np.arange(128)[:, None]
    t = np.arange(128)[None, :]
    for g, w in enumerate(POOL_W):
        if sgn > 0:
            lo_o, hi_o = -(w // 2), (w - 1 - w // 2)
        else:
            lo_o, hi_o = -(w - 1 - w // 2), (w // 2)
        for d in (-1, 0, 1):
            off = 128 * d + tp - t
            m = ((off >= lo_o) & (off <= hi_o)).astype(np.float32) / np.float32(w)
            if d == 0:
                m = m - (tp == t).astype(np.float32)
            out[:, g * 3 + d + 1, :] = m
        off = tp - t
        inwin = (off >= lo_o) & (off <= hi_o)
        cnt = ((t + hi_o) - np.maximum(t + lo_o, 0) + 1).astype(np.float32)
        m = inwin.astype(np.float32) / cnt - (tp == t).astype(np.float32)
        out[:, 12 + g, :] = m
    return out.reshape(128, 16 * 128)


def make_cst(layers, conv_w, pool_scale, attn_sink, g_mix, g_ffn, g_final, sgn):
    nl = len(layers)
    out = np.zeros((128, nl * CST_PER_L + 8), np.float32)

    def pc(v):
        return v.reshape(8, 128).T
    for i, l in enumerate(layers):
        o = i * CST_PER_L
        out[:, o:o + 8] = pc(g_mix[l])
        out[:, o + 8:o + 16] = pc(g_ffn[l])
        out[:, o + 16:o + 24] = pc(pool_scale[l])
        cw = conv_w[l, :, 0, :]
        if sgn < 0:
            cw = cw[::-1]
        out[:, o + 24:o + 48] = np.stack([pc(cw[k]) for k in range(3)], axis=2).reshape(128, 24)
        for qc in range(8):
            out[0:64, o + 48 + qc] = attn_sink[l, HEAD_ORDER[2 * qc]]
            out[64:128, o + 48 + qc] = attn_sink[l, HEAD_ORDER[2 * qc + 1]]
    out[:, nl * CST_PER_L:] = pc(g_final)
    return out


FUSED = True
LAST_LABELS = {}
DBG_LAYERS = None


def kernel(x, w_in, conv_w, w_a_out, w_pool, pool_scale, w_attn_out, attn_sink, w_o,
           g_mix, g_ffn, w_gu, w_down, rel_bias, g_final):
    f = lambda a: np.asarray(a, dtype=np.float32)
    x, w_in, conv_w, w_a_out, w_pool, pool_scale = map(f, (x, w_in, conv_w, w_a_out, w_pool, pool_scale))
    w_attn_out, attn_sink, w_o, g_mix, g_ffn, w_gu, w_down, rel_bias, g_final = map(
        f, (w_attn_out, attn_sink, w_o, g_mix, g_ffn, w_gu, w_down, rel_bias, g_final))
    wst = [pack_layer(w_in[l], w_a_out[l], w_pool[l], w_attn_out[l], w_o[l], w_gu[l], w_down[l])
           for l in range(DEPTH)]
    bias_t = {s: make_bias_tiles(rel_bias, s) for s in (1, -1)}
    bands = {s: make_bands(s) for s in (1, -1)}
    half = SEQ // 2

    def run(layers, regions, tin_blk, do_final, xin):
        nc = build_program(len(layers), regions, tin_blk, do_final)
        wst_l = np.ascontiguousarray(np.stack([wst[l] for l in layers], axis=0))
        in_maps = []
        for c in range(8):
            sgn = 1 if c % 2 == 0 else -1
            in_maps.append({
                "xT": np.ascontiguousarray(xin[c].T),
                "wst": wst_l,
                "cst": make_cst(layers, conv_w, pool_scale, attn_sink, g_mix, g_ffn, g_final, sgn),
                "biasT": bias_t[sgn],
                "bands": bands[sgn],
            })
        res = run_bass_kernel_spmd(nc, in_maps, core_ids=list(range(8)))
        return [np.asarray(r["outT"]).T for r in res.results]

    def local_slices(xfull, ntok):
        outs = []
        for c in range(8):
            b, hf = c // 2, c % 2
            if hf == 0:
                outs.append(xfull[b, 0:ntok, :])
            else:
                outs.append(xfull[b, ::-1, :][0:ntok, :])
        return outs

    def assemble(parts):
        out = np.empty((BATCH, SEQ, D), np.float32)
        for c in range(8):
            b, hf = c // 2, c % 2
            if hf == 0:
                out[b, 0:half, :] = parts[c]
            else:
                out[b, half:, :] = parts[c][::-1, :]
        return out

    nrun = DEPTH if DBG_LAYERS is None else DBG_LAYERS
    fin = DBG_LAYERS is None
    if FUSED:
        regions = [OWN_BLK + (nrun - 1 - l) for l in range(nrun)]
        tin = regions[0] + 1
        parts = run(list(range(nrun)), regions, tin, fin, local_slices(x, tin * 128))
        return assemble(parts)
    else:
        cur = x
        for l in range(nrun):
            parts = run([l], [OWN_BLK], OWN_BLK + 1, fin and l == nrun - 1, local_slices(cur, (OWN_BLK + 1) * 128))
            cur = assemble(parts)
        return cur
```
